# Optimizing a Trainium2 kernel written in Bass

```python
import math
import jax, jax.numpy as jnp
from jax import lax
import numpy as np

D_MODEL = 1024
BATCH = 2
SEQ = 8192
DEPTH = 2

CHUNK = 64
D_HEAD = 64
QBLOCK = 128
ROPE_THETA = 10000.0
EPS = 1e-6

A_HEADS = 4
A_DV = 2 * D_HEAD
B_HEADS = 8
B_LOOKBACK = 8
B_BAND = (B_LOOKBACK + 1) * CHUNK
B_MAX_REL = 256
B_REL_SIZE = CHUNK + B_MAX_REL
C_HEADS = 4
C_DV = 2 * D_HEAD

N_BRANCH = 3
D_FF = 2816

A_QK = A_HEADS * 2 * D_HEAD
A_V = A_HEADS * A_DV
B_QKV = B_HEADS * D_HEAD
C_QK = C_HEADS * D_HEAD
C_V = C_HEADS * C_DV
SPLITS = (A_QK, A_QK, A_V, B_QKV, B_QKV, B_QKV, C_QK, C_QK, C_V, C_V, N_BRANCH * D_MODEL)
IN_COLS = sum(SPLITS)

kernel_name = "hybrid_gated_diffattn_chunkattn_retention_macaron"


def _rms_f32(x, gain):
    xf = x.astype(jnp.float32)
    y = xf * lax.rsqrt(jnp.mean(xf * xf, axis=-1, keepdims=True) + EPS)
    return y * gain.astype(jnp.float32)


def rms_norm(x, gain):
    return _rms_f32(x, gain).astype(x.dtype)


def rope(x):
    S, d = x.shape[1], x.shape[-1]
    half = d // 2
    inv = ROPE_THETA ** (-jnp.arange(half, dtype=jnp.float32) / half)
    ang = jnp.arange(S, dtype=jnp.float32)[:, None] * inv[None, :]
    shape = (1, S) + (1,) * (x.ndim - 3) + (half,)
    cos, sin = jnp.cos(ang).reshape(shape), jnp.sin(ang).reshape(shape)
    x1, x2 = x[..., :half], x[..., half:]
    return jnp.concatenate([x1 * cos - x2 * sin, x2 * cos + x1 * sin], axis=-1)


def swiglu_ffn(x, norm, w_gate, w_up, w_down):
    h = rms_norm(x, norm)
    return (jax.nn.silu(h @ w_gate) * (h @ w_up)) @ w_down


def diff_attention(q, k, v, lam, lam_init, subln):
    bsz, S = q.shape[0], q.shape[1]
    nb = S // QBLOCK
    scale = D_HEAD ** -0.5
    qb = jnp.moveaxis(q.reshape(bsz, nb, QBLOCK, A_HEADS, 2, D_HEAD), 1, 0)
    key_chunk = jnp.arange(S) // CHUNK

    def block(args):
        i, qi = args
        q_chunk = (i * QBLOCK + jnp.arange(QBLOCK)) // CHUNK
        visible = key_chunk[None, :] <= q_chunk[:, None]
        s = jnp.einsum('bqhcd,bkhcd->bhcqk', qi, k) * scale
        p = jax.nn.softmax(jnp.where(visible, s, -jnp.inf), axis=-1)
        w = p[:, :, 0] - lam * p[:, :, 1]
        return jnp.einsum('bhqk,bkhe->bqhe', w, v)

    o = lax.map(block, (jnp.arange(nb), qb))
    o = jnp.moveaxis(o, 0, 1).reshape(bsz, S, A_HEADS, A_DV)
    o = _rms_f32(o, subln) * (1.0 - lam_init)
    return o.reshape(bsz, S, A_HEADS * A_DV)


def chunk_attention(q, k, v, rel_bias):
    bsz, S = q.shape[0], q.shape[1]
    nc = S // CHUNK
    scale = D_HEAD ** -0.5
    qc = q.reshape(bsz, nc, CHUNK, B_HEADS, D_HEAD)

    def band(t):
        tc = t.reshape(bsz, nc, CHUNK, B_HEADS, D_HEAD)
        tp = jnp.pad(tc, ((0, 0), (B_LOOKBACK, 0), (0, 0), (0, 0), (0, 0)))
        return jnp.concatenate([tp[:, j:j + nc] for j in range(B_LOOKBACK + 1)], axis=2)

    kb, vb = band(k), band(v)
    qi = jnp.arange(CHUNK)
    kj = jnp.arange(B_BAND)
    rel = qi[:, None] + B_LOOKBACK * CHUNK - kj[None, :]
    idx = jnp.clip(rel, -(CHUNK - 1), B_MAX_REL) + (CHUNK - 1)
    bias = rel_bias.astype(jnp.float32)[:, idx]
    valid = (jnp.arange(nc)[:, None] - B_LOOKBACK + (kj // CHUNK)[None, :]) >= 0
    s = jnp.einsum('bnqhd,bnkhd->bnhqk', qc, kb) * scale + bias
    s = jnp.where(valid[None, :, None, None, :], s, -jnp.inf)
    p = jax.nn.softmax(s, axis=-1)
    o = jnp.einsum('bnhqk,bnkhd->bnqhd', p, vb)
    return o.reshape(bsz, S, B_HEADS * D_HEAD)


def retention(q, k, v, g, out_norm):
    bsz, S = q.shape[0], q.shape[1]
    nc = S // CHUNK
    log_gamma = jnp.log(1.0 - 2.0 ** (-5.0 - jnp.arange(C_HEADS, dtype=jnp.float32)))
    pos = jnp.arange(CHUNK, dtype=jnp.float32)
    diff = pos[:, None] - pos[None, :]
    decay = jnp.where(diff >= 0, jnp.exp(log_gamma[:, None, None] * jnp.maximum(diff, 0.0)), 0.0)
    zeta = jnp.exp(log_gamma[:, None] * (CHUNK - 1 - pos)[None, :])
    xi = jnp.exp(log_gamma[:, None] * (pos + 1.0)[None, :])
    chunk_decay = jnp.exp(log_gamma * CHUNK)

    qc = q.reshape(bsz, nc, CHUNK, C_HEADS, D_HEAD)
    kc = k.reshape(bsz, nc, CHUNK, C_HEADS, D_HEAD)
    vc = v.reshape(bsz, nc, CHUNK, C_HEADS, C_DV)
    inner = jnp.einsum('bnqhd,bnkhd->bnhqk', qc, kc) * decay
    inner = jnp.einsum('bnhqk,bnkhe->bnqhe', inner, vc)
    kv = jnp.einsum('bnkhd,bnkhe,hk->bnhde', kc, vc, zeta)

    def step(state, kv_n):
        return state * chunk_decay[None, :, None, None] + kv_n, state

    init = jnp.zeros((bsz, C_HEADS, D_HEAD, C_DV), jnp.float32)
    _, prev = lax.scan(step, init, jnp.moveaxis(kv, 1, 0))
    prev = jnp.moveaxis(prev, 0, 1)
    cross = jnp.einsum('bnqhd,bnhde->bnqhe', qc, prev) * xi.T[None, None, :, :, None]
    o = (inner + cross).reshape(bsz, S, C_HEADS, C_DV)
    o = _rms_f32(o, out_norm).reshape(bsz, S, C_HEADS * C_DV)
    return jax.nn.silu(g) * o


def setup_inputs(seed: int = 0) -> dict:
    key = jax.random.key(seed)
    ks = iter(jax.random.split(key, 32))
    f32 = jnp.float32

    def nrm(shape, scale):
        return jax.random.normal(next(ks), shape, f32) * scale

    def gain(shape):
        return 1.0 + 0.02 * jax.random.normal(next(ks), shape, f32)

    L = DEPTH
    return {
        "x": jax.random.normal(next(ks), (BATCH, SEQ, D_MODEL), f32),
        "ffn1_norm": gain((L, D_MODEL)),
        "ffn1_w_gate": nrm((L, D_MODEL, D_FF), D_MODEL ** -0.5),
        "ffn1_w_up": nrm((L, D_MODEL, D_FF), D_MODEL ** -0.5),
        "ffn1_w_down": nrm((L, D_FF, D_MODEL), D_FF ** -0.5),
        "mix_norm": gain((L, D_MODEL)),
        "w_in": nrm((L, D_MODEL, IN_COLS), D_MODEL ** -0.5),
        "a_q_norm": gain((L, D_HEAD)),
        "a_k_norm": gain((L, D_HEAD)),
        "a_lambda_q1": nrm((L, D_HEAD), 0.1),
        "a_lambda_k1": nrm((L, D_HEAD), 0.1),
        "a_lambda_q2": nrm((L, D_HEAD), 0.1),
        "a_lambda_k2": nrm((L, D_HEAD), 0.1),
        "a_subln": gain((L, A_DV)),
        "b_q_norm": gain((L, D_HEAD)),
        "b_k_norm": gain((L, D_HEAD)),
        "b_rel_bias": nrm((L, B_HEADS, B_REL_SIZE), 0.5),
        "c_out_norm": gain((L, C_HEADS, C_DV)),
        "w_branch_a": nrm((L, A_V, D_MODEL), A_V ** -0.5),
        "w_branch_b": nrm((L, B_QKV, D_MODEL), B_QKV ** -0.5),
        "w_branch_c": nrm((L, C_V, D_MODEL), C_V ** -0.5),
        "w_out": nrm((L, D_MODEL, D_MODEL), D_MODEL ** -0.5),
        "ffn2_norm": gain((L, D_MODEL)),
        "ffn2_w_gate": nrm((L, D_MODEL, D_FF), D_MODEL ** -0.5),
        "ffn2_w_up": nrm((L, D_MODEL, D_FF), D_MODEL ** -0.5),
        "ffn2_w_down": nrm((L, D_FF, D_MODEL), D_FF ** -0.5),
    }


def reference(x, ffn1_norm, ffn1_w_gate, ffn1_w_up, ffn1_w_down, mix_norm, w_in,
              a_q_norm, a_k_norm, a_lambda_q1, a_lambda_k1, a_lambda_q2, a_lambda_k2, a_subln,
              b_q_norm, b_k_norm, b_rel_bias, c_out_norm,
              w_branch_a, w_branch_b, w_branch_c, w_out,
              ffn2_norm, ffn2_w_gate, ffn2_w_up, ffn2_w_down):
    f32 = jnp.float32
    bsz, S = x.shape[0], x.shape[1]
    split_points = [int(p) for p in np.cumsum(SPLITS)[:-1]]
    for l in range(DEPTH):
        x = x + 0.5 * swiglu_ffn(x, ffn1_norm[l], ffn1_w_gate[l], ffn1_w_up[l], ffn1_w_down[l])

        h = rms_norm(x, mix_norm[l])
        aq, ak, av, bq, bk, bv, cq, ck, cv, cg, gates = jnp.split(h @ w_in[l], split_points, axis=-1)

        lam_init = 0.8 - 0.6 * math.exp(-0.3 * l)
        lam = (jnp.exp(jnp.sum(a_lambda_q1[l].astype(f32) * a_lambda_k1[l].astype(f32)))
               - jnp.exp(jnp.sum(a_lambda_q2[l].astype(f32) * a_lambda_k2[l].astype(f32))) + lam_init)
        qa = rope(_rms_f32(aq.reshape(bsz, S, A_HEADS, 2, D_HEAD), a_q_norm[l]))
        ka = rope(_rms_f32(ak.reshape(bsz, S, A_HEADS, 2, D_HEAD), a_k_norm[l]))
        va = av.reshape(bsz, S, A_HEADS, A_DV).astype(f32)
        y_a = diff_attention(qa, ka, va, lam, lam_init, a_subln[l])

        qb = _rms_f32(bq.reshape(bsz, S, B_HEADS, D_HEAD), b_q_norm[l])
        kb = _rms_f32(bk.reshape(bsz, S, B_HEADS, D_HEAD), b_k_norm[l])
        vb = bv.reshape(bsz, S, B_HEADS, D_HEAD).astype(f32)
        y_b = chunk_attention(qb, kb, vb, b_rel_bias[l])

        qc = rope(cq.reshape(bsz, S, C_HEADS, D_HEAD).astype(f32))
        kc = rope(ck.reshape(bsz, S, C_HEADS, D_HEAD).astype(f32)) * (D_HEAD ** -0.5)
        vc = cv.reshape(bsz, S, C_HEADS, C_DV).astype(f32)
        y_c = retention(qc, kc, vc, cg.astype(f32), c_out_norm[l])

        g = jax.nn.sigmoid(gates.astype(f32)).reshape(bsz, S, N_BRANCH, D_MODEL)
        merged = (g[:, :, 0] * (y_a.astype(x.dtype) @ w_branch_a[l]).astype(f32)
                  + g[:, :, 1] * (y_b.astype(x.dtype) @ w_branch_b[l]).astype(f32)
                  + g[:, :, 2] * (y_c.astype(x.dtype) @ w_branch_c[l]).astype(f32))
        x = x + merged.astype(x.dtype) @ w_out[l]

        x = x + 0.5 * swiglu_ffn(x, ffn2_norm[l], ffn2_w_gate[l], ffn2_w_up[l], ffn2_w_down[l])
    return x
```

```python
import contextlib
import math
import numpy as np
import ml_dtypes
import concourse.bass as bass
import concourse.mybir as mybir
from concourse.bass_utils import run_bass_kernel_spmd

F32 = mybir.dt.float32
BF16 = mybir.dt.bfloat16
ALU = mybir.AluOpType
AF = mybir.ActivationFunctionType
AX = mybir.AxisListType

COMPUTE = ("pe", "act", "dve", "pool")
QUEUES = ("sp", "pq", "cc")
NSEM_DMA = 8
EPOCH = 24000

DM = 1024
DFF = 2816
NCORE = 8
TOK = 2048
NT = 16
SEQ = 8192
NKT = 64
EPS = 1e-6
IN_COLS = 7680


class Op:
    __slots__ = ("eng", "fn", "deps", "signaled", "sigcount", "dma", "dma_i")

    def __init__(self, eng, fn, dma):
        self.eng = eng
        self.fn = fn
        self.deps = []
        self.signaled = False
        self.sigcount = 0
        self.dma = dma
        self.dma_i = -1


class Prog:
    def __init__(self, nc):
        self.nc = nc
        self.streams = {"pe": [], "act": [], "dve": [], "pool": [], "sp": []}
        self.last_writer = {}
        self.readers = {}
        self.dma_count = {"sp": 0, "pq": 0, "cc": 0}

    @staticmethod
    def stream_of(eng):
        return "pool" if eng in ("pq", "cc") else eng

    def op(self, eng, fn, reads=(), writes=()):
        dma = eng in QUEUES
        o = Op(eng, fn, dma)
        deps = {}
        for k in reads:
            w = self.last_writer.get(k)
            if w is not None:
                deps[id(w)] = w
        for k in writes:
            w = self.last_writer.get(k)
            if w is not None:
                deps[id(w)] = w
            for r in self.readers.get(k, ()):
                deps[id(r)] = r
        for k in writes:
            self.last_writer[k] = o
            self.readers[k] = []
        for k in reads:
            lst = self.readers.setdefault(k, [])
            if not dma:
                lst[:] = [r for r in lst if r.eng != eng]
            lst.append(o)
        for d in deps.values():
            if d is o:
                continue
            if d.eng == "pe" and eng == "pe":
                continue
            o.deps.append(d)
            d.signaled = True
        if dma:
            o.signaled = True
            o.dma_i = self.dma_count[eng]
            self.dma_count[eng] += 1
        self.streams[self.stream_of(eng)].append(o)
        return o

    def barrier(self, keep=()):
        last = {}
        for e in COMPUTE:
            for o in reversed(self.streams[e]):
                if isinstance(o, Op) and not o.dma:
                    o.signaled = True
                    last[e] = o
                    break
        b = ("barrier", last, dict(self.dma_count))
        for stream in self.streams:
            self.streams[stream].append(b)
        kept = {k: w for k, w in self.last_writer.items() if k in keep or w.eng == "cc"}
        self.last_writer.clear()
        self.readers.clear()
        self.last_writer.update(kept)

    def emit(self):
        nc = self.nc
        cnt = {e: 0 for e in COMPUTE}
        for ops in self.streams.values():
            for o in ops:
                if isinstance(o, Op) and not o.dma and o.signaled:
                    cnt[o.eng] += 1
                    o.sigcount = cnt[o.eng]
        nep = {e: max(1, (cnt[e] + EPOCH - 1) // EPOCH) for e in COMPUTE}
        with contextlib.ExitStack() as st:
            sems = {}
            for e in COMPUTE:
                sems[e] = [st.enter_context(nc.semaphore(f"s_{e}{i}")) for i in range(nep[e])]
            for q in ("sp", "pq"):
                sems[q] = [st.enter_context(nc.semaphore(f"s_{q}{i}")) for i in range(NSEM_DMA)]
            sems["cc"] = [st.enter_context(nc.semaphore("s_cc"))]
            block = st.enter_context(nc.Block())

            def sem_val(d):
                if d.eng == "cc":
                    return sems["cc"][0], d.dma_i + 1
                if d.dma:
                    return sems[d.eng][d.dma_i % NSEM_DMA], 16 * (d.dma_i // NSEM_DMA + 1)
                ep = (d.sigcount - 1) // EPOCH
                return sems[d.eng][ep], d.sigcount - ep * EPOCH

            def mk(stream):
                def body(engh):
                    waited = {}

                    def wait(s, v):
                        if waited.get(id(s), 0) >= v:
                            return
                        waited[id(s)] = v
                        engh.wait_ge(s, v)

                    for o in self.streams[stream]:
                        if not isinstance(o, Op):
                            _, last, counts = o
                            for d in last.values():
                                wait(*sem_val(d))
                            for q in ("sp", "pq"):
                                n = counts[q]
                                for i in range(min(n, NSEM_DMA)):
                                    wait(sems[q][i], 16 * ((n - 1 - i) // NSEM_DMA + 1))
                            continue
                        for d in o.deps:
                            wait(*sem_val(d))
                        if o.eng == "cc":
                            pass
                        elif o.dma and o.dma_i >= NSEM_DMA:
                            wait(sems[o.eng][o.dma_i % NSEM_DMA], 16 * (o.dma_i // NSEM_DMA))
                        ins = o.fn(engh)
                        if o.signaled:
                            s, _ = sem_val(o)
                            ins.then_inc(s, 16 if (o.dma and o.eng != "cc") else 1)
                    if stream == "sp":
                        if self.dma_count["cc"]:
                            wait(sems["cc"][0], self.dma_count["cc"])
                        for q in ("sp", "pq"):
                            n = self.dma_count[q]
                            for i in range(min(n, NSEM_DMA)):
                                tot = (n - 1 - i) // NSEM_DMA + 1
                                wait(sems[q][i], 16 * tot)
                return body

            block.tensor(mk("pe"))
            block.scalar(mk("act"))
            block.vector(mk("dve"))
            block.gpsimd(mk("pool"))
            block.sync(mk("sp"))


def mm(out, lhsT, rhs, start, stop):
    return lambda e: e.matmul(out, lhsT=lhsT, rhs=rhs, start=start, stop=stop, skip_group_check=True)


def tr(out, in_, ident):
    return lambda e: e.transpose(out, in_, ident)


def actf(out, in_, func, scale=1.0, bias=None, accum_out=None):
    kw = {}
    if bias is not None:
        kw["bias"] = bias
    if accum_out is not None:
        kw["accum_out"] = accum_out
    return lambda e: e.activation(out=out, in_=in_, func=func, scale=scale, **kw)


def tt(out, in0, in1, op):
    return lambda e: e.tensor_tensor(out=out, in0=in0, in1=in1, op=op)


def ts(out, in0, s1, op0, s2=None, op1=None):
    if op1 is None:
        return lambda e: e.tensor_scalar(out=out, in0=in0, scalar1=s1, scalar2=None, op0=op0)
    return lambda e: e.tensor_scalar(out=out, in0=in0, scalar1=s1, scalar2=s2, op0=op0, op1=op1)


def stt(out, in0, scalar, in1, op0, op1):
    return lambda e: e.scalar_tensor_tensor(out=out, in0=in0, scalar=scalar, in1=in1, op0=op0, op1=op1)


def dma(out, in_):
    return lambda e: e.dma_start(out=out, in_=in_)


def dma_nc(out, in_):
    return lambda e: e.dma_start(out=out, in_=in_, allow_slow_non_contiguous=True)


ARENA_KIB = 204


class Ctx:
    def __init__(self, nc, st):
        self.nc = nc
        self.st = st
        self.P = Prog(nc)
        self.psum = st.enter_context(nc.psum_tensor("psum", [128, 4096], F32))
        self.arena = st.enter_context(nc.sbuf_tensor("arena", [128, ARENA_KIB * 512], BF16))
        self.off = 0
        self.ident = self.sb("ident", [128, 128], BF16)
        self.epsc = self.sb("epsc", [128, 1], F32)
        self.P.op("pool", lambda e: e.memset(self.epsc, EPS), writes=["epsc"])

    def mark(self):
        return self.off

    def reset(self, m):
        self.off = m

    def sb(self, name, shape, dtype):
        n = 1
        for d in shape[1:]:
            n *= d
        nbytes = n * (4 if dtype == F32 else 2)
        nbytes = (nbytes + 63) // 64 * 64
        assert self.off + nbytes <= ARENA_KIB * 1024, (name, self.off, nbytes)
        v = self.arena[:, self.off // 2:(self.off + nbytes) // 2]
        self.off += nbytes
        if dtype == F32:
            v = v.bitcast(F32)
        v = v[:, 0:n]
        if len(shape) == 3:
            v = v.rearrange("p (a b) -> p a b", a=shape[1])
        elif len(shape) == 4:
            v = v.rearrange("p (a b c) -> p a b c", a=shape[1], b=shape[2])
        if shape[0] < 128:
            v = v[0:shape[0]]
        return v

    def bank(self, b, n=1):
        return self.psum[:, b * 512:(b + n) * 512]

    def bank_bf(self, b, n=1):
        return self.psum[:, b * 512:(b + n) * 512].bitcast(BF16)


def emit_rmsnorm_hT(C, x, hT, gain_d, tag, scr):
    P = C.P
    ss, lnv, rstd, gainT, junk, xn = scr["ss"], scr["lnv"], scr["rstd"], scr["gainT"], scr["junk"], scr["xn"]
    P.op("sp", dma_nc(gainT[:], gain_d.rearrange("(k p) -> p k", p=128)), writes=["gainT"])
    for t in range(NT):
        P.op("act", actf(junk, x[:, t, :], AF.Square, accum_out=ss[:, t:t + 1]),
             reads=[f"x{t}"], writes=[f"ss{t}", "sq_0"])
    allss = [f"ss{t}" for t in range(NT)]
    P.op("act", actf(lnv[:], ss[:], AF.Ln, scale=1.0 / DM, bias=C.epsc[:]), reads=allss + ["epsc"], writes=["lnv"])
    P.op("act", actf(rstd[:], lnv[:], AF.Exp, scale=-0.5), reads=["lnv"], writes=["rstd"])
    for t in range(NT):
        b = t % 2
        xk = scr["xnk"][b]
        P.op("act", actf(xn[b], x[:, t, :], AF.Copy, scale=rstd[:, t:t + 1]), reads=[f"x{t}", "rstd"], writes=[xk])
        pst = C.bank_bf(b)
        for k in range(8):
            P.op("pe", tr(pst[:, k * 128:(k + 1) * 128], xn[b][:, k * 128:(k + 1) * 128], C.ident[:]),
                 reads=[xk, "ident"], writes=[f"ps{b}"])
        P.op("dve", tt(hT[:, :, t * 128:(t + 1) * 128], pst.rearrange("p (k c) -> p k c", k=8),
                       gainT[:].unsqueeze(2).to_broadcast([128, 8, 128]), ALU.mult),
             reads=[f"ps{b}", "gainT"], writes=[f"hT{t}"])


FFN_CHUNKS = [(2 * i, 2) for i in range(11)]


def emit_ffn(C, x, hT, wball, wdbuf, aT2, wg_d, wu_d, wd_d, norm_d, scr, aT_keys=(), wd_keys=()):
    P = C.P
    emit_rmsnorm_hT(C, x, hT, norm_d, "f", scr)
    sg = scr["sg"]
    allhT = [f"hT{t}" for t in range(NT)]
    nch = len(FFN_CHUNKS)
    st = {"nmm": 0}

    def views(ci):
        f0, nf = FFN_CHUNKS[ci]
        cw = nf * 128
        j = ci % 3
        wb = wball[:, j * 4096:(j + 1) * 4096]
        wgv = wb[:, 0:8 * cw].rearrange("p (k c) -> p k c", k=8)
        wuv = wb[:, 2048:2048 + 8 * cw].rearrange("p (k c) -> p k c", k=8)
        wdv = wdbuf[ci % 2][:, 0:nf * 1024].rearrange("p (f c) -> p f c", f=nf)
        return f0, nf, cw, wgv, wuv, wdv, [f"wb{2 * j}", f"wb{2 * j + 1}"], f"wd{ci % 2}"

    def issue_wgu(ci):
        if ci >= nch:
            return
        f0, nf, cw, wgv, wuv, wdv, gk, dk = views(ci)
        P.op("pq", dma(wgv, wg_d[:, f0 * 128:f0 * 128 + cw].rearrange("(k p) c -> p k c", p=128)), writes=[gk[0]])
        P.op("pq", dma(wuv, wu_d[:, f0 * 128:f0 * 128 + cw].rearrange("(k p) c -> p k c", p=128)), writes=[gk[1]])

    def issue_wd(ci):
        if ci >= nch:
            return
        f0, nf, cw, wgv, wuv, wdv, gk, dk = views(ci)
        P.op("pq", dma(wdv, wd_d[f0 * 128:f0 * 128 + cw, :].rearrange("(f p) c -> p f c", p=128)),
             writes=[dk] + (list(wd_keys) if ci < 2 else []))

    def gateup(ci, tb, f):
        f0, nf, cw, wgv, wuv, wdv, gk, dk = views(ci)
        aT = aT2[ci % 2]
        pb = 2 * (st["nmm"] % 2)
        st["nmm"] += 1
        par = st["nmm"] % 2
        pg, pu = C.bank(pb), C.bank(pb + 1)
        for k in range(8):
            P.op("pe", mm(pg, wgv[:, k, f * 128:(f + 1) * 128], hT[:, k, tb * 512:(tb + 1) * 512], k == 0, k == 7),
                 reads=[gk[0]] + allhT[tb * 4:tb * 4 + 4], writes=[f"ps{pb}"])
        for k in range(8):
            P.op("pe", mm(pu, wuv[:, k, f * 128:(f + 1) * 128], hT[:, k, tb * 512:(tb + 1) * 512], k == 0, k == 7),
                 reads=[gk[1]] + allhT[tb * 4:tb * 4 + 4], writes=[f"ps{pb + 1}"])
        P.op("act", actf(sg[par][:], pg, AF.Silu), reads=[f"ps{pb}"], writes=[f"sg{par}"])
        P.op("dve", tt(aT[:, f, tb * 512:(tb + 1) * 512], sg[par][:], pu, ALU.mult),
             reads=[f"sg{par}", f"ps{pb + 1}"], writes=[f"aT{ci % 2}_{f}_{tb}"] + (list(aT_keys) if ci < 2 else []))

    def down(ci, t):
        f0, nf, cw, wgv, wuv, wdv, gk, dk = views(ci)
        aT = aT2[ci % 2]
        pb = 4 + 2 * (t % 2)
        py = C.bank(pb, 2)
        for n in range(2):
            for f in range(nf):
                P.op("pe", mm(py[:, n * 512:(n + 1) * 512], aT[:, f, t * 128:(t + 1) * 128], wdv[:, f, n * 512:(n + 1) * 512],
                              f == 0, f == nf - 1),
                     reads=[dk, f"aT{ci % 2}_{f}_{t // 4}"], writes=[f"ps{pb + n}"])
        P.op("dve", stt(x[:, t, :], py, 0.5, x[:, t, :], ALU.mult, ALU.add),
             reads=[f"ps{pb}", f"ps{pb + 1}", f"x{t}"], writes=[f"x{t}"])

    issue_wgu(0)
    issue_wd(0)
    issue_wgu(1)
    for tb in range(4):
        for f in range(FFN_CHUNKS[0][1]):
            gateup(0, tb, f)
    for ci in range(nch):
        issue_wgu(ci + 2)
        issue_wd(ci + 1)
        nxt = [(tb, f) for tb in range(4) for f in range(FFN_CHUNKS[ci + 1][1])] if ci + 1 < nch else []
        for t in range(NT):
            down(ci, t)
            if nxt and t % 2 == 1:
                gateup(ci + 1, *nxt[t // 2])


def make_scr(C):
    s = {}
    s["ss"] = C.sb("ss", [128, NT], F32)
    s["lnv"] = C.sb("lnv", [128, NT], F32)
    s["rstd"] = C.sb("rstd", [128, NT], F32)
    s["gainT"] = C.sb("gainT", [128, 8], F32)
    for n in ("sq", "xq", "xg", "t1", "t2"):
        s[n + "2"] = [C.sb(f"{n}{i}", [128, 512], F32) for i in range(2)]
        s[n] = s[n + "2"][0]
    s["junk"] = s["sq"].bitcast(BF16)
    s["xn"] = [s["xq"].bitcast(BF16), s["xg"].bitcast(BF16)]
    s["xnk"] = ["xq_0", "xg_0"]
    s["sg"] = [C.sb(f"sg{i}", [128, 512], BF16) for i in range(2)]
    s["res"] = [C.sb(f"res{i}", [128, 512], BF16) for i in range(2)]
    s["s8"] = C.sb("s8", [128, 16], F32)
    s["l8"] = C.sb("l8", [128, 16], F32)
    s["r8"] = C.sb("r8", [128, 16], F32)
    return s


def load_x(C, x, x_d):
    xv = x_d.rearrange("(t p) c -> p t c", p=128)
    for t4 in range(4):
        C.P.op("sp", dma(x[:, t4 * 4:(t4 + 1) * 4, :], xv[:, t4 * 4:(t4 + 1) * 4, :]),
               writes=[f"x{t}" for t in range(t4 * 4, t4 * 4 + 4)])


def store_x(C, x, x_d):
    xv = x_d.rearrange("(t p) c -> p t c", p=128)
    for t4 in range(4):
        C.P.op("sp", dma(xv[:, t4 * 4:(t4 + 1) * 4, :], x[:, t4 * 4:(t4 + 1) * 4, :]),
               reads=[f"x{t}" for t in range(t4 * 4, t4 * 4 + 4)])


PROJ_BLOCKS = [(0, "NR", 0), (1, "NR", 1), (2, "V", None), (3, "N", 2), (4, "N", 3), (5, "V", None),
               (6, "R", None), (7, "V", None), (8, "G", None)]
PROJ_OUT = {0: "oAq", 1: "oAk", 2: "oAv", 3: "oBq", 4: "oBk", 5: "oBv", 6: "oCqk", 7: "oCv", 8: "oCg"}


def emit_proj(C, x, hT, wbuf, stage, kstage, w_in_d, mixnorm_d, gains, cos2, sin2, outs, scr):
    P = C.P
    emit_rmsnorm_hT(C, x, hT, mixnorm_d, "m", scr)
    res = scr["res"]

    def v3(ap, h=8):
        return ap.rearrange("p (h d) -> p h d", h=h)

    blocks = [(blk, kind, gi) for (blk, kind, gi) in PROJ_BLOCKS] + [(9 + g, "GATE", None) for g in range(6)]

    def wview(bi):
        return wbuf[bi % 2][:, 0:4096].rearrange("p (k c) -> p k c", k=8)

    def wkeys(bi):
        return [f"wb{3 * (bi % 2)}", f"wb{3 * (bi % 2) + 1}"]

    def issue_w(bi):
        if bi >= len(blocks):
            return
        blk = blocks[bi][0]
        P.op("pq", dma(wview(bi), w_in_d[:, blk * 512:(blk + 1) * 512].rearrange("(k p) c -> p k c", p=128)),
             writes=wkeys(bi))

    def emit_mm(bi, t, pb):
        wv = wview(bi)
        for k in range(8):
            P.op("pe", mm(C.bank(pb), hT[:, k, t * 128:(t + 1) * 128], wv[:, k, :], k == 0, k == 7),
                 reads=wkeys(bi) + [f"hT{t}"], writes=[f"ps{pb}"])

    issue_w(0)
    inst = 0
    pending = []
    for bi, (blk, kind, gi) in enumerate(blocks):
        issue_w(bi + 1)
        wv = wview(bi)
        wk = f"w{bi % 2}"
        stg = stage[bi % 2]
        sk = f"stage{bi % 2}"
        if kind == "GATE" and blk == 10 and outs.early is not None:
            outs.early()
        if kind == "GATE":
            for f in range(4):
                for tb in range(4):
                    pb = inst % 2
                    inst += 1
                    ps = C.bank(pb)
                    for k in range(8):
                        P.op("pe", mm(ps, wv[:, k, f * 128:(f + 1) * 128], hT[:, k, tb * 512:(tb + 1) * 512], k == 0, k == 7),
                             reads=wkeys(bi) + [f"hT{t}" for t in range(tb * 4, tb * 4 + 4)], writes=[f"ps{pb}"])
                    P.op("act", actf(stg[:, f, tb * 512:(tb + 1) * 512], ps, AF.Sigmoid), reads=[f"ps{pb}"], writes=[sk])
            P.op("sp", dma(outs.gate(blk - 9), stg[:]), reads=[sk])
            for fn in pending:
                fn()
            pending = []
            continue
        for t in range(NT):
            pb = inst % 2
            par = inst % 2
            inst += 1
            sq, xq, xg, t1, t2 = (scr[n][par] for n in ("sq2", "xq2", "xg2", "t12", "t22"))
            s8, l8, r8 = (scr[n][:, par * 8:(par + 1) * 8] for n in ("s8", "l8", "r8"))
            kq = lambda n: f"{n}_{par}"
            ps = C.bank(pb)
            if t == 0:
                emit_mm(bi, 0, pb)
            if t + 1 < NT:
                emit_mm(bi, t + 1, 1 - pb)
            dst = stg[:, :, t * 128:(t + 1) * 128]
            if kind == "V":
                P.op("act", actf(dst, v3(ps, 4), AF.Copy), reads=[f"ps{pb}"], writes=[sk])
                continue
            if kind == "G":
                P.op("act", actf(dst, v3(ps, 4), AF.Silu), reads=[f"ps{pb}"], writes=[sk])
                continue
            rs = res[par]
            rk = f"res{par}"
            if kind in ("NR", "N"):
                P.op("act", actf(sq, ps, AF.Square), reads=[f"ps{pb}"], writes=[kq("sq")])
                P.op("dve", lambda e, o=s8, i=v3(sq): e.tensor_reduce(out=o, in_=i, axis=AX.X, op=ALU.add),
                     reads=[kq("sq")], writes=[kq("s8")])
                P.op("act", actf(l8, s8, AF.Ln, scale=1.0 / 64, bias=C.epsc[:]), reads=[kq("s8"), "epsc"], writes=[kq("l8")])
                P.op("act", actf(r8, l8, AF.Exp, scale=-0.5), reads=[kq("l8")], writes=[kq("r8")])
                P.op("dve", tt(v3(xq), v3(ps), r8.unsqueeze(2).to_broadcast([128, 8, 64]), ALU.mult),
                     reads=[f"ps{pb}", kq("r8")], writes=[kq("xq")])
                gb = gains[:, gi, :].unsqueeze(1).to_broadcast([128, 8, 64])
                if kind == "N":
                    P.op("dve", tt(v3(rs[:]), v3(xq), gb, ALU.mult), reads=[kq("xq"), "gains"], writes=[rk])
                else:
                    P.op("dve", tt(v3(xg), v3(xq), gb, ALU.mult), reads=[kq("xq"), "gains"], writes=[kq("xg")])
                src, srck, eng2 = v3(xg), kq("xg"), "pool"
            else:
                src, srck, eng2 = v3(ps), f"ps{pb}", "dve"
            if kind in ("NR", "R"):
                cb = cos2[:, t, :].unsqueeze(1).to_broadcast([128, 8, 64])
                sa = sin2[:, t, 0:32].unsqueeze(1).to_broadcast([128, 8, 32])
                sb_ = sin2[:, t, 32:64].unsqueeze(1).to_broadcast([128, 8, 32])
                P.op("dve", tt(v3(t1), src, cb, ALU.mult), reads=[srck, "cs"], writes=[kq("t1")])
                P.op(eng2, tt(v3(t2)[:, :, 0:32], src[:, :, 32:64], sa, ALU.mult), reads=[srck, "cs"], writes=[kq("t2a")])
                P.op(eng2, tt(v3(t2)[:, :, 32:64], src[:, :, 0:32], sb_, ALU.mult), reads=[srck, "cs"], writes=[kq("t2b")])
                P.op("dve", tt(rs[:], t1, t2, ALU.add), reads=[kq("t1"), kq("t2a"), kq("t2b")], writes=[rk])
            tb = 2 + par
            pst = C.bank_bf(tb)[:, 0:512]
            for c in range(4):
                P.op("pe", tr(pst[:, c * 128:(c + 1) * 128], rs[:, c * 128:(c + 1) * 128], C.ident[:]),
                     reads=[rk, "ident"], writes=[f"ps{tb}"])
            P.op("act", actf(dst, v3(pst, 4), AF.Copy), reads=[f"ps{tb}"], writes=[sk])
            if kind == "R":
                P.op("pool", lambda e, o=kstage[:, :, t * 64:(t + 1) * 64], i=v3(rs[:, 256:512], 4): e.tensor_copy(out=o, in_=i),
                     reads=[rk], writes=["kstage"])
        for fn in pending:
            fn()
        pending = outs.store_block(P, blk, kind, stg, sk, kstage)
    for fn in pending:
        fn()


def rope_tables(core):
    pos0 = (core % 4) * TOK
    pos = (pos0 + np.arange(TOK)).astype(np.float32)
    half = 32
    inv = (10000.0 ** (-np.arange(half, dtype=np.float32) / half)).astype(np.float32)
    ang = pos[:, None] * inv[None, :]
    cos, sin = np.cos(ang).astype(np.float32), np.sin(ang).astype(np.float32)
    cos2 = np.concatenate([cos, cos], -1).reshape(NT, 128, 64).transpose(1, 0, 2)
    sin2 = np.concatenate([-sin, sin], -1).reshape(NT, 128, 64).transpose(1, 0, 2)
    return np.ascontiguousarray(cos2), np.ascontiguousarray(sin2)


def emit_tr_out(C, ybf, ykey, dst, dkey):
    P = C.P
    pst = C.bank_bf(7)[:, 0:128]
    P.op("pe", tr(pst, ybf, C.ident[:]), reads=[ykey, "ident"], writes=["ps7"])
    P.op("act", actf(dst, pst, AF.Copy), reads=["ps7"], writes=[dkey])


def emit_mix_bc(C, d, T):
    P = C.P
    bQ, bK, bV = T["bQ"], T["bK"], T["bV"]

    def Sb(hh):
        return C.bank(2 * hh, 2)[:, 0:640].rearrange("p (o q) -> p o q", o=5)

    def hv(t_, hh):
        return t_[:, hh * 640:(hh + 1) * 640].rearrange("p (o q) -> p o q", o=5)

    EB = T["EB"]
    Ob = C.bank(4)[:, 0:130]
    Ob3 = Ob.rearrange("p (h e) -> p h e", h=2)
    Sc = C.bank(5)[:, 0:128]
    kv = C.bank(5)[0:64, 256:384]
    Oc = C.bank(6)[:, 0:128]
    E, PTb = T["E"], T["PTb"]
    state, state_bf = T["state"], T["state_bf"]
    P.op("pool", lambda e: e.memset(state[:], 0.0), writes=["state"])
    P.op("pool", lambda e: e.memset(state_bf[1][:], 0.0), writes=["state_bf1"])

    def grp(n):
        g, tl = n // 16, n % 16
        gb = g % 2
        return g, tl, gb, f"cg{gb}", {k: T[k][gb] for k in ("cQ", "cK", "cKt", "cV", "cG", "qxi", "kz")}

    def load_group(g):
        gb = g % 2
        gk = f"cg{gb}"
        B = {k: T[k][gb] for k in ("cQ", "cK", "cKt", "cV", "cG", "qxi", "kz")}
        d.load(P, B["cQ"][:], "oCqk", g, [gk + "q"], rows=(0, 64))
        d.load(P, B["cK"][:], "oCqk", g, [gk + "k"], rows=(64, 64))
        d.load(P, B["cKt"][:], "oCkt", g, [gk + "kt"], t=16)
        d.load(P, B["cV"][:], "oCv", g, [gk + "v"], t=16)
        d.load(P, B["cG"][:], "oCg", g, [gk + "g"], t=16)

    def derive_group(g):
        gb = g % 2
        gk = f"cg{gb}"
        B = {k: T[k][gb] for k in ("cQ", "cKt", "qxi", "kz")}
        P.op("dve", tt(B["qxi"][:].rearrange("p (t q) -> p t q", t=16), B["cQ"][:].rearrange("p (t q) -> p t q", t=16),
                       T["xi"][:].unsqueeze(1).to_broadcast([64, 16, 128]), ALU.mult),
             reads=[gk + "q", "cconst"], writes=[gk + "qxi"])
        P.op("dve", ts(B["kz"][:], B["cKt"][:], T["zeta"][:, 0:1], ALU.mult), reads=[gk + "kt", "cconst"], writes=[gk + "kz"])

    def s1(n):
        g, tl, gb, gk, B = grp(n)
        if tl == 2 and g + 1 < 4:
            load_group(g + 1)
        if tl == 11 and g + 1 < 4:
            derive_group(g + 1)
        o_lo = max(0, 4 - n)
        for hh in range(2):
            for o in range(o_lo, 5):
                kt = n - 4 + o
                P.op("pe", mm(Sb(hh)[:, o, :], bK[hh * 64:(hh + 1) * 64, kt * 128:(kt + 1) * 128],
                              bQ[hh * 64:(hh + 1) * 64, n * 128:(n + 1) * 128], True, True),
                     reads=["bQ", "bK"], writes=[f"ps{2 * hh}", f"ps{2 * hh + 1}"])
        cols = slice(tl * 128, (tl + 1) * 128)
        P.op("pe", mm(Sc, B["cK"][:, cols], B["cQ"][:, cols], True, True), reads=[gk + "q", gk + "k"], writes=["ps5"])
        P.op("pe", mm(kv, B["kz"][:, tl, :], B["cV"][:, tl, :], True, True), reads=[gk + "kz", gk + "v"], writes=["ps5"])

    def s2(n):
        o_lo = max(0, 4 - n)
        for hh in range(2):
            P.op("act", actf(hv(E, hh)[:, o_lo:5, :], Sb(hh)[:, o_lo:5, :], AF.Exp, scale=0.125),
                 reads=[f"ps{2 * hh}", f"ps{2 * hh + 1}"], writes=[f"E{hh}"])
            P.op("dve", tt(hv(PTb, hh)[:, o_lo:5, :], hv(E, hh)[:, o_lo:5, :], hv(EB, hh)[:, o_lo:5, :], ALU.mult),
                 reads=[f"E{hh}", "EB"], writes=[f"PTb{hh}"])
        PTc = T["PTc"][n % 2]
        P.op("dve", tt(PTc[:], Sc, T["DT"][:], ALU.mult), reads=["ps5", "cconst"], writes=[f"PTc{n % 2}"])
        P.op("dve", stt(state[:], state[:], T["cd"][0:64, 0:1], kv, ALU.mult, ALU.add),
             reads=["state", "ps5", "cconst"], writes=["state"])
        P.op("act", actf(state_bf[n % 2][:], state[:], AF.Copy), reads=["state"], writes=[f"state_bf{n % 2}"])

    def s3(n):
        g, tl, gb, gk, B = grp(n)
        o_lo = max(0, 4 - n)
        for hh in range(2):
            for o in range(o_lo, 5):
                kt = n - 4 + o
                P.op("pe", mm(Ob[:, hh * 65:(hh + 1) * 65], hv(PTb, hh)[:, o, :], bV[:, kt, hh, :], o == o_lo, o == 4),
                     reads=[f"PTb{hh}", "bV"], writes=["ps4"])
        cols = slice(tl * 128, (tl + 1) * 128)
        PTc = T["PTc"][n % 2]
        P.op("pe", mm(Oc, PTc[:], B["cV"][:, tl, :], True, False), reads=[f"PTc{n % 2}", gk + "v"], writes=["ps6"])
        P.op("pe", mm(Oc, B["qxi"][:, cols], state_bf[(n - 1) % 2][:], False, True),
             reads=[gk + "qxi", f"state_bf{(n - 1) % 2}"], writes=["ps6"])

    def s4(n):
        g, tl, gb, gk, B = grp(n)
        rcb = T["rcb"]
        P.op("dve", lambda e, o_=rcb[:], i_=Ob3[:, :, 64:65]: e.reciprocal(out=o_, in_=i_), reads=["ps4"], writes=["rcb"])
        yb = T["yb"][n % 2]
        P.op("dve", tt(yb[:].rearrange("p (h e) -> p h e", h=2), Ob3[:, :, 0:64], rcb[:].to_broadcast([128, 2, 64]), ALU.mult),
             reads=["ps4", "rcb"], writes=[f"yb{n % 2}"])
        ssc, lsc, rsc, junkc, tmpc = T["ssc"], T["lsc"], T["rsc"], T["junkc"], T["tmpc"]
        P.op("act", actf(junkc[:], Oc, AF.Square, accum_out=ssc[:]), reads=["ps6"], writes=["ssc", "junkc"])
        P.op("act", actf(lsc[:], ssc[:], AF.Ln, scale=1.0 / 128, bias=C.epsc[:]), reads=["ssc", "epsc"], writes=["lsc"])
        P.op("act", actf(rsc[:], lsc[:], AF.Exp, scale=-0.5), reads=["lsc"], writes=["rsc"])
        P.op("dve", stt(tmpc[:], Oc, rsc[:, 0:1], T["cnorm"][:], ALU.mult, ALU.mult),
             reads=["ps6", "rsc", "cconst"], writes=["tmpc"])
        yc = T["yc"][n % 2]
        P.op("pool", tt(yc[:], tmpc[:], B["cG"][:, tl, :], ALU.mult), reads=["tmpc", gk + "g"], writes=[f"yc{n % 2}"])

    def s5(n):
        ysb = T["ystBC"][(n // 4) % 2]
        ysk = f"ystBC{(n // 4) % 2}"
        emit_tr_out(C, T["yb"][n % 2][:], f"yb{n % 2}", ysb[:, 0, (n % 4) * 128:(n % 4 + 1) * 128], ysk)
        emit_tr_out(C, T["yc"][n % 2][:], f"yc{n % 2}", ysb[:, 1, (n % 4) * 128:(n % 4 + 1) * 128], ysk)
        if n % 4 == 3:
            d.store_bc(P, n // 4, ysb[:], ysk)

    load_group(0)
    derive_group(0)
    s1(0)
    s2(0)
    for n in range(NKT):
        if n + 1 < NKT:
            s1(n + 1)
        s3(n)
        if n + 1 < NKT:
            s2(n + 1)
        s4(n)
        if n >= 1:
            s5(n - 1)
    s5(NKT - 1)


def emit_mix_a(C, d, T, hooks=None):
    P = C.P
    aQ, aK, aV = T["aQ"], T["aK"], T["aV"]
    nlam = T["nlam"]
    steps = [(qb, kt) for qb in range(16) for kt in range(4 * qb + 4)]

    def geom(i):
        qb, kt = steps[i]
        dd = kt - 4 * qb
        q0 = max(dd, 0) * 128
        sbi = i % 2
        S = C.bank(2 * sbi, 2).rearrange("p (c n) -> p c n", c=2)
        return qb, kt, dd, q0, S, [f"ps{2 * sbi}", f"ps{2 * sbi + 1}"]

    def emit_s(i):
        qb, kt, dd, q0, S, skeys = geom(i)
        for c in range(2):
            P.op("pe", mm(S[:, c, q0:512], aK[c * 64:(c + 1) * 64, kt * 128:(kt + 1) * 128],
                          aQ[c * 64:(c + 1) * 64, qb * 512 + q0:(qb + 1) * 512], True, True),
                 reads=["aQ", "aK"], writes=[skeys[c]])

    emit_s(0)
    for i in range(len(steps)):
        qb, kt, dd, q0, S, skeys = geom(i)
        PT = T["PT"][i % 3]
        pk = f"pt{i % 3}"
        P.op("act", actf(PT[:, :, q0:512], S[:, :, q0:512], AF.Exp, scale=0.125), reads=skeys, writes=[pk])
        if dd >= 0:
            P.op("pool", lambda e, o_=PT[64:128, :, q0:q0 + 64]: e.memset(o_, 0.0), reads=[], writes=[pk])
        if i + 1 < len(steps):
            emit_s(i + 1)
        for c in range(2):
            for qt in range(max(dd, 0), 4):
                r = qt * 2 + c
                bk = 4 + r // 3
                off = (r % 3) * 129
                P.op("pe", mm(C.bank(bk)[:, off:off + 129], PT[:, c, qt * 128:(qt + 1) * 128], aV[:, kt, :],
                              kt == 0 and r in (0, 4, 6), kt == 4 * qb + qt),
                     reads=[pk, "aV"], writes=[f"ps{bk}"])
        if kt != 4 * qb + 3:
            continue
        yst = T["ystA"][qb % 2]
        ysk = f"ystA{qb % 2}"
        for qt in range(4):
            r0, r1 = qt * 2, qt * 2 + 1
            O0 = C.bank(4 + r0 // 3)[:, (r0 % 3) * 129:(r0 % 3) * 129 + 129]
            O1 = C.bank(4 + r1 // 3)[:, (r1 % 3) * 129:(r1 % 3) * 129 + 129]
            k0, k1 = f"ps{4 + r0 // 3}", f"ps{4 + r1 // 3}"
            rc, oa, ob = T["rc"], T["oa"], T["ob"]
            P.op("dve", lambda e, o_=rc[:, 0:1], i_=O0[:, 128:129]: e.reciprocal(out=o_, in_=i_), reads=[k0], writes=["rc0"])
            P.op("dve", lambda e, o_=rc[:, 1:2], i_=O1[:, 128:129]: e.reciprocal(out=o_, in_=i_), reads=[k1], writes=["rc1"])
            P.op("dve", tt(rc[:, 2:3], rc[:, 1:2], nlam[:], ALU.mult), reads=["rc1", "lam"], writes=["rc2"])
            P.op("dve", ts(oa[:], O0[:, 0:128], rc[:, 0:1], ALU.mult), reads=[k0, "rc0"], writes=["oa"])
            P.op("dve", stt(ob[:], O1[:, 0:128], rc[:, 2:3], oa[:], ALU.mult, ALU.add), reads=[k1, "rc2", "oa"], writes=["ob"])
            ssa, lsa, rsa, junka = T["ssa"], T["lsa"], T["rsa"], T["junka"]
            P.op("act", actf(junka[:], ob[:], AF.Square, accum_out=ssa[:]), reads=["ob"], writes=["ssa", "junka"])
            P.op("act", actf(lsa[:], ssa[:], AF.Ln, scale=1.0 / 128, bias=C.epsc[:]), reads=["ssa", "epsc"], writes=["lsa"])
            P.op("act", actf(rsa[:], lsa[:], AF.Exp, scale=-0.5), reads=["lsa"], writes=["rsa"])
            ya = T["ya"][qt % 2]
            P.op("dve", stt(ya[:], ob[:], rsa[:, 0:1], T["sublnS"][:], ALU.mult, ALU.mult),
                 reads=["ob", "rsa", "sublnS"], writes=[f"ya{qt % 2}"])
            emit_tr_out(C, ya[:], f"ya{qt % 2}", yst[:, qt * 128:(qt + 1) * 128], ysk)
        d.store_a(P, qb, yst[:], ysk)
        if hooks and qb in hooks:
            hooks[qb]()


def mix_consts(j, rel_bias, layer):
    gamma = np.float32(1.0 - 2.0 ** (-5.0 - j))
    lg = np.log(gamma).astype(np.float32)
    pos = np.arange(128, dtype=np.float32)
    diff = pos[None, :] - pos[:, None]
    DT = np.where(diff >= 0, np.exp(lg * np.maximum(diff, 0.0)), 0.0).astype(np.float32) * np.float32(0.125)
    zeta = (np.exp(lg * (127.0 - pos)) * 0.125).astype(np.float32).reshape(128, 1)
    xi = np.broadcast_to(np.exp(lg * (pos + 1.0)).astype(np.float32)[None, :], (64, 128))
    cd = np.full((128, 1), np.exp(lg * 128.0), np.float32)
    k = np.arange(128)[:, None]
    q = np.arange(128)[None, :]
    biasT = np.zeros((128, 10, 128), np.float32)
    maskB = np.zeros((128, 5, 128), np.float32)
    for o in range(5):
        rel = q - k + 128 * (4 - o)
        idx = np.clip(rel, -63, 256) + 63
        for hh in range(2):
            biasT[:, hh * 5 + o, :] = rel_bias[2 * j + hh][idx]
        dc = (q >= 64).astype(np.int64) - (k >= 64).astype(np.int64) + 2 * (4 - o)
        maskB[:, o, :] = ((dc >= 0) & (dc <= 8)).astype(np.float32)
    lam_init = 0.8 - 0.6 * math.exp(-0.3 * layer)
    li = np.zeros((128, 2), np.float32)
    li[:, 0] = lam_init
    li[:, 1] = 1.0 - lam_init
    return dict(DT=DT, zeta=zeta, xi=np.ascontiguousarray(xi), cd=cd, biasT=biasT, maskB=maskB, laminit=li)


def emit_merge(C, x, mT, wbr, gb_all, yt_all, wbr_d, wout_d, yt_load, g_d, scr, after_first_loads=None):
    P = C.P
    allw = [f"wb{i}" for i in range(6)]
    for m in range(3):
        P.op("pq", dma(wbr[:, m, :, :], wbr_d[m].rearrange("(j p) c -> p j c", p=128)), writes=allw)
    gb2 = [gb_all[:, 0:12, :], gb_all[:, 12:24, :]]
    yt2 = [yt_all[:, 0:12, :], yt_all[:, 12:24, :]]
    seq = [(tb, h) for tb in range(4) for h in range(2)]

    def issue_loads(idx):
        tb, h = seq[idx]
        cols = slice(tb * 512, (tb + 1) * 512)
        if h == 0:
            yt_load(P, yt2[tb % 2], tb, f"ytb{tb % 2}")
        for m in range(3):
            f0 = m * 8 + h * 4
            P.op("sp", dma(gb2[idx % 2][:, m * 4:(m + 1) * 4, :], g_d[f0:f0 + 4, :, cols].rearrange("f p n -> p f n")),
                 writes=[f"gbuf{idx % 2}"])

    issue_loads(0)
    if after_first_loads is not None:
        after_first_loads()
    it = 0
    for idx, (tb, h) in enumerate(seq):
        if idx + 1 < len(seq):
            issue_loads(idx + 1)
        cols = slice(tb * 512, (tb + 1) * 512)
        ytb, gbuf = yt2[tb % 2], gb2[idx % 2]
        yk, gk = f"ytb{tb % 2}", f"gbuf{idx % 2}"
        for fo4 in range(4):
            fo = h * 4 + fo4
            par = it % 2
            b0 = 3 * par
            it += 1
            t1, t2, sq = scr["t12"][par], scr["t22"][par], scr["sq2"][par]
            k1, k2, k3 = f"t1_{par}", f"t2a_{par}", f"sq_{par}"
            for m in range(3):
                for j in range(4):
                    P.op("pe", mm(C.bank(b0 + m), wbr[:, m, j, fo * 128:(fo + 1) * 128], ytb[:, j * 3 + m, :], j == 0, j == 3),
                         reads=allw + [yk], writes=[f"ps{b0 + m}"])
            P.op("dve", tt(t1, C.bank(b0), gbuf[:, fo4, :], ALU.mult), reads=[f"ps{b0}", gk], writes=[k1])
            P.op("dve", tt(t2, C.bank(b0 + 1), gbuf[:, 4 + fo4, :], ALU.mult), reads=[f"ps{b0 + 1}", gk], writes=[k2])
            P.op("dve", tt(sq, C.bank(b0 + 2), gbuf[:, 8 + fo4, :], ALU.mult), reads=[f"ps{b0 + 2}", gk], writes=[k3])
            P.op("pool", tt(t1, t1, t2, ALU.add), reads=[k1, k2], writes=[k1])
            P.op("pool", tt(mT[:, fo, cols], t1, sq, ALU.add), reads=[k1, k3],
                 writes=[f"hT{t}" for t in range(tb * 4, tb * 4 + 4)])
    wo = yt_all.rearrange("p f n -> p (f n)")[:, 0:8192].rearrange("p (k c) -> p k c", k=8)
    P.op("pq", dma(wo, wout_d.rearrange("(k p) c -> p k c", p=128)), writes=["ytb0", "ytb1"])
    for t in range(NT):
        b0 = 6 * (t % 2)
        py = C.bank(b0, 2)
        for n in range(2):
            for k in range(8):
                P.op("pe", mm(py[:, n * 512:(n + 1) * 512], mT[:, k, t * 128:(t + 1) * 128], wo[:, k, n * 512:(n + 1) * 512], k == 0, k == 7),
                     reads=["ytb0", "ytb1", f"hT{t}"], writes=[f"ps{b0 + n}"])
        P.op("dve", tt(x[:, t, :], x[:, t, :], py, ALU.add), reads=[f"ps{b0}", f"ps{b0 + 1}", f"x{t}"], writes=[f"x{t}"])


DEPTH = 2
GROUPS = [[0, 1, 2, 3], [4, 5, 6, 7]]
PW = 512
SEG = {"oAq": ("A", 0), "oAk": ("A", 2048), "oAv": ("A", 4096),
       "oBq": ("B", 0), "oBk": ("B", 2048), "oBv": ("B", 4096),
       "oCqk": ("C", 0), "oCv": ("C", 2048), "oCg": ("C", 4096), "oCkt": ("C", 6144)}
SEGW = {k: 2048 for k in SEG}
SEGW["oCkt"] = 1024
FAMW = {"A": 6144, "B": 6144, "C": 7168}
NPIECE = {f: w // PW for f, w in FAMW.items()}
_RANK = {}


def _rank(e):
    k = id(e)
    if k not in _RANK:
        _RANK[k] = e.snap(e.partition_id() % 4, min_val=0, max_val=3)
    return _RANK[k]


def _gather(P, snd_ap, rcv_ap, reads, wkey):
    P.op("cc", lambda e: e.collective_compute("AllGather", ALU.bypass, replica_groups=GROUPS,
                                              ins=[snd_ap.opt()], outs=[rcv_ap.opt()]), reads=reads, writes=[wkey])


class PreIO:
    def __init__(self, snd, rcv, gsc):
        self.snd, self.rcv = snd, rcv
        self.g = gsc.ap()
        self.deferred = []
        self.early = None

    def gate(self, g):
        return self.g[g * 4:(g + 1) * 4].rearrange("c p n -> p c n")

    def piece(self, f, k):
        return self.snd[f].ap()[k].rearrange("(j p) w -> p j w", p=128)

    def store_block(self, P, blk, kind, stg, sk, kstage):
        f, off = SEG[PROJ_OUT[blk]]
        k0 = off // PW
        later = []
        for k in range(4):
            cs = slice(k * PW, (k + 1) * PW)
            dv = self.piece(f, k0 + k)
            keys = []
            if kind == "R":
                for j in range(4):
                    h2 = slice((j % 2) * 64, (j % 2 + 1) * 64)
                    keys += [f"s{f}{k0 + k}q{j}", f"s{f}{k0 + k}k{j}"]
                    P.op("sp", dma(dv[0:64, j, :], stg[h2, j // 2, cs]), reads=[sk], writes=[keys[-2]])
                    P.op("sp", dma(dv[64:128, j, :], stg[h2, 2 + j // 2, cs]), reads=[sk], writes=[keys[-1]])
            else:
                keys = [f"s{f}{k0 + k}"]
                P.op("sp", dma(dv, stg[:, :, cs]), reads=[sk], writes=keys)
            later.append(lambda extra=(), kk=k0 + k, keys=keys: _gather(P, self.snd[f].ap()[kk], self.rcv[f].ap()[kk], list(keys) + list(extra), f"rcv{f}{kk}"))
        if kind == "R":
            f2, off2 = SEG["oCkt"]
            for k in range(2):
                kk = off2 // PW + k
                P.op("sp", dma(self.piece(f2, kk), kstage[:, :, k * PW:(k + 1) * PW]), reads=["kstage"], writes=[f"s{f2}{kk}"])
                later.append(lambda extra=(), kk=kk: _gather(P, self.snd[f2].ap()[kk], self.rcv[f2].ap()[kk], [f"s{f2}{kk}"] + list(extra), f"rcv{f2}{kk}"))
        if f == "C":
            self.deferred += later
            return []
        return later


class MixIO:
    def __init__(self, rcv, mine, snd2, rcv2):
        self.r = {f: t.ap() for f, t in rcv.items()}
        self.m = {f: t.ap() for f, t in mine.items()}
        self.snd2, self.rcv2 = snd2, rcv2
        self.pend2 = []

    def flush2(self):
        for fn in self.pend2:
            fn()
        self.pend2 = []

    def fetch(self, P, f):
        def fn(e):
            src = self.r[f].rearrange("k (i q) w -> q k i w", q=512)[bass.ds(_rank(e) * 128, 128), :, :, :]
            return e.dma_start(out=self.m[f], in_=src)
        P.op("sp", fn, reads=[f"rcv{f}{k}" for k in range(NPIECE[f])], writes=["mine" + f])

    def key(self, name):
        return "mine" + SEG[name][0]

    def load(self, P, dst, name, i, writes, rows=(0, 128), t=None, he=None):
        (f, off), w = SEG[name], SEGW[name]
        for kk in range(w // PW):
            src = self.m[f][rows[0]:rows[0] + rows[1], off // PW + kk, i, :]
            if he is not None:
                h, e_, hh = he
                a = PW // (h * e_)
                src = src.rearrange("p (a h e) -> p a h e", h=h, e=e_)[:, :, hh, :]
                dv = dst[:, kk * a:(kk + 1) * a, :]
            elif t is not None:
                dd = w // t
                a = PW // dd
                src = src.rearrange("p (a d) -> p a d", d=dd)
                dv = dst[:, kk * a:(kk + 1) * a, :]
            else:
                dv = dst[:, kk * PW:(kk + 1) * PW]
            P.op("sp", dma(dv, src), reads=["mine" + f], writes=writes)

    def store_a(self, P, qb, yst, ysk):
        k = qb // 4
        P.op("sp", dma(self.snd2["A"].ap()[k][:, (qb % 4) * 512:(qb % 4 + 1) * 512], yst), reads=[ysk], writes=[f"s2A{k}_{qb % 4}"])
        self.flush2()
        if qb % 4 == 3:
            self.pend2.append(lambda k=k: _gather(P, self.snd2["A"].ap()[k], self.rcv2["A"].ap()[k], [f"s2A{k}_{i}" for i in range(4)], f"rcv2A{k}"))

    def store_bc(self, P, q4, ysb, ysk):
        k = q4 // 2
        dv = self.snd2["BC"].ap()[k].rearrange("(m p) n -> p m n", p=128)
        P.op("sp", dma(dv[:, :, (q4 % 2) * 512:(q4 % 2 + 1) * 512], ysb), reads=[ysk], writes=[f"s2BC{k}_{q4 % 2}"])
        self.flush2()
        if q4 % 2 == 1:
            self.pend2.append(lambda k=k: _gather(P, self.snd2["BC"].ap()[k], self.rcv2["BC"].ap()[k], [f"s2BC{k}_{i}" for i in range(2)], f"rcv2BC{k}"))


MIXC = {"DT": ([128, 128], F32), "zeta": ([128, 1], F32), "xi": ([64, 128], F32), "cd": ([128, 1], F32),
        "maskB": ([128, 5, 128], F32)}
MIXL = {"biasT": ([128, 10, 128], F32), "laminit": ([128, 2], F32), "lamv": ([4, 64], F32),
        "a_subln": ([128], F32), "cnorm": ([128], F32)}
WNAMES = {"ffn1_norm": [DM], "ffn1_w_gate": [DM, DFF], "ffn1_w_up": [DM, DFF], "ffn1_w_down": [DFF, DM],
          "mix_norm": [DM], "w_in": [DM, IN_COLS], "a_q_norm": [64], "a_k_norm": [64], "b_q_norm": [64], "b_k_norm": [64],
          "w_branch_a": [512, DM], "w_branch_b": [512, DM], "w_branch_c": [512, DM], "w_out": [DM, DM],
          "ffn2_norm": [DM], "ffn2_w_gate": [DM, DFF], "ffn2_w_up": [DM, DFF], "ffn2_w_down": [DFF, DM]}


def emit_mix_setup(C, d, T, cst, l):
    P = C.P
    for n, sh in (("cQ", [64, TOK]), ("cK", [64, TOK]), ("cKt", [128, 16, 64]), ("cV", [128, 16, 128]),
                  ("cG", [128, 16, 128]), ("qxi", [64, TOK]), ("kz", [128, 16, 64])):
        T[n] = [C.sb(f"{n}{i}", sh, BF16) for i in range(2)]
    T["DT"] = C.sb("DT", [128, 128], F32)
    T["zeta"] = C.sb("zeta", [128, 1], F32)
    T["xi"] = C.sb("xi", [64, 128], F32)
    T["cd"] = C.sb("cd", [128, 1], F32)
    T["cnorm"] = C.sb("cnorm", [128, 128], F32)
    for n in ("DT", "zeta", "xi", "cd"):
        P.op("sp", dma(T[n], cst[n]), writes=["cconst"])
    P.op("sp", dma_nc(T["cnorm"], cst["cnorm"][l].partition_broadcast(128)), writes=["cconst"])
    bT = C.sb("biasT", [128, 10, 128], F32)
    mB = C.sb("maskB", [128, 5, 128], F32)
    T["EB"] = C.sb("EB", [128, 1280], BF16)
    P.op("sp", dma(bT, cst["biasT"][l]), writes=["bT"])
    P.op("sp", dma(mB, cst["maskB"]), writes=["mB"])
    P.op("act", actf(bT, bT, AF.Exp), reads=["bT"], writes=["bT"])
    P.op("dve", tt(T["EB"].rearrange("p (h o q) -> p h o q", h=2, o=5), bT.rearrange("p (h o) q -> p h o q", h=2),
                   mB.unsqueeze(1).to_broadcast([128, 2, 5, 128]), ALU.mult), reads=["bT", "mB"], writes=["EB"])
    lv = C.sb("lamv", [128, 4, 64], F32)
    P.op("sp", dma_nc(lv, cst["lamv"][l].partition_broadcast(128)), writes=["lv"])
    li = C.sb("laminit", [128, 2], F32)
    P.op("sp", dma(li, cst["laminit"][l]), writes=["li"])
    lp = C.sb("lamp", [128, 2, 64], F32)
    ls = C.sb("lams", [128, 4], F32)
    lv4 = lv.rearrange("p (a b) d -> p a b d", a=2)
    P.op("dve", tt(lp, lv4[:, :, 0, :], lv4[:, :, 1, :], ALU.mult), reads=["lv"], writes=["lp"])
    P.op("dve", lambda e: e.tensor_reduce(out=ls[:, 0:2], in_=lp, axis=AX.X, op=ALU.add), reads=["lp"], writes=["ls01"])
    P.op("act", actf(ls[:, 0:2], ls[:, 0:2], AF.Exp), reads=["ls01"], writes=["ls01"])
    P.op("dve", tt(ls[:, 2:3], ls[:, 0:1], ls[:, 1:2], ALU.subtract), reads=["ls01"], writes=["ls2"])
    P.op("dve", tt(ls[:, 3:4], ls[:, 2:3], li[:, 0:1], ALU.add), reads=["ls2", "li"], writes=["ls3"])
    T["nlam"] = C.sb("nlam", [128, 1], F32)
    P.op("dve", ts(T["nlam"], ls[:, 3:4], -1.0, ALU.mult), reads=["ls3"], writes=["lam"])
    sub = C.sb("subln", [128, 128], F32)
    P.op("sp", dma_nc(sub, cst["a_subln"][l].partition_broadcast(128)), writes=["sub"])
    T["sublnS"] = C.sb("sublnS", [128, 128], F32)
    P.op("dve", ts(T["sublnS"], sub, li[:, 1:2], ALU.mult), reads=["sub", "li"], writes=["sublnS"])
    T["E"] = C.sb("E", [128, 1280], BF16)
    T["PTb"] = C.sb("PTb", [128, 1280], BF16)
    T["rcb"] = C.sb("rcb", [128, 2, 1], F32)
    for n in ("yb", "yc", "ya", "PTc"):
        T[n] = [C.sb(f"{n}{i}", [128, 128], BF16) for i in range(2)]
    T["ystBC"] = [C.sb(f"ystBC{i}", [128, 2, 512], BF16) for i in range(2)]
    T["ystA"] = [C.sb(f"ystA{i}", [128, 512], BF16) for i in range(2)]
    T["state"] = C.sb("state", [64, 128], F32)
    T["state_bf"] = [C.sb(f"state_bf{i}", [64, 128], BF16) for i in range(2)]
    for n in ("ssc", "lsc", "rsc", "ssa", "lsa", "rsa"):
        T[n] = C.sb(n, [128, 1], F32)
    for n in ("junkc", "tmpc", "junka", "oa", "ob"):
        T[n] = C.sb(n, [128, 128], F32)
    T["rc"] = C.sb("rc", [128, 3], F32)
    T["PT"] = [C.sb(f"PT{i}", [128, 2, 512], BF16) for i in range(3)]


def emit_mix_setup_b(C, d, T):
    P = C.P
    d.fetch(P, "B")
    for i in range(4):
        for n, src in (("bQ", "oBq"), ("bK", "oBk")):
            d.load(P, T[n][:, i * TOK:(i + 1) * TOK], src, i, [n])
        for hh in range(2):
            d.load(P, T["bV"][:, i * 16:(i + 1) * 16, hh, 0:64], "oBv", i, ["bV"], he=(2, 64, hh))


HEADW = 1152


def emit_pre_tok(C, x, hT, wbuf, stage, w_in_d, mixnorm_d, snd_h, rcv_h, gsc, scr):
    P = C.P
    emit_rmsnorm_hT(C, x, hT, mixnorm_d, "m", scr)

    def wview(bi):
        return wbuf[bi % 2][:, 0:4096].rearrange("p (k c) -> p k c", k=8)

    def wkeys(bi):
        return [f"wb{3 * (bi % 2)}", f"wb{3 * (bi % 2) + 1}"]

    def issue_w(g):
        if g < 6:
            blk = 9 + g
            P.op("pq", dma(wview(g), w_in_d[:, blk * 512:(blk + 1) * 512].rearrange("(k p) c -> p k c", p=128)), writes=wkeys(g))

    issue_w(0)
    issue_w(1)
    for k in range(8):
        P.op("sp", dma(snd_h.ap()[k], hT[:, k, :]), reads=[f"hT{t}" for t in range(NT)], writes=[f"sndh{k}"])
    for k in range(8):
        _gather(P, snd_h.ap()[k], rcv_h.ap()[k], [f"sndh{k}"], f"rcvh{k}")
    inst = 0
    for g in range(6):
        if g >= 1:
            issue_w(g + 1)
        wv = wview(g)
        stg = stage[g % 2]
        sk = f"stage{g % 2}"
        for f in range(4):
            for tb in range(4):
                pb = inst % 2
                inst += 1
                ps = C.bank(pb)
                for k in range(8):
                    P.op("pe", mm(ps, wv[:, k, f * 128:(f + 1) * 128], hT[:, k, tb * 512:(tb + 1) * 512], k == 0, k == 7),
                         reads=wkeys(g) + [f"hT{t}" for t in range(tb * 4, tb * 4 + 4)], writes=[f"ps{pb}"])
                P.op("act", actf(stg[:, f, tb * 512:(tb + 1) * 512], ps, AF.Sigmoid), reads=[f"ps{pb}"], writes=[sk])
        P.op("sp", dma(gsc.ap()[g * 4:(g + 1) * 4].rearrange("c p n -> p c n"), stg[:]), reads=[sk])


def emit_head_proj(C, T, rcv_h, mineC, wh_d, cos_d, sin_d, gains_d, l):
    P = C.P
    QK4 = T["QK4"]
    Wh = C.sb("Wh", [128, 8, HEADW], BF16)
    hTt = [C.sb(f"hTt{i}", [128, 8, 512], BF16) for i in range(2)]
    cs = [(C.sb(f"cosg{i}", [128, 16, 64], F32), C.sb(f"sing{i}", [128, 16, 64], F32)) for i in range(2)]
    G8 = C.sb("G8", [128, 8, 64], F32)
    scr = {}
    ND = 3
    for n in ("sq", "xq", "xg", "t1", "t2"):
        scr[n] = [C.sb(f"h{n}{i}", [128, 512], F32) for i in range(ND)]
    res = [C.sb(f"hres{i}", [128, 512], BF16) for i in range(ND)]
    resz = [C.sb(f"hresz{i}", [128, 128], BF16) for i in range(ND)]
    zt = [[C.sb(f"hz{n}{i}", [128, 128], F32) for i in range(ND)] for n in ("a", "b")]
    s8 = C.sb("hs8", [128, 8 * ND], F32)
    l8 = C.sb("hl8", [128, 8 * ND], F32)
    r8 = C.sb("hr8", [128, 8 * ND], F32)
    eg = [C.sb(f"heg{i}", [128, 128], F32) for i in range(ND)]
    cQK = C.sb("cQKst", [128, 2048], BF16)
    cVs = C.sb("cVst", [128, 16, 128], BF16)
    cGs = C.sb("cGst", [128, 16, 128], BF16)
    cKs = C.sb("cKst", [128, 16, 64], BF16)
    P.op("pq", dma(Wh, wh_d.rearrange("(k p) c -> p k c", p=128)), writes=["Wh"])
    for i in range(4):
        for r in range(2):
            P.op("sp", dma_nc(G8[:, 2 * i + r, :], gains_d[i].partition_broadcast(128)), writes=["G8"])
    rh = rcv_h.ap().rearrange("k (i p) n -> i p k n", p=128)

    def v3(ap, h=8):
        return ap.rearrange("p (h d) -> p h d", h=h)

    def mm_stage(n):
        i, tl = n // 16, n % 16
        par = n % 2
        if n % 4 == 0:
            hb = (n // 4) % 2
            P.op("sp", dma(hTt[hb], rh[i][:, :, (tl // 4) * 512:(tl // 4 + 1) * 512]),
                 reads=[f"rcvh{k}" for k in range(8)], writes=[f"hTt{hb}"])
        hb = (n // 4) % 2
        hcols = slice((n % 4) * 128, (n % 4 + 1) * 128)
        pX, pY, pZ = C.bank(par), C.bank(2 + par), C.bank(4 + par)[:, 0:128]
        for (ps, c0, c1, key) in ((pX, 0, 512, f"ps{par}"), (pY, 512, 1024, f"ps{2 + par}"), (pZ, 1024, 1152, f"ps{4 + par}")):
            for k in range(8):
                P.op("pe", mm(ps, hTt[hb][:, k, hcols], Wh[:, k, c0:c1], k == 0, k == 7),
                     reads=[f"hTt{hb}", "Wh"], writes=[key])

    gt = [C.sb(f"hgt{i}", [128, 128], F32) for i in range(ND)]

    def ctx(n):
        i, tl = n // 16, n % 16
        par, sd = n % 2, n % ND
        cosg, sing = cs[i % 2]
        return dict(i=i, tl=tl, par=par, sd=sd, cosg=cosg, sing=sing, csk=f"cs{i % 2}",
                    pX=C.bank(par), pY=C.bank(2 + par), pZ=C.bank(4 + par)[:, 0:128],
                    sq=scr["sq"][sd], xq=scr["xq"][sd], xg=scr["xg"][sd], t1=scr["t1"][sd], t2=scr["t2"][sd],
                    s8p=s8[:, sd * 8:sd * 8 + 8], l8p=l8[:, sd * 8:sd * 8 + 8], r8p=r8[:, sd * 8:sd * 8 + 8],
                    rs=res[sd], rk=f"hres{sd}", za=zt[0][sd], zb=zt[1][sd], rz=resz[sd], zk=f"hz{sd}",
                    eg=eg[sd], egk=f"heg{sd}", gt=gt[sd], gtk=f"hgt{sd}", kq=lambda m, sd=sd: f"h{m}_{sd}")

    def post_a(n):
        c = ctx(n)
        kq, par, tl = c["kq"], c["par"], c["tl"]
        if tl == 0:
            P.op("sp", dma(c["cosg"], cos_d[c["i"]]), writes=[c["csk"]])
            P.op("sp", dma(c["sing"], sin_d[c["i"]]), writes=[c["csk"]])
        pX, pY, pZ = c["pX"], c["pY"], c["pZ"]
        P.op("act", actf(c["sq"], pX, AF.Square), reads=[f"ps{par}"], writes=[kq("sq")])
        P.op("dve", lambda e, o=c["s8p"], i_=v3(c["sq"]): e.tensor_reduce(out=o, in_=i_, axis=AX.X, op=ALU.add), reads=[kq("sq")], writes=[kq("s8")])
        P.op("act", actf(c["l8p"], c["s8p"], AF.Ln, scale=1.0 / 64, bias=C.epsc[:]), reads=[kq("s8"), "epsc"], writes=[kq("l8")])
        P.op("act", actf(c["r8p"], c["l8p"], AF.Exp, scale=-0.5), reads=[kq("l8")], writes=[kq("r8")])
        P.op("dve", tt(v3(c["xq"]), v3(pX), c["r8p"].unsqueeze(2).to_broadcast([128, 8, 64]), ALU.mult), reads=[f"ps{par}", kq("r8")], writes=[kq("xq")])
        cb2 = c["cosg"][:, tl, :].unsqueeze(1).to_broadcast([128, 2, 64])
        sa2 = c["sing"][:, tl, 0:32].unsqueeze(1).to_broadcast([128, 2, 32])
        sb2 = c["sing"][:, tl, 32:64].unsqueeze(1).to_broadcast([128, 2, 32])
        pz3 = v3(pZ, 2)
        zk = c["zk"]
        P.op("dve", tt(v3(c["za"], 2), pz3, cb2, ALU.mult), reads=[f"ps{4 + par}", c["csk"]], writes=[zk + "a"])
        P.op("dve", tt(v3(c["zb"], 2)[:, :, 0:32], pz3[:, :, 32:64], sa2, ALU.mult), reads=[f"ps{4 + par}", c["csk"]], writes=[zk + "b0"])
        P.op("dve", tt(v3(c["zb"], 2)[:, :, 32:64], pz3[:, :, 0:32], sb2, ALU.mult), reads=[f"ps{4 + par}", c["csk"]], writes=[zk + "b1"])
        P.op("act", actf(T["aV"][:, n, 0:128], pY[:, 0:128], AF.Copy), reads=[f"ps{2 + par}"], writes=["aV"])
        P.op("act", actf(T["bV"][:, n, :, 0:64], pY[:, 128:256].rearrange("p (h e) -> p h e", h=2), AF.Copy),
             reads=[f"ps{2 + par}"], writes=["bV"])
        P.op("act", actf(cVs[:, tl, :], pY[:, 256:384], AF.Copy), reads=[f"ps{2 + par}"], writes=["cVst"])
        P.op("act", actf(c["eg"], pY[:, 384:512], AF.Exp, scale=-1.0), reads=[f"ps{2 + par}"], writes=[c["egk"]])
        P.op("act", actf(c["gt"], pY[:, 384:512], AF.Copy), reads=[f"ps{2 + par}"], writes=[c["gtk"]])

    def post_b(n):
        c = ctx(n)
        kq, par, tl, i = c["kq"], c["par"], c["tl"], c["i"]
        xq, xg, t1, t2, rs, rk = c["xq"], c["xg"], c["t1"], c["t2"], c["rs"], c["rk"]
        csk = c["csk"]
        P.op("dve", tt(v3(xg)[:, 0:4, :], v3(xq)[:, 0:4, :], G8[:, 0:4, :], ALU.mult), reads=[kq("xq"), "G8"], writes=[kq("xg")])
        P.op("dve", tt(v3(rs)[:, 4:8, :], v3(xq)[:, 4:8, :], G8[:, 4:8, :], ALU.mult), reads=[kq("xq"), "G8"], writes=[rk + "b"])
        xa = v3(xg)[:, 0:4, :]
        cb = c["cosg"][:, tl, :].unsqueeze(1).to_broadcast([128, 4, 64])
        sa = c["sing"][:, tl, 0:32].unsqueeze(1).to_broadcast([128, 4, 32])
        sb_ = c["sing"][:, tl, 32:64].unsqueeze(1).to_broadcast([128, 4, 32])
        P.op("dve", tt(v3(t1)[:, 0:4, :], xa, cb, ALU.mult), reads=[kq("xg"), csk], writes=[kq("t1")])
        P.op("pool", tt(v3(t2)[:, 0:4, 0:32], xa[:, :, 32:64], sa, ALU.mult), reads=[kq("xg"), csk], writes=[kq("t2a")])
        P.op("pool", tt(v3(t2)[:, 0:4, 32:64], xa[:, :, 0:32], sb_, ALU.mult), reads=[kq("xg"), csk], writes=[kq("t2b")])
        P.op("dve", tt(rs[:, 0:256], t1[:, 0:256], t2[:, 0:256], ALU.add), reads=[kq("t1"), kq("t2a"), kq("t2b")], writes=[rk + "a"])
        rz, zk = c["rz"], c["zk"]
        P.op("dve", tt(rz, c["za"], c["zb"], ALU.add), reads=[zk + "a", zk + "b0", zk + "b1"], writes=[zk + "r"])
        P.op("pool", lambda e, o=cKs[:, tl, :], i_=rz[:, 64:128]: e.tensor_copy(out=o, in_=i_), reads=[zk + "r"], writes=["cKst"])
        eg_, egk = c["eg"], c["egk"]
        P.op("dve", ts(eg_, eg_, 1.0, ALU.add), reads=[egk], writes=[egk])
        P.op("dve", lambda e, o=eg_, i_=eg_: e.reciprocal(out=o, in_=i_), reads=[egk], writes=[egk])
        P.op("dve", tt(cGs[:, tl, :], c["gt"], eg_, ALU.mult), reads=[c["gtk"], egk], writes=["cGst"])
        tb = 6 + par
        pst = C.bank_bf(tb)[:, 0:640]
        for cc_ in range(4):
            P.op("pe", tr(pst[:, cc_ * 128:(cc_ + 1) * 128], rs[:, cc_ * 128:(cc_ + 1) * 128], C.ident[:]),
                 reads=[rk + "a", rk + "b", "ident"], writes=[f"ps{tb}"])
        P.op("pe", tr(pst[:, 512:640], rz, C.ident[:]), reads=[zk + "r", "ident"], writes=[f"ps{tb}"])
        P.op("act", actf(QK4[:, :, n * 128:(n + 1) * 128], pst[:, 0:512].rearrange("p (c q) -> p c q", c=4), AF.Copy),
             reads=[f"ps{tb}"], writes=["QK4"])
        P.op("act", actf(cQK[:, tl * 128:(tl + 1) * 128], pst[:, 512:640], AF.Copy), reads=[f"ps{tb}"], writes=["cQKst"])
        if tl == 15:
            mc = mineC.ap()
            P.op("sp", dma(mc[:, 0:4, i, :], cQK.rearrange("p (k w) -> p k w", w=PW)), reads=["cQKst"], writes=["mineC"])
            P.op("sp", dma(mc[:, 4:8, i, :], cVs.rearrange("p t d -> p (t d)").rearrange("p (k w) -> p k w", w=PW)), reads=["cVst"], writes=["mineC"])
            P.op("sp", dma(mc[:, 8:12, i, :], cGs.rearrange("p t d -> p (t d)").rearrange("p (k w) -> p k w", w=PW)), reads=["cGst"], writes=["mineC"])
            P.op("sp", dma(mc[:, 12:14, i, :], cKs.rearrange("p t d -> p (t d)").rearrange("p (k w) -> p k w", w=PW)), reads=["cKst"], writes=["mineC"])

    mm_stage(0)
    mm_stage(1)
    for n in range(NKT):
        post_a(n)
        if n + 2 < NKT:
            mm_stage(n + 2)
        post_b(n)


def build_fused():
    nc = bass.Bass("TRN2", target_bir_lowering=False)

    def din(name, shape, dt=F32):
        return nc.dram_tensor(name, shape, dt, kind="ExternalInput").ap()

    x_d = din("x", [TOK, DM])
    Wd = {n: din(n, [DEPTH] + sh) for n, sh in WNAMES.items()}
    wh_d = din("w_head", [DEPTH, DM, HEADW])
    cos_d, sin_d = din("cos_all", [4, 128, NT, 64]), din("sin_all", [4, 128, NT, 64])
    idn = din("ident", [128, 128], BF16)
    cst = {n: din(n, sh, dt) for n, (sh, dt) in MIXC.items()}
    cst.update({n: din(n, [DEPTH] + sh, dt) for n, (sh, dt) in MIXL.items()})
    out_d = nc.dram_tensor("x_out", [TOK, DM], F32, kind="ExternalOutput").ap()
    snd_h = nc.dram_tensor("snd_h", [8, 128, TOK], BF16)
    rcv_h = nc.dram_tensor("rcv_h", [8, 4 * 128, TOK], BF16)
    mineC = nc.dram_tensor("mineC", [128, NPIECE["C"], 4, PW], BF16)
    snd2 = {"A": nc.dram_tensor("snd2A", [4, 128, TOK], BF16), "BC": nc.dram_tensor("snd2BC", [8, 256, 1024], BF16)}
    rcv2 = {"A": nc.dram_tensor("rcv2A", [4, 4 * 128, TOK], BF16), "BC": nc.dram_tensor("rcv2BC", [8, 4 * 256, 1024], BF16)}
    mine2 = {"A": nc.dram_tensor("mine2A", [4 * 128, TOK], BF16), "BC": nc.dram_tensor("mine2BC", [2, 4 * 256, 1024], BF16)}
    gsc = nc.dram_tensor("gsc", [24, 128, TOK], BF16)
    xs = nc.dram_tensor("xs", [TOK, DM], F32)
    with contextlib.ExitStack() as st:
        C = Ctx(nc, st)
        P = C.P
        P.op("sp", dma(C.ident, idn), writes=["ident"])
        base = C.mark()

        def token_layout():
            C.reset(base)
            L = {}
            L["x"] = C.sb("x", [128, NT, DM], F32)
            L["hT"] = C.sb("hT", [128, 8, TOK], BF16)
            wball = C.sb("wball", [128, 2 * 6144], BF16)
            L["wball"] = wball
            L["wbuf"] = [wball[:, 0:6144], wball[:, 6144:12288]]
            L["wbr"] = wball.rearrange("p (m j c) -> p m j c", m=3, j=4)
            u = C.mark()
            L["stage"] = [C.sb(f"stage{i}", [128, 4, TOK], BF16) for i in range(2)]
            L["kstage"] = C.sb("kstage", [128, 4, 1024], BF16)
            e1 = C.mark()
            C.reset(u)
            L["gbuf"] = C.sb("gbuf", [128, 24, 512], BF16)
            L["ytb"] = C.sb("ytb", [128, 24, 512], BF16)
            C.reset(max(e1, C.mark()))
            L["scr"] = make_scr(C)
            return L

        mix_io = MixIO({}, {"C": mineC}, snd2, rcv2)

        def yt_fetch_a(e):
            return e.dma_start(out=mine2["A"].ap(), in_=rcv2["A"].ap()[bass.ds(_rank(e), 1), :, :].rearrange("o r n -> (o r) n"))

        def yt_fetch_bc(e):
            return e.dma_start(out=mine2["BC"].ap(), in_=rcv2["BC"].ap()[bass.ds(_rank(e) * 2, 2), :, :])

        def yt_load(P, ytb, tb, key):
            y4 = ytb.rearrange("p (j m) n -> p j m n", m=3)
            P.op("sp", dma(y4[:, :, 0, :], mine2["A"].ap()[:, tb * 512:(tb + 1) * 512].rearrange("(j p) n -> p j n", p=128)),
                 reads=["mine2A"], writes=[key])
            for mm_ in range(2):
                P.op("sp", dma(y4[:, :, 1 + mm_, :], mine2["BC"].ap()[tb // 2][:, (tb % 2) * 512:(tb % 2 + 1) * 512]
                               .rearrange("(j m p) n -> p j m n", m=2, p=128)[:, :, mm_, :]), reads=["mine2BC"], writes=[key])

        for l in range(DEPTH):
            L = token_layout()
            x, hT, scr = L["x"], L["hT"], L["scr"]
            if l == 0:
                load_x(C, x, x_d)
            aT2 = [L["stage"][0][:, 0:2, :], L["stage"][0][:, 2:4, :]]
            wdb = [L["kstage"][:, 0:2, :].rearrange("p a n -> p (a n)"), L["kstage"][:, 2:4, :].rearrange("p a n -> p (a n)")]
            emit_ffn(C, x, hT, L["wball"], wdb, aT2, Wd["ffn1_w_gate"][l], Wd["ffn1_w_up"][l], Wd["ffn1_w_down"][l],
                     Wd["ffn1_norm"][l], scr, aT_keys=["stage0"], wd_keys=["kstage"])
            store_x(C, x, xs.ap())
            emit_pre_tok(C, x, hT, L["wbuf"], L["stage"], Wd["w_in"][l], Wd["mix_norm"][l], snd_h, rcv_h, gsc, scr)
            P.barrier()
            C.reset(base)
            T = {}
            T["QK4"] = C.sb("QK4", [128, 4, SEQ], BF16)
            for i, n in enumerate(("aQ", "aK", "bQ", "bK")):
                T[n] = T["QK4"][:, i, :]
            T["aV"] = C.sb("aV", [128, NKT, 129], BF16)
            T["bV"] = C.sb("bV", [128, NKT, 2, 65], BF16)
            P.op("pool", lambda e, T=T: e.memset(T["aV"][:, :, 128:129], 1.0), writes=["aV"])
            P.op("pool", lambda e, T=T: e.memset(T["bV"][:, :, :, 64:65], 1.0), writes=["bV"])
            m1 = C.mark()
            emit_head_proj(C, T, rcv_h, mineC, wh_d[l], cos_d, sin_d,
                           [Wd[n][l] for n in ("a_q_norm", "a_k_norm", "b_q_norm", "b_k_norm")], l)
            P.barrier()
            C.reset(m1)
            emit_mix_setup(C, mix_io, T, cst, l)
            emit_mix_a(C, mix_io, T)
            emit_mix_bc(C, mix_io, T)
            mix_io.flush2()
            P.barrier()
            P.op("sp", yt_fetch_a, reads=[f"rcv2A{k}" for k in range(4)], writes=["mine2A"])
            P.op("sp", yt_fetch_bc, reads=[f"rcv2BC{k}" for k in range(8)], writes=["mine2BC"])
            L = token_layout()
            x, hT, scr = L["x"], L["hT"], L["scr"]
            emit_merge(C, x, hT, L["wbr"], L["gbuf"], L["ytb"],
                       [Wd["w_branch_a"][l], Wd["w_branch_b"][l], Wd["w_branch_c"][l]], Wd["w_out"][l],
                       yt_load, gsc.ap(), scr, after_first_loads=lambda x=x: load_x(C, x, xs.ap()))
            gflat = L["gbuf"].rearrange("p a n -> p (a n)")
            aT2 = [gflat[:, 0:4096].rearrange("p (f n) -> p f n", f=2), gflat[:, 4096:8192].rearrange("p (f n) -> p f n", f=2)]
            yflat = L["ytb"].rearrange("p a n -> p (a n)")
            wdb = [yflat[:, 8192:10240], yflat[:, 10240:12288]]
            emit_ffn(C, x, hT, L["wball"], wdb, aT2, Wd["ffn2_w_gate"][l], Wd["ffn2_w_up"][l], Wd["ffn2_w_down"][l],
                     Wd["ffn2_norm"][l], scr, aT_keys=["gbuf0", "gbuf1"], wd_keys=["ytb1"])
            if l < DEPTH - 1:
                P.barrier()
        store_x(C, x, out_d)
        P.emit()
    return nc


_BF = ml_dtypes.bfloat16
_PROG = []


def head_cols(j):
    r = lambda a, n: list(range(a, a + n))
    return (r(j * 128, 128) + r(512 + j * 128, 128) + r(1536 + j * 128, 128) + r(2048 + j * 128, 128)
            + r(1024 + j * 128, 128) + r(2560 + j * 128, 128) + r(3584 + j * 128, 128) + r(4096 + j * 128, 128)
            + r(3072 + j * 64, 64) + r(3328 + j * 64, 64))


def kernel(**inp):
    x = np.asarray(inp["x"], np.float32)
    if not _PROG:
        _RANK.clear()
        _PROG.append(build_fused())
    nc = _PROG[0]
    ident = np.eye(128, dtype=_BF)
    W = {n: np.ascontiguousarray(np.asarray(inp[n], np.float32)) for n in WNAMES}
    lamv = np.ascontiguousarray(np.stack([inp["a_lambda_q1"], inp["a_lambda_k1"], inp["a_lambda_q2"], inp["a_lambda_k2"]], axis=1)).astype(np.float32)
    tabs = [rope_tables(i) for i in range(4)]
    cos_all = np.ascontiguousarray(np.stack([t[0] for t in tabs]))
    sin_all = np.ascontiguousarray(np.stack([t[1] for t in tabs]))
    maps = []
    for c in range(NCORE):
        b, j = c // 4, c % 4
        m = dict(W)
        m["x"] = np.ascontiguousarray(x[b, j * TOK:(j + 1) * TOK])
        m["w_head"] = np.ascontiguousarray(W["w_in"][:, :, head_cols(j)])
        m["cos_all"], m["sin_all"] = cos_all, sin_all
        m["ident"] = ident
        mc = [mix_consts(j, np.asarray(inp["b_rel_bias"][l], np.float32), l) for l in range(DEPTH)]
        for n in MIXC:
            m[n] = mc[0][n]
        m["biasT"] = np.stack([mc[l]["biasT"] for l in range(DEPTH)])
        m["laminit"] = np.stack([mc[l]["laminit"] for l in range(DEPTH)])
        m["lamv"] = lamv
        m["a_subln"] = np.ascontiguousarray(np.asarray(inp["a_subln"], np.float32))
        m["cnorm"] = np.ascontiguousarray(np.asarray(inp["c_out_norm"], np.float32)[:, j])
        maps.append(m)
    res = run_bass_kernel_spmd(nc, maps, core_ids=list(range(NCORE))).results
    out = np.empty_like(x)
    for c in range(NCORE):
        out[c // 4, (c % 4) * TOK:(c % 4 + 1) * TOK] = np.asarray(res[c]["x_out"])
    return out
```

```python
import contextlib
import math
import numpy as np
import ml_dtypes
import concourse.bass as bass
import concourse.mybir as mybir
from concourse.bass_utils import run_bass_kernel_spmd

F32 = mybir.dt.float32
BF16 = mybir.dt.bfloat16
ALU = mybir.AluOpType
AF = mybir.ActivationFunctionType
AX = mybir.AxisListType

COMPUTE = ("pe", "act", "dve", "pool")
QUEUES = ("sp", "pq", "cc")
NSEM_DMA = 8
EPOCH = 24000

DM = 1024
DFF = 2816
NCORE = 8
TOK = 2048
NT = 16
SEQ = 8192
NKT = 64
EPS = 1e-6
IN_COLS = 7680


class Op:
    __slots__ = ("eng", "fn", "deps", "signaled", "sigcount", "dma", "dma_i")

    def __init__(self, eng, fn, dma):
        self.eng = eng
        self.fn = fn
        self.deps = []
        self.signaled = False
        self.sigcount = 0
        self.dma = dma
        self.dma_i = -1


class Prog:
    def __init__(self, nc):
        self.nc = nc
        self.streams = {"pe": [], "act": [], "dve": [], "pool": [], "sp": []}
        self.last_writer = {}
        self.readers = {}
        self.dma_count = {"sp": 0, "pq": 0, "cc": 0}

    @staticmethod
    def stream_of(eng):
        return "pool" if eng in ("pq", "cc") else eng

    def op(self, eng, fn, reads=(), writes=()):
        dma = eng in QUEUES
        o = Op(eng, fn, dma)
        deps = {}
        for k in reads:
            w = self.last_writer.get(k)
            if w is not None:
                deps[id(w)] = w
        for k in writes:
            w = self.last_writer.get(k)
            if w is not None:
                deps[id(w)] = w
            for r in self.readers.get(k, ()):
                deps[id(r)] = r
        for k in writes:
            self.last_writer[k] = o
            self.readers[k] = []
        for k in reads:
            lst = self.readers.setdefault(k, [])
            if not dma:
                lst[:] = [r for r in lst if r.eng != eng]
            lst.append(o)
        for d in deps.values():
            if d is o:
                continue
            if d.eng == "pe" and eng == "pe":
                continue
            o.deps.append(d)
            d.signaled = True
        if dma:
            o.signaled = True
            o.dma_i = self.dma_count[eng]
            self.dma_count[eng] += 1
        self.streams[self.stream_of(eng)].append(o)
        return o

    def barrier(self, keep=()):
        last = {}
        for e in COMPUTE:
            for o in reversed(self.streams[e]):
                if isinstance(o, Op) and not o.dma:
                    o.signaled = True
                    last[e] = o
                    break
        b = ("barrier", last, dict(self.dma_count))
        for stream in self.streams:
            self.streams[stream].append(b)
        kept = {k: w for k, w in self.last_writer.items() if k in keep or w.eng == "cc"}
        self.last_writer.clear()
        self.readers.clear()
        self.last_writer.update(kept)

    def emit(self):
        nc = self.nc
        cnt = {e: 0 for e in COMPUTE}
        for ops in self.streams.values():
            for o in ops:
                if isinstance(o, Op) and not o.dma and o.signaled:
                    cnt[o.eng] += 1
                    o.sigcount = cnt[o.eng]
        nep = {e: max(1, (cnt[e] + EPOCH - 1) // EPOCH) for e in COMPUTE}
        with contextlib.ExitStack() as st:
            sems = {}
            for e in COMPUTE:
                sems[e] = [st.enter_context(nc.semaphore(f"s_{e}{i}")) for i in range(nep[e])]
            for q in ("sp", "pq"):
                sems[q] = [st.enter_context(nc.semaphore(f"s_{q}{i}")) for i in range(NSEM_DMA)]
            sems["cc"] = [st.enter_context(nc.semaphore("s_cc"))]
            block = st.enter_context(nc.Block())

            def sem_val(d):
                if d.eng == "cc":
                    return sems["cc"][0], d.dma_i + 1
                if d.dma:
                    return sems[d.eng][d.dma_i % NSEM_DMA], 16 * (d.dma_i // NSEM_DMA + 1)
                ep = (d.sigcount - 1) // EPOCH
                return sems[d.eng][ep], d.sigcount - ep * EPOCH

            def mk(stream):
                def body(engh):
                    waited = {}

                    def wait(s, v):
                        if waited.get(id(s), 0) >= v:
                            return
                        waited[id(s)] = v
                        engh.wait_ge(s, v)

                    for o in self.streams[stream]:
                        if not isinstance(o, Op):
                            _, last, counts = o
                            for d in last.values():
                                wait(*sem_val(d))
                            for q in ("sp", "pq"):
                                n = counts[q]
                                for i in range(min(n, NSEM_DMA)):
                                    wait(sems[q][i], 16 * ((n - 1 - i) // NSEM_DMA + 1))
                            continue
                        for d in o.deps:
                            wait(*sem_val(d))
                        if o.eng == "cc":
                            pass
                        elif o.dma and o.dma_i >= NSEM_DMA:
                            wait(sems[o.eng][o.dma_i % NSEM_DMA], 16 * (o.dma_i // NSEM_DMA))
                        ins = o.fn(engh)
                        if o.signaled:
                            s, _ = sem_val(o)
                            ins.then_inc(s, 16 if (o.dma and o.eng != "cc") else 1)
                    if stream == "sp":
                        if self.dma_count["cc"]:
                            wait(sems["cc"][0], self.dma_count["cc"])
                        for q in ("sp", "pq"):
                            n = self.dma_count[q]
                            for i in range(min(n, NSEM_DMA)):
                                tot = (n - 1 - i) // NSEM_DMA + 1
                                wait(sems[q][i], 16 * tot)
                return body

            block.tensor(mk("pe"))
            block.scalar(mk("act"))
            block.vector(mk("dve"))
            block.gpsimd(mk("pool"))
            block.sync(mk("sp"))


def mm(out, lhsT, rhs, start, stop):
    return lambda e: e.matmul(out, lhsT=lhsT, rhs=rhs, start=start, stop=stop, skip_group_check=True)


def tr(out, in_, ident):
    return lambda e: e.transpose(out, in_, ident)


def actf(out, in_, func, scale=1.0, bias=None, accum_out=None):
    kw = {}
    if bias is not None:
        kw["bias"] = bias
    if accum_out is not None:
        kw["accum_out"] = accum_out
    return lambda e: e.activation(out=out, in_=in_, func=func, scale=scale, **kw)


def tt(out, in0, in1, op):
    return lambda e: e.tensor_tensor(out=out, in0=in0, in1=in1, op=op)


def ts(out, in0, s1, op0, s2=None, op1=None):
    if op1 is None:
        return lambda e: e.tensor_scalar(out=out, in0=in0, scalar1=s1, scalar2=None, op0=op0)
    return lambda e: e.tensor_scalar(out=out, in0=in0, scalar1=s1, scalar2=s2, op0=op0, op1=op1)


def stt(out, in0, scalar, in1, op0, op1):
    return lambda e: e.scalar_tensor_tensor(out=out, in0=in0, scalar=scalar, in1=in1, op0=op0, op1=op1)


def dma(out, in_):
    return lambda e: e.dma_start(out=out, in_=in_)


def dma_nc(out, in_):
    return lambda e: e.dma_start(out=out, in_=in_, allow_slow_non_contiguous=True)


ARENA_KIB = 204


class Ctx:
    def __init__(self, nc, st):
        self.nc = nc
        self.st = st
        self.P = Prog(nc)
        self.psum = st.enter_context(nc.psum_tensor("psum", [128, 4096], F32))
        self.arena = st.enter_context(nc.sbuf_tensor("arena", [128, ARENA_KIB * 512], BF16))
        self.off = 0
        self.ident = self.sb("ident", [128, 128], BF16)
        self.epsc = self.sb("epsc", [128, 1], F32)
        self.P.op("pool", lambda e: e.memset(self.epsc, EPS), writes=["epsc"])

    def mark(self):
        return self.off

    def reset(self, m):
        self.off = m

    def sb(self, name, shape, dtype):
        n = 1
        for d in shape[1:]:
            n *= d
        nbytes = n * (4 if dtype == F32 else 2)
        nbytes = (nbytes + 63) // 64 * 64
        assert self.off + nbytes <= ARENA_KIB * 1024, (name, self.off, nbytes)
        v = self.arena[:, self.off // 2:(self.off + nbytes) // 2]
        self.off += nbytes
        if dtype == F32:
            v = v.bitcast(F32)
        v = v[:, 0:n]
        if len(shape) == 3:
            v = v.rearrange("p (a b) -> p a b", a=shape[1])
        elif len(shape) == 4:
            v = v.rearrange("p (a b c) -> p a b c", a=shape[1], b=shape[2])
        if shape[0] < 128:
            v = v[0:shape[0]]
        return v

    def bank(self, b, n=1):
        return self.psum[:, b * 512:(b + n) * 512]

    def bank_bf(self, b, n=1):
        return self.psum[:, b * 512:(b + n) * 512].bitcast(BF16)


def emit_rmsnorm_hT(C, x, hT, gain_d, tag, scr):
    P = C.P
    ss, lnv, rstd, gainT, junk, xn = scr["ss"], scr["lnv"], scr["rstd"], scr["gainT"], scr["junk"], scr["xn"]
    P.op("sp", dma_nc(gainT[:], gain_d.rearrange("(k p) -> p k", p=128)), writes=["gainT"])
    for t in range(NT):
        P.op("act", actf(junk, x[:, t, :], AF.Square, accum_out=ss[:, t:t + 1]),
             reads=[f"x{t}"], writes=[f"ss{t}", "sq_0"])
    allss = [f"ss{t}" for t in range(NT)]
    P.op("act", actf(lnv[:], ss[:], AF.Ln, scale=1.0 / DM, bias=C.epsc[:]), reads=allss + ["epsc"], writes=["lnv"])
    P.op("act", actf(rstd[:], lnv[:], AF.Exp, scale=-0.5), reads=["lnv"], writes=["rstd"])
    for t in range(NT):
        b = t % 2
        xk = scr["xnk"][b]
        P.op("act", actf(xn[b], x[:, t, :], AF.Copy, scale=rstd[:, t:t + 1]), reads=[f"x{t}", "rstd"], writes=[xk])
        pst = C.bank_bf(b)
        for k in range(8):
            P.op("pe", tr(pst[:, k * 128:(k + 1) * 128], xn[b][:, k * 128:(k + 1) * 128], C.ident[:]),
                 reads=[xk, "ident"], writes=[f"ps{b}"])
        P.op("dve", tt(hT[:, :, t * 128:(t + 1) * 128], pst.rearrange("p (k c) -> p k c", k=8),
                       gainT[:].unsqueeze(2).to_broadcast([128, 8, 128]), ALU.mult),
             reads=[f"ps{b}", "gainT"], writes=[f"hT{t}"])


FFN_CHUNKS = [(2 * i, 2) for i in range(11)]


def emit_ffn(C, x, hT, wball, wdbuf, aT2, wg_d, wu_d, wd_d, norm_d, scr, aT_keys=(), wd_keys=()):
    P = C.P
    emit_rmsnorm_hT(C, x, hT, norm_d, "f", scr)
    sg = scr["sg"]
    allhT = [f"hT{t}" for t in range(NT)]
    nch = len(FFN_CHUNKS)
    st = {"nmm": 0}

    def views(ci):
        f0, nf = FFN_CHUNKS[ci]
        cw = nf * 128
        j = ci % 3
        wb = wball[:, j * 4096:(j + 1) * 4096]
        wgv = wb[:, 0:8 * cw].rearrange("p (k c) -> p k c", k=8)
        wuv = wb[:, 2048:2048 + 8 * cw].rearrange("p (k c) -> p k c", k=8)
        wdv = wdbuf[ci % 2][:, 0:nf * 1024].rearrange("p (f c) -> p f c", f=nf)
        return f0, nf, cw, wgv, wuv, wdv, [f"wb{2 * j}", f"wb{2 * j + 1}"], f"wd{ci % 2}"

    def issue_wgu(ci):
        if ci >= nch:
            return
        f0, nf, cw, wgv, wuv, wdv, gk, dk = views(ci)
        P.op("pq", dma(wgv, wg_d[:, f0 * 128:f0 * 128 + cw].rearrange("(k p) c -> p k c", p=128)), writes=[gk[0]])
        P.op("pq", dma(wuv, wu_d[:, f0 * 128:f0 * 128 + cw].rearrange("(k p) c -> p k c", p=128)), writes=[gk[1]])

    def issue_wd(ci):
        if ci >= nch:
            return
        f0, nf, cw, wgv, wuv, wdv, gk, dk = views(ci)
        P.op("pq", dma(wdv, wd_d[f0 * 128:f0 * 128 + cw, :].rearrange("(f p) c -> p f c", p=128)),
             writes=[dk] + (list(wd_keys) if ci < 2 else []))

    def gateup(ci, tb, f):
        f0, nf, cw, wgv, wuv, wdv, gk, dk = views(ci)
        aT = aT2[ci % 2]
        pb = 2 * (st["nmm"] % 2)
        st["nmm"] += 1
        par = st["nmm"] % 2
        pg, pu = C.bank(pb), C.bank(pb + 1)
        for k in range(8):
            P.op("pe", mm(pg, wgv[:, k, f * 128:(f + 1) * 128], hT[:, k, tb * 512:(tb + 1) * 512], k == 0, k == 7),
                 reads=[gk[0]] + allhT[tb * 4:tb * 4 + 4], writes=[f"ps{pb}"])
        for k in range(8):
            P.op("pe", mm(pu, wuv[:, k, f * 128:(f + 1) * 128], hT[:, k, tb * 512:(tb + 1) * 512], k == 0, k == 7),
                 reads=[gk[1]] + allhT[tb * 4:tb * 4 + 4], writes=[f"ps{pb + 1}"])
        P.op("act", actf(sg[par][:], pg, AF.Silu), reads=[f"ps{pb}"], writes=[f"sg{par}"])
        P.op("dve", tt(aT[:, f, tb * 512:(tb + 1) * 512], sg[par][:], pu, ALU.mult),
             reads=[f"sg{par}", f"ps{pb + 1}"], writes=[f"aT{ci % 2}_{f}_{tb}"] + (list(aT_keys) if ci < 2 else []))

    def down(ci, t):
        f0, nf, cw, wgv, wuv, wdv, gk, dk = views(ci)
        aT = aT2[ci % 2]
        pb = 4 + 2 * (t % 2)
        py = C.bank(pb, 2)
        for n in range(2):
            for f in range(nf):
                P.op("pe", mm(py[:, n * 512:(n + 1) * 512], aT[:, f, t * 128:(t + 1) * 128], wdv[:, f, n * 512:(n + 1) * 512],
                              f == 0, f == nf - 1),
                     reads=[dk, f"aT{ci % 2}_{f}_{t // 4}"], writes=[f"ps{pb + n}"])
        P.op("dve", stt(x[:, t, :], py, 0.5, x[:, t, :], ALU.mult, ALU.add),
             reads=[f"ps{pb}", f"ps{pb + 1}", f"x{t}"], writes=[f"x{t}"])

    issue_wgu(0)
    issue_wd(0)
    issue_wgu(1)
    for tb in range(4):
        for f in range(FFN_CHUNKS[0][1]):
            gateup(0, tb, f)
    for ci in range(nch):
        issue_wgu(ci + 2)
        issue_wd(ci + 1)
        nxt = [(tb, f) for tb in range(4) for f in range(FFN_CHUNKS[ci + 1][1])] if ci + 1 < nch else []
        for t in range(NT):
            down(ci, t)
            if nxt and t % 2 == 1:
                gateup(ci + 1, *nxt[t // 2])


def make_scr(C):
    s = {}
    s["ss"] = C.sb("ss", [128, NT], F32)
    s["lnv"] = C.sb("lnv", [128, NT], F32)
    s["rstd"] = C.sb("rstd", [128, NT], F32)
    s["gainT"] = C.sb("gainT", [128, 8], F32)
    for n in ("sq", "xq", "xg", "t1", "t2"):
        s[n + "2"] = [C.sb(f"{n}{i}", [128, 512], F32) for i in range(2)]
        s[n] = s[n + "2"][0]
    s["junk"] = s["sq"].bitcast(BF16)
    s["xn"] = [s["xq"].bitcast(BF16), s["xg"].bitcast(BF16)]
    s["xnk"] = ["xq_0", "xg_0"]
    s["sg"] = [C.sb(f"sg{i}", [128, 512], BF16) for i in range(2)]
    s["res"] = [C.sb(f"res{i}", [128, 512], BF16) for i in range(2)]
    s["s8"] = C.sb("s8", [128, 16], F32)
    s["l8"] = C.sb("l8", [128, 16], F32)
    s["r8"] = C.sb("r8", [128, 16], F32)
    return s


def load_x(C, x, x_d):
    xv = x_d.rearrange("(t p) c -> p t c", p=128)
    for t4 in range(4):
        C.P.op("sp", dma(x[:, t4 * 4:(t4 + 1) * 4, :], xv[:, t4 * 4:(t4 + 1) * 4, :]),
               writes=[f"x{t}" for t in range(t4 * 4, t4 * 4 + 4)])


def store_x(C, x, x_d):
    xv = x_d.rearrange("(t p) c -> p t c", p=128)
    for t4 in range(4):
        C.P.op("sp", dma(xv[:, t4 * 4:(t4 + 1) * 4, :], x[:, t4 * 4:(t4 + 1) * 4, :]),
               reads=[f"x{t}" for t in range(t4 * 4, t4 * 4 + 4)])


PROJ_BLOCKS = [(0, "NR", 0), (1, "NR", 1), (2, "V", None), (3, "N", 2), (4, "N", 3), (5, "V", None),
               (6, "R", None), (7, "V", None), (8, "G", None)]
PROJ_OUT = {0: "oAq", 1: "oAk", 2: "oAv", 3: "oBq", 4: "oBk", 5: "oBv", 6: "oCqk", 7: "oCv", 8: "oCg"}


def emit_proj(C, x, hT, wbuf, stage, kstage, w_in_d, mixnorm_d, gains, cos2, sin2, outs, scr):
    P = C.P
    emit_rmsnorm_hT(C, x, hT, mixnorm_d, "m", scr)
    res = scr["res"]

    def v3(ap, h=8):
        return ap.rearrange("p (h d) -> p h d", h=h)

    blocks = [(blk, kind, gi) for (blk, kind, gi) in PROJ_BLOCKS] + [(9 + g, "GATE", None) for g in range(6)]

    def wview(bi):
        return wbuf[bi % 2][:, 0:4096].rearrange("p (k c) -> p k c", k=8)

    def wkeys(bi):
        return [f"wb{3 * (bi % 2)}", f"wb{3 * (bi % 2) + 1}"]

    def issue_w(bi):
        if bi >= len(blocks):
            return
        blk = blocks[bi][0]
        P.op("pq", dma(wview(bi), w_in_d[:, blk * 512:(blk + 1) * 512].rearrange("(k p) c -> p k c", p=128)),
             writes=wkeys(bi))

    def emit_mm(bi, t, pb):
        wv = wview(bi)
        for k in range(8):
            P.op("pe", mm(C.bank(pb), hT[:, k, t * 128:(t + 1) * 128], wv[:, k, :], k == 0, k == 7),
                 reads=wkeys(bi) + [f"hT{t}"], writes=[f"ps{pb}"])

    issue_w(0)
    inst = 0
    pending = []
    for bi, (blk, kind, gi) in enumerate(blocks):
        issue_w(bi + 1)
        wv = wview(bi)
        wk = f"w{bi % 2}"
        stg = stage[bi % 2]
        sk = f"stage{bi % 2}"
        if kind == "GATE" and blk == 10 and outs.early is not None:
            outs.early()
        if kind == "GATE":
            for f in range(4):
                for tb in range(4):
                    pb = inst % 2
                    inst += 1
                    ps = C.bank(pb)
                    for k in range(8):
                        P.op("pe", mm(ps, wv[:, k, f * 128:(f + 1) * 128], hT[:, k, tb * 512:(tb + 1) * 512], k == 0, k == 7),
                             reads=wkeys(bi) + [f"hT{t}" for t in range(tb * 4, tb * 4 + 4)], writes=[f"ps{pb}"])
                    P.op("act", actf(stg[:, f, tb * 512:(tb + 1) * 512], ps, AF.Sigmoid), reads=[f"ps{pb}"], writes=[sk])
            P.op("sp", dma(outs.gate(blk - 9), stg[:]), reads=[sk])
            for fn in pending:
                fn()
            pending = []
            continue
        for t in range(NT):
            pb = inst % 2
            par = inst % 2
            inst += 1
            sq, xq, xg, t1, t2 = (scr[n][par] for n in ("sq2", "xq2", "xg2", "t12", "t22"))
            s8, l8, r8 = (scr[n][:, par * 8:(par + 1) * 8] for n in ("s8", "l8", "r8"))
            kq = lambda n: f"{n}_{par}"
            ps = C.bank(pb)
            if t == 0:
                emit_mm(bi, 0, pb)
            if t + 1 < NT:
                emit_mm(bi, t + 1, 1 - pb)
            dst = stg[:, :, t * 128:(t + 1) * 128]
            if kind == "V":
                P.op("act", actf(dst, v3(ps, 4), AF.Copy), reads=[f"ps{pb}"], writes=[sk])
                continue
            if kind == "G":
                P.op("act", actf(dst, v3(ps, 4), AF.Silu), reads=[f"ps{pb}"], writes=[sk])
                continue
            rs = res[par]
            rk = f"res{par}"
            if kind in ("NR", "N"):
                P.op("act", actf(sq, ps, AF.Square), reads=[f"ps{pb}"], writes=[kq("sq")])
                P.op("dve", lambda e, o=s8, i=v3(sq): e.tensor_reduce(out=o, in_=i, axis=AX.X, op=ALU.add),
                     reads=[kq("sq")], writes=[kq("s8")])
                P.op("act", actf(l8, s8, AF.Ln, scale=1.0 / 64, bias=C.epsc[:]), reads=[kq("s8"), "epsc"], writes=[kq("l8")])
                P.op("act", actf(r8, l8, AF.Exp, scale=-0.5), reads=[kq("l8")], writes=[kq("r8")])
                P.op("dve", tt(v3(xq), v3(ps), r8.unsqueeze(2).to_broadcast([128, 8, 64]), ALU.mult),
                     reads=[f"ps{pb}", kq("r8")], writes=[kq("xq")])
                gb = gains[:, gi, :].unsqueeze(1).to_broadcast([128, 8, 64])
                if kind == "N":
                    P.op("dve", tt(v3(rs[:]), v3(xq), gb, ALU.mult), reads=[kq("xq"), "gains"], writes=[rk])
                else:
                    P.op("dve", tt(v3(xg), v3(xq), gb, ALU.mult), reads=[kq("xq"), "gains"], writes=[kq("xg")])
                src, srck, eng2 = v3(xg), kq("xg"), "pool"
            else:
                src, srck, eng2 = v3(ps), f"ps{pb}", "dve"
            if kind in ("NR", "R"):
                cb = cos2[:, t, :].unsqueeze(1).to_broadcast([128, 8, 64])
                sa = sin2[:, t, 0:32].unsqueeze(1).to_broadcast([128, 8, 32])
                sb_ = sin2[:, t, 32:64].unsqueeze(1).to_broadcast([128, 8, 32])
                P.op("dve", tt(v3(t1), src, cb, ALU.mult), reads=[srck, "cs"], writes=[kq("t1")])
                P.op(eng2, tt(v3(t2)[:, :, 0:32], src[:, :, 32:64], sa, ALU.mult), reads=[srck, "cs"], writes=[kq("t2a")])
                P.op(eng2, tt(v3(t2)[:, :, 32:64], src[:, :, 0:32], sb_, ALU.mult), reads=[srck, "cs"], writes=[kq("t2b")])
                P.op("dve", tt(rs[:], t1, t2, ALU.add), reads=[kq("t1"), kq("t2a"), kq("t2b")], writes=[rk])
            tb = 2 + par
            pst = C.bank_bf(tb)[:, 0:512]
            for c in range(4):
                P.op("pe", tr(pst[:, c * 128:(c + 1) * 128], rs[:, c * 128:(c + 1) * 128], C.ident[:]),
                     reads=[rk, "ident"], writes=[f"ps{tb}"])
            P.op("act", actf(dst, v3(pst, 4), AF.Copy), reads=[f"ps{tb}"], writes=[sk])
            if kind == "R":
                P.op("pool", lambda e, o=kstage[:, :, t * 64:(t + 1) * 64], i=v3(rs[:, 256:512], 4): e.tensor_copy(out=o, in_=i),
                     reads=[rk], writes=["kstage"])
        for fn in pending:
            fn()
        pending = outs.store_block(P, blk, kind, stg, sk, kstage)
    for fn in pending:
        fn()


def rope_tables(core):
    pos0 = (core % 4) * TOK
    pos = (pos0 + np.arange(TOK)).astype(np.float32)
    half = 32
    inv = (10000.0 ** (-np.arange(half, dtype=np.float32) / half)).astype(np.float32)
    ang = pos[:, None] * inv[None, :]
    cos, sin = np.cos(ang).astype(np.float32), np.sin(ang).astype(np.float32)
    cos2 = np.concatenate([cos, cos], -1).reshape(NT, 128, 64).transpose(1, 0, 2)
    sin2 = np.concatenate([-sin, sin], -1).reshape(NT, 128, 64).transpose(1, 0, 2)
    return np.ascontiguousarray(cos2), np.ascontiguousarray(sin2)


def emit_tr_out(C, ybf, ykey, dst, dkey):
    P = C.P
    pst = C.bank_bf(7)[:, 0:128]
    P.op("pe", tr(pst, ybf, C.ident[:]), reads=[ykey, "ident"], writes=["ps7"])
    P.op("act", actf(dst, pst, AF.Copy), reads=["ps7"], writes=[dkey])


def emit_mix_bc(C, d, T):
    P = C.P
    bQ, bK, bV = T["bQ"], T["bK"], T["bV"]

    def Sb(hh):
        return C.bank(2 * hh, 2)[:, 0:640].rearrange("p (o q) -> p o q", o=5)

    def hv(t_, hh):
        return t_[:, hh * 640:(hh + 1) * 640].rearrange("p (o q) -> p o q", o=5)

    EB = T["EB"]
    Ob = C.bank(4)[:, 0:130]
    Ob3 = Ob.rearrange("p (h e) -> p h e", h=2)
    Sc = C.bank(5)[:, 0:128]
    kv = C.bank(5)[0:64, 256:384]
    Oc = C.bank(6)[:, 0:128]
    E, PTb = T["E"], T["PTb"]
    state, state_bf = T["state"], T["state_bf"]
    P.op("pool", lambda e: e.memset(state[:], 0.0), writes=["state"])
    P.op("pool", lambda e: e.memset(state_bf[1][:], 0.0), writes=["state_bf1"])

    def grp(n):
        g, tl = n // 16, n % 16
        gb = g % 2
        return g, tl, gb, f"cg{gb}", {k: T[k][gb] for k in ("cQ", "cK", "cKt", "cV", "cG", "qxi", "kz")}

    def load_group(g):
        gb = g % 2
        gk = f"cg{gb}"
        B = {k: T[k][gb] for k in ("cQ", "cK", "cKt", "cV", "cG", "qxi", "kz")}
        d.load(P, B["cQ"][:], "oCqk", g, [gk + "q"], rows=(0, 64))
        d.load(P, B["cK"][:], "oCqk", g, [gk + "k"], rows=(64, 64))
        d.load(P, B["cKt"][:], "oCkt", g, [gk + "kt"], t=16)
        d.load(P, B["cV"][:], "oCv", g, [gk + "v"], t=16)
        d.load(P, B["cG"][:], "oCg", g, [gk + "g"], t=16)

    def derive_group(g):
        gb = g % 2
        gk = f"cg{gb}"
        B = {k: T[k][gb] for k in ("cQ", "cKt", "qxi", "kz")}
        P.op("dve", tt(B["qxi"][:].rearrange("p (t q) -> p t q", t=16), B["cQ"][:].rearrange("p (t q) -> p t q", t=16),
                       T["xi"][:].unsqueeze(1).to_broadcast([64, 16, 128]), ALU.mult),
             reads=[gk + "q", "cconst"], writes=[gk + "qxi"])
        P.op("dve", ts(B["kz"][:], B["cKt"][:], T["zeta"][:, 0:1], ALU.mult), reads=[gk + "kt", "cconst"], writes=[gk + "kz"])

    def s1(n):
        g, tl, gb, gk, B = grp(n)
        if tl == 2 and g + 1 < 4:
            load_group(g + 1)
        if tl == 11 and g + 1 < 4:
            derive_group(g + 1)
        o_lo = max(0, 4 - n)
        for hh in range(2):
            for o in range(o_lo, 5):
                kt = n - 4 + o
                P.op("pe", mm(Sb(hh)[:, o, :], bK[hh * 64:(hh + 1) * 64, kt * 128:(kt + 1) * 128],
                              bQ[hh * 64:(hh + 1) * 64, n * 128:(n + 1) * 128], True, True),
                     reads=["bQ", "bK"], writes=[f"ps{2 * hh}", f"ps{2 * hh + 1}"])
        cols = slice(tl * 128, (tl + 1) * 128)
        P.op("pe", mm(Sc, B["cK"][:, cols], B["cQ"][:, cols], True, True), reads=[gk + "q", gk + "k"], writes=["ps5"])
        P.op("pe", mm(kv, B["kz"][:, tl, :], B["cV"][:, tl, :], True, True), reads=[gk + "kz", gk + "v"], writes=["ps5"])

    def s2(n):
        o_lo = max(0, 4 - n)
        for hh in range(2):
            P.op("act", actf(hv(E, hh)[:, o_lo:5, :], Sb(hh)[:, o_lo:5, :], AF.Exp, scale=0.125),
                 reads=[f"ps{2 * hh}", f"ps{2 * hh + 1}"], writes=[f"E{hh}"])
            P.op("dve", tt(hv(PTb, hh)[:, o_lo:5, :], hv(E, hh)[:, o_lo:5, :], hv(EB, hh)[:, o_lo:5, :], ALU.mult),
                 reads=[f"E{hh}", "EB"], writes=[f"PTb{hh}"])
        PTc = T["PTc"][n % 2]
        P.op("dve", tt(PTc[:], Sc, T["DT"][:], ALU.mult), reads=["ps5", "cconst"], writes=[f"PTc{n % 2}"])
        P.op("dve", stt(state[:], state[:], T["cd"][0:64, 0:1], kv, ALU.mult, ALU.add),
             reads=["state", "ps5", "cconst"], writes=["state"])
        P.op("act", actf(state_bf[n % 2][:], state[:], AF.Copy), reads=["state"], writes=[f"state_bf{n % 2}"])

    def s3(n):
        g, tl, gb, gk, B = grp(n)
        o_lo = max(0, 4 - n)
        for hh in range(2):
            for o in range(o_lo, 5):
                kt = n - 4 + o
                P.op("pe", mm(Ob[:, hh * 65:(hh + 1) * 65], hv(PTb, hh)[:, o, :], bV[:, kt, hh, :], o == o_lo, o == 4),
                     reads=[f"PTb{hh}", "bV"], writes=["ps4"])
        cols = slice(tl * 128, (tl + 1) * 128)
        PTc = T["PTc"][n % 2]
        P.op("pe", mm(Oc, PTc[:], B["cV"][:, tl, :], True, False), reads=[f"PTc{n % 2}", gk + "v"], writes=["ps6"])
        P.op("pe", mm(Oc, B["qxi"][:, cols], state_bf[(n - 1) % 2][:], False, True),
             reads=[gk + "qxi", f"state_bf{(n - 1) % 2}"], writes=["ps6"])

    def s4(n):
        g, tl, gb, gk, B = grp(n)
        rcb = T["rcb"]
        P.op("dve", lambda e, o_=rcb[:], i_=Ob3[:, :, 64:65]: e.reciprocal(out=o_, in_=i_), reads=["ps4"], writes=["rcb"])
        yb = T["yb"][n % 2]
        P.op("dve", tt(yb[:].rearrange("p (h e) -> p h e", h=2), Ob3[:, :, 0:64], rcb[:].to_broadcast([128, 2, 64]), ALU.mult),
             reads=["ps4", "rcb"], writes=[f"yb{n % 2}"])
        ssc, lsc, rsc, junkc, tmpc = T["ssc"], T["lsc"], T["rsc"], T["junkc"], T["tmpc"]
        P.op("act", actf(junkc[:], Oc, AF.Square, accum_out=ssc[:]), reads=["ps6"], writes=["ssc", "junkc"])
        P.op("act", actf(lsc[:], ssc[:], AF.Ln, scale=1.0 / 128, bias=C.epsc[:]), reads=["ssc", "epsc"], writes=["lsc"])
        P.op("act", actf(rsc[:], lsc[:], AF.Exp, scale=-0.5), reads=["lsc"], writes=["rsc"])
        P.op("dve", stt(tmpc[:], Oc, rsc[:, 0:1], T["cnorm"][:], ALU.mult, ALU.mult),
             reads=["ps6", "rsc", "cconst"], writes=["tmpc"])
        yc = T["yc"][n % 2]
        P.op("pool", tt(yc[:], tmpc[:], B["cG"][:, tl, :], ALU.mult), reads=["tmpc", gk + "g"], writes=[f"yc{n % 2}"])

    def s5(n):
        ysb = T["ystBC"][(n // 4) % 2]
        ysk = f"ystBC{(n // 4) % 2}"
        emit_tr_out(C, T["yb"][n % 2][:], f"yb{n % 2}", ysb[:, 0, (n % 4) * 128:(n % 4 + 1) * 128], ysk)
        emit_tr_out(C, T["yc"][n % 2][:], f"yc{n % 2}", ysb[:, 1, (n % 4) * 128:(n % 4 + 1) * 128], ysk)
        if n % 4 == 3:
            d.store_bc(P, n // 4, ysb[:], ysk)

    load_group(0)
    derive_group(0)
    s1(0)
    s2(0)
    for n in range(NKT):
        if n + 1 < NKT:
            s1(n + 1)
        s3(n)
        if n + 1 < NKT:
            s2(n + 1)
        s4(n)
        if n >= 1:
            s5(n - 1)
    s5(NKT - 1)


def emit_mix_a(C, d, T, hooks=None):
    P = C.P
    aQ, aK, aV = T["aQ"], T["aK"], T["aV"]
    nlam = T["nlam"]
    steps = [(qb, kt) for qb in range(16) for kt in range(4 * qb + 4)]

    def geom(i):
        qb, kt = steps[i]
        dd = kt - 4 * qb
        q0 = max(dd, 0) * 128
        sbi = i % 2
        S = C.bank(2 * sbi, 2).rearrange("p (c n) -> p c n", c=2)
        return qb, kt, dd, q0, S, [f"ps{2 * sbi}", f"ps{2 * sbi + 1}"]

    def emit_s(i):
        qb, kt, dd, q0, S, skeys = geom(i)
        for c in range(2):
            P.op("pe", mm(S[:, c, q0:512], aK[c * 64:(c + 1) * 64, kt * 128:(kt + 1) * 128],
                          aQ[c * 64:(c + 1) * 64, qb * 512 + q0:(qb + 1) * 512], True, True),
                 reads=["aQ", "aK"], writes=[skeys[c]])

    emit_s(0)
    for i in range(len(steps)):
        qb, kt, dd, q0, S, skeys = geom(i)
        PT = T["PT"][i % 3]
        pk = f"pt{i % 3}"
        P.op("act", actf(PT[:, :, q0:512], S[:, :, q0:512], AF.Exp, scale=0.125), reads=skeys, writes=[pk])
        if dd >= 0:
            P.op("pool", lambda e, o_=PT[64:128, :, q0:q0 + 64]: e.memset(o_, 0.0), reads=[], writes=[pk])
        if i + 1 < len(steps):
            emit_s(i + 1)
        for c in range(2):
            for qt in range(max(dd, 0), 4):
                r = qt * 2 + c
                bk = 4 + r // 3
                off = (r % 3) * 129
                P.op("pe", mm(C.bank(bk)[:, off:off + 129], PT[:, c, qt * 128:(qt + 1) * 128], aV[:, kt, :],
                              kt == 0 and r in (0, 4, 6), kt == 4 * qb + qt),
                     reads=[pk, "aV"], writes=[f"ps{bk}"])
        if kt != 4 * qb + 3:
            continue
        yst = T["ystA"][qb % 2]
        ysk = f"ystA{qb % 2}"
        for qt in range(4):
            r0, r1 = qt * 2, qt * 2 + 1
            O0 = C.bank(4 + r0 // 3)[:, (r0 % 3) * 129:(r0 % 3) * 129 + 129]
            O1 = C.bank(4 + r1 // 3)[:, (r1 % 3) * 129:(r1 % 3) * 129 + 129]
            k0, k1 = f"ps{4 + r0 // 3}", f"ps{4 + r1 // 3}"
            rc, oa, ob = T["rc"], T["oa"], T["ob"]
            P.op("dve", lambda e, o_=rc[:, 0:1], i_=O0[:, 128:129]: e.reciprocal(out=o_, in_=i_), reads=[k0], writes=["rc0"])
            P.op("dve", lambda e, o_=rc[:, 1:2], i_=O1[:, 128:129]: e.reciprocal(out=o_, in_=i_), reads=[k1], writes=["rc1"])
            P.op("dve", tt(rc[:, 2:3], rc[:, 1:2], nlam[:], ALU.mult), reads=["rc1", "lam"], writes=["rc2"])
            P.op("dve", ts(oa[:], O0[:, 0:128], rc[:, 0:1], ALU.mult), reads=[k0, "rc0"], writes=["oa"])
            P.op("dve", stt(ob[:], O1[:, 0:128], rc[:, 2:3], oa[:], ALU.mult, ALU.add), reads=[k1, "rc2", "oa"], writes=["ob"])
            ssa, lsa, rsa, junka = T["ssa"], T["lsa"], T["rsa"], T["junka"]
            P.op("act", actf(junka[:], ob[:], AF.Square, accum_out=ssa[:]), reads=["ob"], writes=["ssa", "junka"])
            P.op("act", actf(lsa[:], ssa[:], AF.Ln, scale=1.0 / 128, bias=C.epsc[:]), reads=["ssa", "epsc"], writes=["lsa"])
            P.op("act", actf(rsa[:], lsa[:], AF.Exp, scale=-0.5), reads=["lsa"], writes=["rsa"])
            ya = T["ya"][qt % 2]
            P.op("dve", stt(ya[:], ob[:], rsa[:, 0:1], T["sublnS"][:], ALU.mult, ALU.mult),
                 reads=["ob", "rsa", "sublnS"], writes=[f"ya{qt % 2}"])
            emit_tr_out(C, ya[:], f"ya{qt % 2}", yst[:, qt * 128:(qt + 1) * 128], ysk)
        d.store_a(P, qb, yst[:], ysk)
        if hooks and qb in hooks:
            hooks[qb]()


def mix_consts(j, rel_bias, layer):
    gamma = np.float32(1.0 - 2.0 ** (-5.0 - j))
    lg = np.log(gamma).astype(np.float32)
    pos = np.arange(128, dtype=np.float32)
    diff = pos[None, :] - pos[:, None]
    DT = np.where(diff >= 0, np.exp(lg * np.maximum(diff, 0.0)), 0.0).astype(np.float32) * np.float32(0.125)
    zeta = (np.exp(lg * (127.0 - pos)) * 0.125).astype(np.float32).reshape(128, 1)
    xi = np.broadcast_to(np.exp(lg * (pos + 1.0)).astype(np.float32)[None, :], (64, 128))
    cd = np.full((128, 1), np.exp(lg * 128.0), np.float32)
    k = np.arange(128)[:, None]
    q = np.arange(128)[None, :]
    biasT = np.zeros((128, 10, 128), np.float32)
    maskB = np.zeros((128, 5, 128), np.float32)
    for o in range(5):
        rel = q - k + 128 * (4 - o)
        idx = np.clip(rel, -63, 256) + 63
        for hh in range(2):
            biasT[:, hh * 5 + o, :] = rel_bias[2 * j + hh][idx]
        dc = (q >= 64).astype(np.int64) - (k >= 64).astype(np.int64) + 2 * (4 - o)
        maskB[:, o, :] = ((dc >= 0) & (dc <= 8)).astype(np.float32)
    lam_init = 0.8 - 0.6 * math.exp(-0.3 * layer)
    li = np.zeros((128, 2), np.float32)
    li[:, 0] = lam_init
    li[:, 1] = 1.0 - lam_init
    return dict(DT=DT, zeta=zeta, xi=np.ascontiguousarray(xi), cd=cd, biasT=biasT, maskB=maskB, laminit=li)


def emit_merge(C, x, mT, wbr, gb_all, yt_all, wbr_d, wout_d, yt_load, g_d, scr, after_first_loads=None):
    P = C.P
    allw = [f"wb{i}" for i in range(6)]
    for m in range(3):
        P.op("pq", dma(wbr[:, m, :, :], wbr_d[m].rearrange("(j p) c -> p j c", p=128)), writes=allw)
    gb2 = [gb_all[:, 0:12, :], gb_all[:, 12:24, :]]
    yt2 = [yt_all[:, 0:12, :], yt_all[:, 12:24, :]]
    seq = [(tb, h) for tb in range(4) for h in range(2)]

    def issue_loads(idx):
        tb, h = seq[idx]
        cols = slice(tb * 512, (tb + 1) * 512)
        if h == 0:
            yt_load(P, yt2[tb % 2], tb, f"ytb{tb % 2}")
        for m in range(3):
            f0 = m * 8 + h * 4
            P.op("sp", dma(gb2[idx % 2][:, m * 4:(m + 1) * 4, :], g_d[f0:f0 + 4, :, cols].rearrange("f p n -> p f n")),
                 writes=[f"gbuf{idx % 2}"])

    issue_loads(0)
    if after_first_loads is not None:
        after_first_loads()
    it = 0
    for idx, (tb, h) in enumerate(seq):
        if idx + 1 < len(seq):
            issue_loads(idx + 1)
        cols = slice(tb * 512, (tb + 1) * 512)
        ytb, gbuf = yt2[tb % 2], gb2[idx % 2]
        yk, gk = f"ytb{tb % 2}", f"gbuf{idx % 2}"
        for fo4 in range(4):
            fo = h * 4 + fo4
            par = it % 2
            b0 = 3 * par
            it += 1
            t1, t2, sq = scr["t12"][par], scr["t22"][par], scr["sq2"][par]
            k1, k2, k3 = f"t1_{par}", f"t2a_{par}", f"sq_{par}"
            for m in range(3):
                for j in range(4):
                    P.op("pe", mm(C.bank(b0 + m), wbr[:, m, j, fo * 128:(fo + 1) * 128], ytb[:, j * 3 + m, :], j == 0, j == 3),
                         reads=allw + [yk], writes=[f"ps{b0 + m}"])
            P.op("dve", tt(t1, C.bank(b0), gbuf[:, fo4, :], ALU.mult), reads=[f"ps{b0}", gk], writes=[k1])
            P.op("dve", tt(t2, C.bank(b0 + 1), gbuf[:, 4 + fo4, :], ALU.mult), reads=[f"ps{b0 + 1}", gk], writes=[k2])
            P.op("dve", tt(sq, C.bank(b0 + 2), gbuf[:, 8 + fo4, :], ALU.mult), reads=[f"ps{b0 + 2}", gk], writes=[k3])
            P.op("pool", tt(t1, t1, t2, ALU.add), reads=[k1, k2], writes=[k1])
            P.op("pool", tt(mT[:, fo, cols], t1, sq, ALU.add), reads=[k1, k3],
                 writes=[f"hT{t}" for t in range(tb * 4, tb * 4 + 4)])
    wo = yt_all.rearrange("p f n -> p (f n)")[:, 0:8192].rearrange("p (k c) -> p k c", k=8)
    P.op("pq", dma(wo, wout_d.rearrange("(k p) c -> p k c", p=128)), writes=["ytb0", "ytb1"])
    for t in range(NT):
        b0 = 6 * (t % 2)
        py = C.bank(b0, 2)
        for n in range(2):
            for k in range(8):
                P.op("pe", mm(py[:, n * 512:(n + 1) * 512], mT[:, k, t * 128:(t + 1) * 128], wo[:, k, n * 512:(n + 1) * 512], k == 0, k == 7),
                     reads=["ytb0", "ytb1", f"hT{t}"], writes=[f"ps{b0 + n}"])
        P.op("dve", tt(x[:, t, :], x[:, t, :], py, ALU.add), reads=[f"ps{b0}", f"ps{b0 + 1}", f"x{t}"], writes=[f"x{t}"])


DEPTH = 2
GROUPS = [[0, 1, 2, 3], [4, 5, 6, 7]]
PW = 512
SEG = {"oAq": ("A", 0), "oAk": ("A", 2048), "oAv": ("A", 4096),
       "oBq": ("B", 0), "oBk": ("B", 2048), "oBv": ("B", 4096),
       "oCqk": ("C", 0), "oCv": ("C", 2048), "oCg": ("C", 4096), "oCkt": ("C", 6144)}
SEGW = {k: 2048 for k in SEG}
SEGW["oCkt"] = 1024
FAMW = {"A": 6144, "B": 6144, "C": 7168}
NPIECE = {f: w // PW for f, w in FAMW.items()}
_RANK = {}


def _rank(e):
    k = id(e)
    if k not in _RANK:
        _RANK[k] = e.snap(e.partition_id() % 4, min_val=0, max_val=3)
    return _RANK[k]


def _gather(P, snd_ap, rcv_ap, reads, wkey):
    P.op("cc", lambda e: e.collective_compute("AllGather", ALU.bypass, replica_groups=GROUPS,
                                              ins=[snd_ap.opt()], outs=[rcv_ap.opt()]), reads=reads, writes=[wkey])


class PreIO:
    def __init__(self, snd, rcv, gsc):
        self.snd, self.rcv = snd, rcv
        self.g = gsc.ap()
        self.deferred = []
        self.early = None

    def gate(self, g):
        return self.g[g * 4:(g + 1) * 4].rearrange("c p n -> p c n")

    def piece(self, f, k):
        return self.snd[f].ap()[k].rearrange("(j p) w -> p j w", p=128)

    def store_block(self, P, blk, kind, stg, sk, kstage):
        f, off = SEG[PROJ_OUT[blk]]
        k0 = off // PW
        later = []
        for k in range(4):
            cs = slice(k * PW, (k + 1) * PW)
            dv = self.piece(f, k0 + k)
            keys = []
            if kind == "R":
                for j in range(4):
                    h2 = slice((j % 2) * 64, (j % 2 + 1) * 64)
                    keys += [f"s{f}{k0 + k}q{j}", f"s{f}{k0 + k}k{j}"]
                    P.op("sp", dma(dv[0:64, j, :], stg[h2, j // 2, cs]), reads=[sk], writes=[keys[-2]])
                    P.op("sp", dma(dv[64:128, j, :], stg[h2, 2 + j // 2, cs]), reads=[sk], writes=[keys[-1]])
            else:
                keys = [f"s{f}{k0 + k}"]
                P.op("sp", dma(dv, stg[:, :, cs]), reads=[sk], writes=keys)
            later.append(lambda extra=(), kk=k0 + k, keys=keys: _gather(P, self.snd[f].ap()[kk], self.rcv[f].ap()[kk], list(keys) + list(extra), f"rcv{f}{kk}"))
        if kind == "R":
            f2, off2 = SEG["oCkt"]
            for k in range(2):
                kk = off2 // PW + k
                P.op("sp", dma(self.piece(f2, kk), kstage[:, :, k * PW:(k + 1) * PW]), reads=["kstage"], writes=[f"s{f2}{kk}"])
                later.append(lambda extra=(), kk=kk: _gather(P, self.snd[f2].ap()[kk], self.rcv[f2].ap()[kk], [f"s{f2}{kk}"] + list(extra), f"rcv{f2}{kk}"))
        if f == "C":
            self.deferred += later
            return []
        return later


class MixIO:
    def __init__(self, rcv, mine, snd2, rcv2):
        self.r = {f: t.ap() for f, t in rcv.items()}
        self.m = {f: t.ap() for f, t in mine.items()}
        self.snd2, self.rcv2 = snd2, rcv2
        self.pend2 = []

    def flush2(self):
        for fn in self.pend2:
            fn()
        self.pend2 = []

    def fetch(self, P, f):
        def fn(e):
            src = self.r[f].rearrange("k (i q) w -> q k i w", q=512)[bass.ds(_rank(e) * 128, 128), :, :, :]
            return e.dma_start(out=self.m[f], in_=src)
        P.op("sp", fn, reads=[f"rcv{f}{k}" for k in range(NPIECE[f])], writes=["mine" + f])

    def key(self, name):
        return "mine" + SEG[name][0]

    def load(self, P, dst, name, i, writes, rows=(0, 128), t=None, he=None):
        (f, off), w = SEG[name], SEGW[name]
        for kk in range(w // PW):
            src = self.m[f][rows[0]:rows[0] + rows[1], off // PW + kk, i, :]
            if he is not None:
                h, e_, hh = he
                a = PW // (h * e_)
                src = src.rearrange("p (a h e) -> p a h e", h=h, e=e_)[:, :, hh, :]
                dv = dst[:, kk * a:(kk + 1) * a, :]
            elif t is not None:
                dd = w // t
                a = PW // dd
                src = src.rearrange("p (a d) -> p a d", d=dd)
                dv = dst[:, kk * a:(kk + 1) * a, :]
            else:
                dv = dst[:, kk * PW:(kk + 1) * PW]
            P.op("sp", dma(dv, src), reads=["mine" + f], writes=writes)

    def store_a(self, P, qb, yst, ysk):
        k = qb // 4
        P.op("sp", dma(self.snd2["A"].ap()[k][:, (qb % 4) * 512:(qb % 4 + 1) * 512], yst), reads=[ysk], writes=[f"s2A{k}_{qb % 4}"])
        self.flush2()
        if qb % 4 == 3:
            self.pend2.append(lambda k=k: _gather(P, self.snd2["A"].ap()[k], self.rcv2["A"].ap()[k], [f"s2A{k}_{i}" for i in range(4)], f"rcv2A{k}"))

    def store_bc(self, P, q4, ysb, ysk):
        k = q4 // 2
        dv = self.snd2["BC"].ap()[k].rearrange("(m p) n -> p m n", p=128)
        P.op("sp", dma(dv[:, :, (q4 % 2) * 512:(q4 % 2 + 1) * 512], ysb), reads=[ysk], writes=[f"s2BC{k}_{q4 % 2}"])
        self.flush2()
        if q4 % 2 == 1:
            self.pend2.append(lambda k=k: _gather(P, self.snd2["BC"].ap()[k], self.rcv2["BC"].ap()[k], [f"s2BC{k}_{i}" for i in range(2)], f"rcv2BC{k}"))


MIXC = {"DT": ([128, 128], F32), "zeta": ([128, 1], F32), "xi": ([64, 128], F32), "cd": ([128, 1], F32),
        "maskB": ([128, 5, 128], F32)}
MIXL = {"biasT": ([128, 10, 128], F32), "laminit": ([128, 2], F32), "lamv": ([4, 64], F32),
        "a_subln": ([128], F32), "cnorm": ([128], F32)}
WNAMES = {"ffn1_norm": [DM], "ffn1_w_gate": [DM, DFF], "ffn1_w_up": [DM, DFF], "ffn1_w_down": [DFF, DM],
          "mix_norm": [DM], "w_in": [DM, IN_COLS], "a_q_norm": [64], "a_k_norm": [64], "b_q_norm": [64], "b_k_norm": [64],
          "w_branch_a": [512, DM], "w_branch_b": [512, DM], "w_branch_c": [512, DM], "w_out": [DM, DM],
          "ffn2_norm": [DM], "ffn2_w_gate": [DM, DFF], "ffn2_w_up": [DM, DFF], "ffn2_w_down": [DFF, DM]}


def emit_mix_setup(C, d, T, cst, l):
    P = C.P
    for n, sh in (("cQ", [64, TOK]), ("cK", [64, TOK]), ("cKt", [128, 16, 64]), ("cV", [128, 16, 128]),
                  ("cG", [128, 16, 128]), ("qxi", [64, TOK]), ("kz", [128, 16, 64])):
        T[n] = [C.sb(f"{n}{i}", sh, BF16) for i in range(2)]
    T["DT"] = C.sb("DT", [128, 128], F32)
    T["zeta"] = C.sb("zeta", [128, 1], F32)
    T["xi"] = C.sb("xi", [64, 128], F32)
    T["cd"] = C.sb("cd", [128, 1], F32)
    T["cnorm"] = C.sb("cnorm", [128, 128], F32)
    for n in ("DT", "zeta", "xi", "cd"):
        P.op("sp", dma(T[n], cst[n]), writes=["cconst"])
    P.op("sp", dma_nc(T["cnorm"], cst["cnorm"][l].partition_broadcast(128)), writes=["cconst"])
    bT = C.sb("biasT", [128, 10, 128], F32)
    mB = C.sb("maskB", [128, 5, 128], F32)
    T["EB"] = C.sb("EB", [128, 1280], BF16)
    P.op("sp", dma(bT, cst["biasT"][l]), writes=["bT"])
    P.op("sp", dma(mB, cst["maskB"]), writes=["mB"])
    P.op("act", actf(bT, bT, AF.Exp), reads=["bT"], writes=["bT"])
    P.op("dve", tt(T["EB"].rearrange("p (h o q) -> p h o q", h=2, o=5), bT.rearrange("p (h o) q -> p h o q", h=2),
                   mB.unsqueeze(1).to_broadcast([128, 2, 5, 128]), ALU.mult), reads=["bT", "mB"], writes=["EB"])
    lv = C.sb("lamv", [128, 4, 64], F32)
    P.op("sp", dma_nc(lv, cst["lamv"][l].partition_broadcast(128)), writes=["lv"])
    li = C.sb("laminit", [128, 2], F32)
    P.op("sp", dma(li, cst["laminit"][l]), writes=["li"])
    lp = C.sb("lamp", [128, 2, 64], F32)
    ls = C.sb("lams", [128, 4], F32)
    lv4 = lv.rearrange("p (a b) d -> p a b d", a=2)
    P.op("dve", tt(lp, lv4[:, :, 0, :], lv4[:, :, 1, :], ALU.mult), reads=["lv"], writes=["lp"])
    P.op("dve", lambda e: e.tensor_reduce(out=ls[:, 0:2], in_=lp, axis=AX.X, op=ALU.add), reads=["lp"], writes=["ls01"])
    P.op("act", actf(ls[:, 0:2], ls[:, 0:2], AF.Exp), reads=["ls01"], writes=["ls01"])
    P.op("dve", tt(ls[:, 2:3], ls[:, 0:1], ls[:, 1:2], ALU.subtract), reads=["ls01"], writes=["ls2"])
    P.op("dve", tt(ls[:, 3:4], ls[:, 2:3], li[:, 0:1], ALU.add), reads=["ls2", "li"], writes=["ls3"])
    T["nlam"] = C.sb("nlam", [128, 1], F32)
    P.op("dve", ts(T["nlam"], ls[:, 3:4], -1.0, ALU.mult), reads=["ls3"], writes=["lam"])
    sub = C.sb("subln", [128, 128], F32)
    P.op("sp", dma_nc(sub, cst["a_subln"][l].partition_broadcast(128)), writes=["sub"])
    T["sublnS"] = C.sb("sublnS", [128, 128], F32)
    P.op("dve", ts(T["sublnS"], sub, li[:, 1:2], ALU.mult), reads=["sub", "li"], writes=["sublnS"])
    T["E"] = C.sb("E", [128, 1280], BF16)
    T["PTb"] = C.sb("PTb", [128, 1280], BF16)
    T["rcb"] = C.sb("rcb", [128, 2, 1], F32)
    for n in ("yb", "yc", "ya", "PTc"):
        T[n] = [C.sb(f"{n}{i}", [128, 128], BF16) for i in range(2)]
    T["ystBC"] = [C.sb(f"ystBC{i}", [128, 2, 512], BF16) for i in range(2)]
    T["ystA"] = [C.sb(f"ystA{i}", [128, 512], BF16) for i in range(2)]
    T["state"] = C.sb("state", [64, 128], F32)
    T["state_bf"] = [C.sb(f"state_bf{i}", [64, 128], BF16) for i in range(2)]
    for n in ("ssc", "lsc", "rsc", "ssa", "lsa", "rsa"):
        T[n] = C.sb(n, [128, 1], F32)
    for n in ("junkc", "tmpc", "junka", "oa", "ob"):
        T[n] = C.sb(n, [128, 128], F32)
    T["rc"] = C.sb("rc", [128, 3], F32)
    T["PT"] = [C.sb(f"PT{i}", [128, 2, 512], BF16) for i in range(3)]


def emit_mix_setup_b(C, d, T):
    P = C.P
    d.fetch(P, "B")
    for i in range(4):
        for n, src in (("bQ", "oBq"), ("bK", "oBk")):
            d.load(P, T[n][:, i * TOK:(i + 1) * TOK], src, i, [n])
        for hh in range(2):
            d.load(P, T["bV"][:, i * 16:(i + 1) * 16, hh, 0:64], "oBv", i, ["bV"], he=(2, 64, hh))


HEADW = 1152


def emit_pre_tok(C, x, hT, wbuf, stage, w_in_d, mixnorm_d, snd_h, rcv_h, gsc, scr):
    P = C.P
    emit_rmsnorm_hT(C, x, hT, mixnorm_d, "m", scr)

    def wview(bi):
        return wbuf[bi % 2][:, 0:4096].rearrange("p (k c) -> p k c", k=8)

    def wkeys(bi):
        return [f"wb{3 * (bi % 2)}", f"wb{3 * (bi % 2) + 1}"]

    def issue_w(g):
        if g < 6:
            blk = 9 + g
            P.op("pq", dma(wview(g), w_in_d[:, blk * 512:(blk + 1) * 512].rearrange("(k p) c -> p k c", p=128)), writes=wkeys(g))

    issue_w(0)
    issue_w(1)
    for k in range(8):
        P.op("sp", dma(snd_h.ap()[k], hT[:, k, :]), reads=[f"hT{t}" for t in range(NT)], writes=[f"sndh{k}"])
    for k in range(8):
        _gather(P, snd_h.ap()[k], rcv_h.ap()[k], [f"sndh{k}"], f"rcvh{k}")
    inst = 0
    for g in range(6):
        if g >= 1:
            issue_w(g + 1)
        wv = wview(g)
        stg = stage[g % 2]
        sk = f"stage{g % 2}"
        for f in range(4):
            for tb in range(4):
                pb = inst % 2
                inst += 1
                ps = C.bank(pb)
                for k in range(8):
                    P.op("pe", mm(ps, wv[:, k, f * 128:(f + 1) * 128], hT[:, k, tb * 512:(tb + 1) * 512], k == 0, k == 7),
                         reads=wkeys(g) + [f"hT{t}" for t in range(tb * 4, tb * 4 + 4)], writes=[f"ps{pb}"])
                P.op("act", actf(stg[:, f, tb * 512:(tb + 1) * 512], ps, AF.Sigmoid), reads=[f"ps{pb}"], writes=[sk])
        P.op("sp", dma(gsc.ap()[g * 4:(g + 1) * 4].rearrange("c p n -> p c n"), stg[:]), reads=[sk])


def emit_head_proj(C, T, rcv_h, mineC, wh_d, cos_d, sin_d, gains_d, l):
    P = C.P
    QK4 = T["QK4"]
    Wh = C.sb("Wh", [128, 8, HEADW], BF16)
    hTt = [C.sb(f"hTt{i}", [128, 8, 512], BF16) for i in range(2)]
    cs = [(C.sb(f"cosg{i}", [128, 16, 64], F32), C.sb(f"sing{i}", [128, 16, 64], F32)) for i in range(2)]
    G8 = C.sb("G8", [128, 8, 64], F32)
    scr = {}
    ND = 2
    for n in ("sq", "xq", "xg", "t1", "t2"):
        scr[n] = [C.sb(f"h{n}{i}", [128, 512], F32) for i in range(ND)]
    res = [C.sb(f"hres{i}", [128, 512], BF16) for i in range(ND)]
    resz = [C.sb(f"hresz{i}", [128, 128], BF16) for i in range(ND)]
    zt = [[C.sb(f"hz{n}{i}", [128, 128], F32) for i in range(ND)] for n in ("a", "b")]
    s8 = C.sb("hs8", [128, 8 * ND], F32)
    l8 = C.sb("hl8", [128, 8 * ND], F32)
    r8 = C.sb("hr8", [128, 8 * ND], F32)
    eg = [C.sb(f"heg{i}", [128, 128], F32) for i in range(ND)]
    cQK2 = [C.sb(f"cQKst{i}", [128, 2048], BF16) for i in range(2)]
    cVs2 = [C.sb(f"cVst{i}", [128, 16, 128], BF16) for i in range(2)]
    cGs2 = [C.sb(f"cGst{i}", [128, 16, 128], BF16) for i in range(2)]
    cKs2 = [C.sb(f"cKst{i}", [128, 16, 64], BF16) for i in range(2)]
    P.op("pq", dma(Wh, wh_d.rearrange("(k p) c -> p k c", p=128)), writes=["Wh"])
    for i in range(4):
        for r in range(2):
            P.op("sp", dma_nc(G8[:, 2 * i + r, :], gains_d[i].partition_broadcast(128)), writes=["G8"])
    rh = rcv_h.ap().rearrange("k (i p) n -> i p k n", p=128)

    def v3(ap, h=8):
        return ap.rearrange("p (h d) -> p h d", h=h)

    def mm_stage(n):
        i, tl = n // 16, n % 16
        par = n % 2
        if n % 4 == 0:
            hb = (n // 4) % 2
            P.op("sp", dma(hTt[hb], rh[i][:, :, (tl // 4) * 512:(tl // 4 + 1) * 512]),
                 reads=[f"rcvh{k}" for k in range(8)], writes=[f"hTt{hb}"])
        hb = (n // 4) % 2
        hcols = slice((n % 4) * 128, (n % 4 + 1) * 128)
        pX, pY, pZ = C.bank(par), C.bank(2 + par), C.bank(4 + par)[:, 0:128]
        for (ps, c0, c1, key) in ((pX, 0, 512, f"ps{par}"), (pY, 512, 1024, f"ps{2 + par}"), (pZ, 1024, 1152, f"ps{4 + par}")):
            for k in range(8):
                P.op("pe", mm(ps, hTt[hb][:, k, hcols], Wh[:, k, c0:c1], k == 0, k == 7),
                     reads=[f"hTt{hb}", "Wh"], writes=[key])

    mm_stage(0)
    for n in range(NKT):
        i, tl = n // 16, n % 16
        par = n % 2
        if tl == 0:
            cosg, sing = cs[i % 2]
            P.op("sp", dma(cosg, cos_d[i]), writes=[f"cs{i % 2}"])
            P.op("sp", dma(sing, sin_d[i]), writes=[f"cs{i % 2}"])
        if n + 1 < NKT:
            mm_stage(n + 1)
        cosg, sing = cs[i % 2]
        csk = f"cs{i % 2}"
        cQK, cVs, cGs, cKs = cQK2[i % 2], cVs2[i % 2], cGs2[i % 2], cKs2[i % 2]
        stk = f"_{i % 2}"
        pX, pY, pZ = C.bank(par), C.bank(2 + par), C.bank(4 + par)[:, 0:128]
        sd = n % ND
        sq, xq, xg, t1, t2 = (scr[m][sd] for m in ("sq", "xq", "xg", "t1", "t2"))
        s8p, l8p, r8p = s8[:, sd * 8:sd * 8 + 8], l8[:, sd * 8:sd * 8 + 8], r8[:, sd * 8:sd * 8 + 8]
        kq = lambda m: f"h{m}_{sd}"
        P.op("act", actf(sq, pX, AF.Square), reads=[f"ps{par}"], writes=[kq("sq")])
        P.op("dve", lambda e, o=s8p, i_=v3(sq): e.tensor_reduce(out=o, in_=i_, axis=AX.X, op=ALU.add), reads=[kq("sq")], writes=[kq("s8")])
        P.op("act", actf(l8p, s8p, AF.Ln, scale=1.0 / 64, bias=C.epsc[:]), reads=[kq("s8"), "epsc"], writes=[kq("l8")])
        P.op("act", actf(r8p, l8p, AF.Exp, scale=-0.5), reads=[kq("l8")], writes=[kq("r8")])
        P.op("dve", tt(v3(xq), v3(pX), r8p.unsqueeze(2).to_broadcast([128, 8, 64]), ALU.mult), reads=[f"ps{par}", kq("r8")], writes=[kq("xq")])
        rs = res[sd]
        rk = f"hres{sd}"
        P.op("dve", tt(v3(xg)[:, 0:4, :], v3(xq)[:, 0:4, :], G8[:, 0:4, :], ALU.mult), reads=[kq("xq"), "G8"], writes=[kq("xg")])
        P.op("dve", tt(v3(rs)[:, 4:8, :], v3(xq)[:, 4:8, :], G8[:, 4:8, :], ALU.mult), reads=[kq("xq"), "G8"], writes=[rk + "b"])
        xa = v3(xg)[:, 0:4, :]
        cb = cosg[:, tl, :].unsqueeze(1).to_broadcast([128, 4, 64])
        sa = sing[:, tl, 0:32].unsqueeze(1).to_broadcast([128, 4, 32])
        sb_ = sing[:, tl, 32:64].unsqueeze(1).to_broadcast([128, 4, 32])
        P.op("dve", tt(v3(t1)[:, 0:4, :], xa, cb, ALU.mult), reads=[kq("xg"), csk], writes=[kq("t1")])
        P.op("pool", tt(v3(t2)[:, 0:4, 0:32], xa[:, :, 32:64], sa, ALU.mult), reads=[kq("xg"), csk], writes=[kq("t2a")])
        P.op("pool", tt(v3(t2)[:, 0:4, 32:64], xa[:, :, 0:32], sb_, ALU.mult), reads=[kq("xg"), csk], writes=[kq("t2b")])
        P.op("dve", tt(rs[:, 0:256], t1[:, 0:256], t2[:, 0:256], ALU.add), reads=[kq("t1"), kq("t2a"), kq("t2b")], writes=[rk + "a"])
        za, zb, rz = zt[0][sd], zt[1][sd], resz[sd]
        zk = f"hz{sd}"
        cb2 = cosg[:, tl, :].unsqueeze(1).to_broadcast([128, 2, 64])
        sa2 = sing[:, tl, 0:32].unsqueeze(1).to_broadcast([128, 2, 32])
        sb2 = sing[:, tl, 32:64].unsqueeze(1).to_broadcast([128, 2, 32])
        pz3 = v3(pZ, 2)
        P.op("dve", tt(v3(za, 2), pz3, cb2, ALU.mult), reads=[f"ps{4 + par}", csk], writes=[zk + "a"])
        P.op("dve", tt(v3(zb, 2)[:, :, 0:32], pz3[:, :, 32:64], sa2, ALU.mult), reads=[f"ps{4 + par}", csk], writes=[zk + "b0"])
        P.op("dve", tt(v3(zb, 2)[:, :, 32:64], pz3[:, :, 0:32], sb2, ALU.mult), reads=[f"ps{4 + par}", csk], writes=[zk + "b1"])
        P.op("dve", tt(rz, za, zb, ALU.add), reads=[zk + "a", zk + "b0", zk + "b1"], writes=[zk + "r"])
        P.op("pool", lambda e, o=cKs[:, tl, :], i_=rz[:, 64:128]: e.tensor_copy(out=o, in_=i_), reads=[zk + "r"], writes=["cKst" + stk])
        P.op("act", actf(T["aV"][:, n, 0:128], pY[:, 0:128], AF.Copy), reads=[f"ps{2 + par}"], writes=["aV"])
        P.op("act", actf(T["bV"][:, n, :, 0:64], pY[:, 128:256].rearrange("p (h e) -> p h e", h=2), AF.Copy),
             reads=[f"ps{2 + par}"], writes=["bV"])
        P.op("act", actf(cVs[:, tl, :], pY[:, 256:384], AF.Copy), reads=[f"ps{2 + par}"], writes=["cVst" + stk])
        P.op("act", actf(eg[sd], pY[:, 384:512], AF.Exp, scale=-1.0), reads=[f"ps{2 + par}"], writes=[f"heg{sd}"])
        P.op("dve", ts(eg[sd], eg[sd], 1.0, ALU.add), reads=[f"heg{sd}"], writes=[f"heg{sd}"])
        P.op("dve", lambda e, o=eg[sd], i_=eg[sd]: e.reciprocal(out=o, in_=i_), reads=[f"heg{sd}"], writes=[f"heg{sd}"])
        P.op("dve", tt(cGs[:, tl, :], pY[:, 384:512], eg[sd], ALU.mult), reads=[f"ps{2 + par}", f"heg{sd}"], writes=["cGst" + stk])
        tb = 6 + par
        pst = C.bank_bf(tb)[:, 0:640]
        for c in range(4):
            P.op("pe", tr(pst[:, c * 128:(c + 1) * 128], rs[:, c * 128:(c + 1) * 128], C.ident[:]),
                 reads=[rk + "a", rk + "b", "ident"], writes=[f"ps{tb}"])
        P.op("pe", tr(pst[:, 512:640], rz, C.ident[:]), reads=[zk + "r", "ident"], writes=[f"ps{tb}"])
        P.op("act", actf(QK4[:, :, n * 128:(n + 1) * 128], pst[:, 0:512].rearrange("p (c q) -> p c q", c=4), AF.Copy),
             reads=[f"ps{tb}"], writes=["QK4"])
        P.op("act", actf(cQK[:, tl * 128:(tl + 1) * 128], pst[:, 512:640], AF.Copy), reads=[f"ps{tb}"], writes=["cQKst" + stk])
        if tl == 15:
            mc = mineC.ap()
            P.op("sp", dma(mc[:, 0:4, i, :], cQK.rearrange("p (k w) -> p k w", w=PW)), reads=["cQKst" + stk], writes=[f"mineCq{i}"])
            P.op("sp", dma(mc[:, 4:8, i, :], cVs.rearrange("p t d -> p (t d)").rearrange("p (k w) -> p k w", w=PW)), reads=["cVst" + stk], writes=[f"mineCv{i}"])
            P.op("sp", dma(mc[:, 8:12, i, :], cGs.rearrange("p t d -> p (t d)").rearrange("p (k w) -> p k w", w=PW)), reads=["cGst" + stk], writes=[f"mineCg{i}"])
            P.op("sp", dma(mc[:, 12:14, i, :], cKs.rearrange("p t d -> p (t d)").rearrange("p (k w) -> p k w", w=PW)), reads=["cKst" + stk], writes=[f"mineCk{i}"])


def build_fused():
    nc = bass.Bass("TRN2", target_bir_lowering=False)

    def din(name, shape, dt=F32):
        return nc.dram_tensor(name, shape, dt, kind="ExternalInput").ap()

    x_d = din("x", [TOK, DM])
    Wd = {n: din(n, [DEPTH] + sh) for n, sh in WNAMES.items()}
    wh_d = din("w_head", [DEPTH, DM, HEADW])
    cos_d, sin_d = din("cos_all", [4, 128, NT, 64]), din("sin_all", [4, 128, NT, 64])
    idn = din("ident", [128, 128], BF16)
    cst = {n: din(n, sh, dt) for n, (sh, dt) in MIXC.items()}
    cst.update({n: din(n, [DEPTH] + sh, dt) for n, (sh, dt) in MIXL.items()})
    out_d = nc.dram_tensor("x_out", [TOK, DM], F32, kind="ExternalOutput").ap()
    snd_h = nc.dram_tensor("snd_h", [8, 128, TOK], BF16)
    rcv_h = nc.dram_tensor("rcv_h", [8, 4 * 128, TOK], BF16)
    mineC = nc.dram_tensor("mineC", [128, NPIECE["C"], 4, PW], BF16)
    snd2 = {"A": nc.dram_tensor("snd2A", [4, 128, TOK], BF16), "BC": nc.dram_tensor("snd2BC", [8, 256, 1024], BF16)}
    rcv2 = {"A": nc.dram_tensor("rcv2A", [4, 4 * 128, TOK], BF16), "BC": nc.dram_tensor("rcv2BC", [8, 4 * 256, 1024], BF16)}
    mine2 = {"A": nc.dram_tensor("mine2A", [4 * 128, TOK], BF16), "BC": nc.dram_tensor("mine2BC", [2, 4 * 256, 1024], BF16)}
    gsc = nc.dram_tensor("gsc", [24, 128, TOK], BF16)
    xs = nc.dram_tensor("xs", [TOK, DM], F32)
    with contextlib.ExitStack() as st:
        C = Ctx(nc, st)
        P = C.P
        P.op("sp", dma(C.ident, idn), writes=["ident"])
        base = C.mark()

        def token_layout():
            C.reset(base)
            L = {}
            L["x"] = C.sb("x", [128, NT, DM], F32)
            L["hT"] = C.sb("hT", [128, 8, TOK], BF16)
            wball = C.sb("wball", [128, 2 * 6144], BF16)
            L["wball"] = wball
            L["wbuf"] = [wball[:, 0:6144], wball[:, 6144:12288]]
            L["wbr"] = wball.rearrange("p (m j c) -> p m j c", m=3, j=4)
            u = C.mark()
            L["stage"] = [C.sb(f"stage{i}", [128, 4, TOK], BF16) for i in range(2)]
            L["kstage"] = C.sb("kstage", [128, 4, 1024], BF16)
            e1 = C.mark()
            C.reset(u)
            L["gbuf"] = C.sb("gbuf", [128, 24, 512], BF16)
            L["ytb"] = C.sb("ytb", [128, 24, 512], BF16)
            C.reset(max(e1, C.mark()))
            L["scr"] = make_scr(C)
            return L

        mix_io = MixIO({}, {"C": mineC}, snd2, rcv2)

        def yt_fetch_a(e):
            return e.dma_start(out=mine2["A"].ap(), in_=rcv2["A"].ap()[bass.ds(_rank(e), 1), :, :].rearrange("o r n -> (o r) n"))

        def yt_fetch_bc(e):
            return e.dma_start(out=mine2["BC"].ap(), in_=rcv2["BC"].ap()[bass.ds(_rank(e) * 2, 2), :, :])

        def yt_load(P, ytb, tb, key):
            y4 = ytb.rearrange("p (j m) n -> p j m n", m=3)
            P.op("sp", dma(y4[:, :, 0, :], mine2["A"].ap()[:, tb * 512:(tb + 1) * 512].rearrange("(j p) n -> p j n", p=128)),
                 reads=["mine2A"], writes=[key])
            for mm_ in range(2):
                P.op("sp", dma(y4[:, :, 1 + mm_, :], mine2["BC"].ap()[tb // 2][:, (tb % 2) * 512:(tb % 2 + 1) * 512]
                               .rearrange("(j m p) n -> p j m n", m=2, p=128)[:, :, mm_, :]), reads=["mine2BC"], writes=[key])

        for l in range(DEPTH):
            L = token_layout()
            x, hT, scr = L["x"], L["hT"], L["scr"]
            if l == 0:
                load_x(C, x, x_d)
            aT2 = [L["stage"][0][:, 0:2, :], L["stage"][0][:, 2:4, :]]
            wdb = [L["kstage"][:, 0:2, :].rearrange("p a n -> p (a n)"), L["kstage"][:, 2:4, :].rearrange("p a n -> p (a n)")]
            emit_ffn(C, x, hT, L["wball"], wdb, aT2, Wd["ffn1_w_gate"][l], Wd["ffn1_w_up"][l], Wd["ffn1_w_down"][l],
                     Wd["ffn1_norm"][l], scr, aT_keys=["stage0"], wd_keys=["kstage"])
            store_x(C, x, xs.ap())
            emit_pre_tok(C, x, hT, L["wbuf"], L["stage"], Wd["w_in"][l], Wd["mix_norm"][l], snd_h, rcv_h, gsc, scr)
            P.barrier()
            C.reset(base)
            T = {}
            T["QK4"] = C.sb("QK4", [128, 4, SEQ], BF16)
            for i, n in enumerate(("aQ", "aK", "bQ", "bK")):
                T[n] = T["QK4"][:, i, :]
            T["aV"] = C.sb("aV", [128, NKT, 129], BF16)
            T["bV"] = C.sb("bV", [128, NKT, 2, 65], BF16)
            P.op("pool", lambda e, T=T: e.memset(T["aV"][:, :, 128:129], 1.0), writes=["aV"])
            P.op("pool", lambda e, T=T: e.memset(T["bV"][:, :, :, 64:65], 1.0), writes=["bV"])
            m1 = C.mark()
            emit_head_proj(C, T, rcv_h, mineC, wh_d[l], cos_d, sin_d,
                           [Wd[n][l] for n in ("a_q_norm", "a_k_norm", "b_q_norm", "b_k_norm")], l)
            P.barrier()
            C.reset(m1)
            emit_mix_setup(C, mix_io, T, cst, l)
            emit_mix_a(C, mix_io, T)
            emit_mix_bc(C, mix_io, T)
            mix_io.flush2()
            P.barrier()
            P.op("sp", yt_fetch_a, reads=[f"rcv2A{k}" for k in range(4)], writes=["mine2A"])
            P.op("sp", yt_fetch_bc, reads=[f"rcv2BC{k}" for k in range(8)], writes=["mine2BC"])
            L = token_layout()
            x, hT, scr = L["x"], L["hT"], L["scr"]
            emit_merge(C, x, hT, L["wbr"], L["gbuf"], L["ytb"],
                       [Wd["w_branch_a"][l], Wd["w_branch_b"][l], Wd["w_branch_c"][l]], Wd["w_out"][l],
                       yt_load, gsc.ap(), scr, after_first_loads=lambda x=x: load_x(C, x, xs.ap()))
            gflat = L["gbuf"].rearrange("p a n -> p (a n)")
            aT2 = [gflat[:, 0:4096].rearrange("p (f n) -> p f n", f=2), gflat[:, 4096:8192].rearrange("p (f n) -> p f n", f=2)]
            yflat = L["ytb"].rearrange("p a n -> p (a n)")
            wdb = [yflat[:, 8192:10240], yflat[:, 10240:12288]]
            emit_ffn(C, x, hT, L["wball"], wdb, aT2, Wd["ffn2_w_gate"][l], Wd["ffn2_w_up"][l], Wd["ffn2_w_down"][l],
                     Wd["ffn2_norm"][l], scr, aT_keys=["gbuf0", "gbuf1"], wd_keys=["ytb1"])
            if l < DEPTH - 1:
                P.barrier()
        store_x(C, x, out_d)
        P.emit()
    return nc


_BF = ml_dtypes.bfloat16
_PROG = []


def head_cols(j):
    r = lambda a, n: list(range(a, a + n))
    return (r(j * 128, 128) + r(512 + j * 128, 128) + r(1536 + j * 128, 128) + r(2048 + j * 128, 128)
            + r(1024 + j * 128, 128) + r(2560 + j * 128, 128) + r(3584 + j * 128, 128) + r(4096 + j * 128, 128)
            + r(3072 + j * 64, 64) + r(3328 + j * 64, 64))


def kernel(**inp):
    x = np.asarray(inp["x"], np.float32)
    if not _PROG:
        _RANK.clear()
        _PROG.append(build_fused())
    nc = _PROG[0]
    ident = np.eye(128, dtype=_BF)
    W = {n: np.ascontiguousarray(np.asarray(inp[n], np.float32)) for n in WNAMES}
    lamv = np.ascontiguousarray(np.stack([inp["a_lambda_q1"], inp["a_lambda_k1"], inp["a_lambda_q2"], inp["a_lambda_k2"]], axis=1)).astype(np.float32)
    tabs = [rope_tables(i) for i in range(4)]
    cos_all = np.ascontiguousarray(np.stack([t[0] for t in tabs]))
    sin_all = np.ascontiguousarray(np.stack([t[1] for t in tabs]))
    maps = []
    for c in range(NCORE):
        b, j = c // 4, c % 4
        m = dict(W)
        m["x"] = np.ascontiguousarray(x[b, j * TOK:(j + 1) * TOK])
        m["w_head"] = np.ascontiguousarray(W["w_in"][:, :, head_cols(j)])
        m["cos_all"], m["sin_all"] = cos_all, sin_all
        m["ident"] = ident
        mc = [mix_consts(j, np.asarray(inp["b_rel_bias"][l], np.float32), l) for l in range(DEPTH)]
        for n in MIXC:
            m[n] = mc[0][n]
        m["biasT"] = np.stack([mc[l]["biasT"] for l in range(DEPTH)])
        m["laminit"] = np.stack([mc[l]["laminit"] for l in range(DEPTH)])
        m["lamv"] = lamv
        m["a_subln"] = np.ascontiguousarray(np.asarray(inp["a_subln"], np.float32))
        m["cnorm"] = np.ascontiguousarray(np.asarray(inp["c_out_norm"], np.float32)[:, j])
        maps.append(m)
    res = run_bass_kernel_spmd(nc, maps, core_ids=list(range(NCORE))).results
    out = np.empty_like(x)
    for c in range(NCORE):
        out[c // 4, (c % 4) * TOK:(c % 4 + 1) * TOK] = np.asarray(res[c]["x_out"])
    return out
```

```python
import contextlib
import math
import numpy as np
import ml_dtypes
import concourse.bass as bass
import concourse.mybir as mybir
from concourse.bass_utils import run_bass_kernel_spmd

F32 = mybir.dt.float32
BF16 = mybir.dt.bfloat16
ALU = mybir.AluOpType
AF = mybir.ActivationFunctionType
AX = mybir.AxisListType

COMPUTE = ("pe", "act", "dve", "pool")
QUEUES = ("sp", "pq", "cc")
NSEM_DMA = 8
EPOCH = 24000

DM = 1024
DFF = 2816
NCORE = 8
TOK = 2048
NT = 16
SEQ = 8192
NKT = 64
EPS = 1e-6
IN_COLS = 7680


class Op:
    __slots__ = ("eng", "fn", "deps", "signaled", "sigcount", "dma", "dma_i")

    def __init__(self, eng, fn, dma):
        self.eng = eng
        self.fn = fn
        self.deps = []
        self.signaled = False
        self.sigcount = 0
        self.dma = dma
        self.dma_i = -1


class Prog:
    def __init__(self, nc):
        self.nc = nc
        self.streams = {"pe": [], "act": [], "dve": [], "pool": [], "sp": []}
        self.last_writer = {}
        self.readers = {}
        self.dma_count = {"sp": 0, "pq": 0, "cc": 0}

    @staticmethod
    def stream_of(eng):
        return "pool" if eng in ("pq", "cc") else eng

    def op(self, eng, fn, reads=(), writes=()):
        dma = eng in QUEUES
        o = Op(eng, fn, dma)
        deps = {}
        for k in reads:
            w = self.last_writer.get(k)
            if w is not None:
                deps[id(w)] = w
        for k in writes:
            w = self.last_writer.get(k)
            if w is not None:
                deps[id(w)] = w
            for r in self.readers.get(k, ()):
                deps[id(r)] = r
        for k in writes:
            self.last_writer[k] = o
            self.readers[k] = []
        for k in reads:
            lst = self.readers.setdefault(k, [])
            if not dma:
                lst[:] = [r for r in lst if r.eng != eng]
            lst.append(o)
        for d in deps.values():
            if d is o:
                continue
            if d.eng == "pe" and eng == "pe":
                continue
            o.deps.append(d)
            d.signaled = True
        if dma:
            o.signaled = True
            o.dma_i = self.dma_count[eng]
            self.dma_count[eng] += 1
        self.streams[self.stream_of(eng)].append(o)
        return o

    def barrier(self, keep=()):
        last = {}
        for e in COMPUTE:
            for o in reversed(self.streams[e]):
                if isinstance(o, Op) and not o.dma:
                    o.signaled = True
                    last[e] = o
                    break
        b = ("barrier", last, dict(self.dma_count))
        for stream in self.streams:
            self.streams[stream].append(b)
        kept = {k: w for k, w in self.last_writer.items() if k in keep or w.eng == "cc"}
        self.last_writer.clear()
        self.readers.clear()
        self.last_writer.update(kept)

    def emit(self):
        nc = self.nc
        cnt = {e: 0 for e in COMPUTE}
        for ops in self.streams.values():
            for o in ops:
                if isinstance(o, Op) and not o.dma and o.signaled:
                    cnt[o.eng] += 1
                    o.sigcount = cnt[o.eng]
        nep = {e: max(1, (cnt[e] + EPOCH - 1) // EPOCH) for e in COMPUTE}
        with contextlib.ExitStack() as st:
            sems = {}
            for e in COMPUTE:
                sems[e] = [st.enter_context(nc.semaphore(f"s_{e}{i}")) for i in range(nep[e])]
            for q in ("sp", "pq"):
                sems[q] = [st.enter_context(nc.semaphore(f"s_{q}{i}")) for i in range(NSEM_DMA)]
            sems["cc"] = [st.enter_context(nc.semaphore("s_cc"))]
            block = st.enter_context(nc.Block())

            def sem_val(d):
                if d.eng == "cc":
                    return sems["cc"][0], d.dma_i + 1
                if d.dma:
                    return sems[d.eng][d.dma_i % NSEM_DMA], 16 * (d.dma_i // NSEM_DMA + 1)
                ep = (d.sigcount - 1) // EPOCH
                return sems[d.eng][ep], d.sigcount - ep * EPOCH

            def mk(stream):
                def body(engh):
                    waited = {}

                    def wait(s, v):
                        if waited.get(id(s), 0) >= v:
                            return
                        waited[id(s)] = v
                        engh.wait_ge(s, v)

                    for o in self.streams[stream]:
                        if not isinstance(o, Op):
                            _, last, counts = o
                            for d in last.values():
                                wait(*sem_val(d))
                            for q in ("sp", "pq"):
                                n = counts[q]
                                for i in range(min(n, NSEM_DMA)):
                                    wait(sems[q][i], 16 * ((n - 1 - i) // NSEM_DMA + 1))
                            continue
                        for d in o.deps:
                            wait(*sem_val(d))
                        if o.eng == "cc":
                            pass
                        elif o.dma and o.dma_i >= NSEM_DMA:
                            wait(sems[o.eng][o.dma_i % NSEM_DMA], 16 * (o.dma_i // NSEM_DMA))
                        ins = o.fn(engh)
                        if o.signaled:
                            s, _ = sem_val(o)
                            ins.then_inc(s, 16 if (o.dma and o.eng != "cc") else 1)
                    if stream == "sp":
                        if self.dma_count["cc"]:
                            wait(sems["cc"][0], self.dma_count["cc"])
                        for q in ("sp", "pq"):
                            n = self.dma_count[q]
                            for i in range(min(n, NSEM_DMA)):
                                tot = (n - 1 - i) // NSEM_DMA + 1
                                wait(sems[q][i], 16 * tot)
                return body

            block.tensor(mk("pe"))
            block.scalar(mk("act"))
            block.vector(mk("dve"))
            block.gpsimd(mk("pool"))
            block.sync(mk("sp"))


def mm(out, lhsT, rhs, start, stop):
    return lambda e: e.matmul(out, lhsT=lhsT, rhs=rhs, start=start, stop=stop, skip_group_check=True)


def tr(out, in_, ident):
    return lambda e: e.transpose(out, in_, ident)


def actf(out, in_, func, scale=1.0, bias=None, accum_out=None):
    kw = {}
    if bias is not None:
        kw["bias"] = bias
    if accum_out is not None:
        kw["accum_out"] = accum_out
    return lambda e: e.activation(out=out, in_=in_, func=func, scale=scale, **kw)


def tt(out, in0, in1, op):
    return lambda e: e.tensor_tensor(out=out, in0=in0, in1=in1, op=op)


def ts(out, in0, s1, op0, s2=None, op1=None):
    if op1 is None:
        return lambda e: e.tensor_scalar(out=out, in0=in0, scalar1=s1, scalar2=None, op0=op0)
    return lambda e: e.tensor_scalar(out=out, in0=in0, scalar1=s1, scalar2=s2, op0=op0, op1=op1)


def stt(out, in0, scalar, in1, op0, op1):
    return lambda e: e.scalar_tensor_tensor(out=out, in0=in0, scalar=scalar, in1=in1, op0=op0, op1=op1)


def dma(out, in_):
    return lambda e: e.dma_start(out=out, in_=in_)


def dma_nc(out, in_):
    return lambda e: e.dma_start(out=out, in_=in_, allow_slow_non_contiguous=True)


ARENA_KIB = 204


class Ctx:
    def __init__(self, nc, st):
        self.nc = nc
        self.st = st
        self.P = Prog(nc)
        self.psum = st.enter_context(nc.psum_tensor("psum", [128, 4096], F32))
        self.arena = st.enter_context(nc.sbuf_tensor("arena", [128, ARENA_KIB * 512], BF16))
        self.off = 0
        self.ident = self.sb("ident", [128, 128], BF16)
        self.epsc = self.sb("epsc", [128, 1], F32)
        self.P.op("pool", lambda e: e.memset(self.epsc, EPS), writes=["epsc"])

    def mark(self):
        return self.off

    def reset(self, m):
        self.off = m

    def sb(self, name, shape, dtype):
        n = 1
        for d in shape[1:]:
            n *= d
        nbytes = n * (4 if dtype == F32 else 2)
        nbytes = (nbytes + 63) // 64 * 64
        assert self.off + nbytes <= ARENA_KIB * 1024, (name, self.off, nbytes)
        v = self.arena[:, self.off // 2:(self.off + nbytes) // 2]
        self.off += nbytes
        if dtype == F32:
            v = v.bitcast(F32)
        v = v[:, 0:n]
        if len(shape) == 3:
            v = v.rearrange("p (a b) -> p a b", a=shape[1])
        elif len(shape) == 4:
            v = v.rearrange("p (a b c) -> p a b c", a=shape[1], b=shape[2])
        if shape[0] < 128:
            v = v[0:shape[0]]
        return v

    def bank(self, b, n=1):
        return self.psum[:, b * 512:(b + n) * 512]

    def bank_bf(self, b, n=1):
        return self.psum[:, b * 512:(b + n) * 512].bitcast(BF16)


def emit_rmsnorm_hT(C, x, hT, gain_d, tag, scr):
    P = C.P
    ss, lnv, rstd, gainT, junk, xn = scr["ss"], scr["lnv"], scr["rstd"], scr["gainT"], scr["junk"], scr["xn"]
    P.op("sp", dma_nc(gainT[:], gain_d.rearrange("(k p) -> p k", p=128)), writes=["gainT"])
    for t in range(NT):
        P.op("act", actf(junk, x[:, t, :], AF.Square, accum_out=ss[:, t:t + 1]),
             reads=[f"x{t}"], writes=[f"ss{t}", "sq_0"])
    allss = [f"ss{t}" for t in range(NT)]
    P.op("act", actf(lnv[:], ss[:], AF.Ln, scale=1.0 / DM, bias=C.epsc[:]), reads=allss + ["epsc"], writes=["lnv"])
    P.op("act", actf(rstd[:], lnv[:], AF.Exp, scale=-0.5), reads=["lnv"], writes=["rstd"])
    for t in range(NT):
        b = t % 2
        xk = scr["xnk"][b]
        P.op("act", actf(xn[b], x[:, t, :], AF.Copy, scale=rstd[:, t:t + 1]), reads=[f"x{t}", "rstd"], writes=[xk])
        pst = C.bank_bf(b)
        for k in range(8):
            P.op("pe", tr(pst[:, k * 128:(k + 1) * 128], xn[b][:, k * 128:(k + 1) * 128], C.ident[:]),
                 reads=[xk, "ident"], writes=[f"ps{b}"])
        P.op("dve", tt(hT[:, :, t * 128:(t + 1) * 128], pst.rearrange("p (k c) -> p k c", k=8),
                       gainT[:].unsqueeze(2).to_broadcast([128, 8, 128]), ALU.mult),
             reads=[f"ps{b}", "gainT"], writes=[f"hT{t}"])


FFN_CHUNKS = [(2 * i, 2) for i in range(11)]


def emit_ffn(C, x, hT, wball, wdbuf, aT2, wg_d, wu_d, wd_d, norm_d, scr, aT_keys=(), wd_keys=()):
    P = C.P
    emit_rmsnorm_hT(C, x, hT, norm_d, "f", scr)
    sg = scr["sg"]
    allhT = [f"hT{t}" for t in range(NT)]
    nch = len(FFN_CHUNKS)
    st = {"nmm": 0}

    def views(ci):
        f0, nf = FFN_CHUNKS[ci]
        cw = nf * 128
        j = ci % 3
        wb = wball[:, j * 4096:(j + 1) * 4096]
        wgv = wb[:, 0:8 * cw].rearrange("p (k c) -> p k c", k=8)
        wuv = wb[:, 2048:2048 + 8 * cw].rearrange("p (k c) -> p k c", k=8)
        wdv = wdbuf[ci % 2][:, 0:nf * 1024].rearrange("p (f c) -> p f c", f=nf)
        return f0, nf, cw, wgv, wuv, wdv, [f"wb{2 * j}", f"wb{2 * j + 1}"], f"wd{ci % 2}"

    def issue_wgu(ci):
        if ci >= nch:
            return
        f0, nf, cw, wgv, wuv, wdv, gk, dk = views(ci)
        P.op("pq", dma(wgv, wg_d[:, f0 * 128:f0 * 128 + cw].rearrange("(k p) c -> p k c", p=128)), writes=[gk[0]])
        P.op("pq", dma(wuv, wu_d[:, f0 * 128:f0 * 128 + cw].rearrange("(k p) c -> p k c", p=128)), writes=[gk[1]])

    def issue_wd(ci):
        if ci >= nch:
            return
        f0, nf, cw, wgv, wuv, wdv, gk, dk = views(ci)
        P.op("pq", dma(wdv, wd_d[f0 * 128:f0 * 128 + cw, :].rearrange("(f p) c -> p f c", p=128)),
             writes=[dk] + (list(wd_keys) if ci < 2 else []))

    def gateup(ci, tb, f):
        f0, nf, cw, wgv, wuv, wdv, gk, dk = views(ci)
        aT = aT2[ci % 2]
        pb = 2 * (st["nmm"] % 2)
        st["nmm"] += 1
        par = st["nmm"] % 2
        pg, pu = C.bank(pb), C.bank(pb + 1)
        for k in range(8):
            P.op("pe", mm(pg, wgv[:, k, f * 128:(f + 1) * 128], hT[:, k, tb * 512:(tb + 1) * 512], k == 0, k == 7),
                 reads=[gk[0]] + allhT[tb * 4:tb * 4 + 4], writes=[f"ps{pb}"])
        for k in range(8):
            P.op("pe", mm(pu, wuv[:, k, f * 128:(f + 1) * 128], hT[:, k, tb * 512:(tb + 1) * 512], k == 0, k == 7),
                 reads=[gk[1]] + allhT[tb * 4:tb * 4 + 4], writes=[f"ps{pb + 1}"])
        P.op("act", actf(sg[par][:], pg, AF.Silu), reads=[f"ps{pb}"], writes=[f"sg{par}"])
        P.op("dve", tt(aT[:, f, tb * 512:(tb + 1) * 512], sg[par][:], pu, ALU.mult),
             reads=[f"sg{par}", f"ps{pb + 1}"], writes=[f"aT{ci % 2}_{f}_{tb}"] + (list(aT_keys) if ci < 2 else []))

    def down(ci, t):
        f0, nf, cw, wgv, wuv, wdv, gk, dk = views(ci)
        aT = aT2[ci % 2]
        pb = 4 + 2 * (t % 2)
        py = C.bank(pb, 2)
        for n in range(2):
            for f in range(nf):
                P.op("pe", mm(py[:, n * 512:(n + 1) * 512], aT[:, f, t * 128:(t + 1) * 128], wdv[:, f, n * 512:(n + 1) * 512],
                              f == 0, f == nf - 1),
                     reads=[dk, f"aT{ci % 2}_{f}_{t // 4}"], writes=[f"ps{pb + n}"])
        P.op("dve", stt(x[:, t, :], py, 0.5, x[:, t, :], ALU.mult, ALU.add),
             reads=[f"ps{pb}", f"ps{pb + 1}", f"x{t}"], writes=[f"x{t}"])

    issue_wgu(0)
    issue_wd(0)
    issue_wgu(1)
    for tb in range(4):
        for f in range(FFN_CHUNKS[0][1]):
            gateup(0, tb, f)
    for ci in range(nch):
        issue_wgu(ci + 2)
        issue_wd(ci + 1)
        nxt = [(tb, f) for tb in range(4) for f in range(FFN_CHUNKS[ci + 1][1])] if ci + 1 < nch else []
        for t in range(NT):
            down(ci, t)
            if nxt and t % 2 == 1:
                gateup(ci + 1, *nxt[t // 2])


def make_scr(C):
    s = {}
    s["ss"] = C.sb("ss", [128, NT], F32)
    s["lnv"] = C.sb("lnv", [128, NT], F32)
    s["rstd"] = C.sb("rstd", [128, NT], F32)
    s["gainT"] = C.sb("gainT", [128, 8], F32)
    for n in ("sq", "xq", "xg", "t1", "t2"):
        s[n + "2"] = [C.sb(f"{n}{i}", [128, 512], F32) for i in range(2)]
        s[n] = s[n + "2"][0]
    s["junk"] = s["sq"].bitcast(BF16)
    s["xn"] = [s["xq"].bitcast(BF16), s["xg"].bitcast(BF16)]
    s["xnk"] = ["xq_0", "xg_0"]
    s["sg"] = [C.sb(f"sg{i}", [128, 512], BF16) for i in range(2)]
    s["res"] = [C.sb(f"res{i}", [128, 512], BF16) for i in range(2)]
    s["s8"] = C.sb("s8", [128, 16], F32)
    s["l8"] = C.sb("l8", [128, 16], F32)
    s["r8"] = C.sb("r8", [128, 16], F32)
    return s


def load_x(C, x, x_d):
    xv = x_d.rearrange("(t p) c -> p t c", p=128)
    for t4 in range(4):
        C.P.op("sp", dma(x[:, t4 * 4:(t4 + 1) * 4, :], xv[:, t4 * 4:(t4 + 1) * 4, :]),
               writes=[f"x{t}" for t in range(t4 * 4, t4 * 4 + 4)])


def store_x(C, x, x_d):
    xv = x_d.rearrange("(t p) c -> p t c", p=128)
    for t4 in range(4):
        C.P.op("sp", dma(xv[:, t4 * 4:(t4 + 1) * 4, :], x[:, t4 * 4:(t4 + 1) * 4, :]),
               reads=[f"x{t}" for t in range(t4 * 4, t4 * 4 + 4)])


PROJ_BLOCKS = [(0, "NR", 0), (1, "NR", 1), (2, "V", None), (3, "N", 2), (4, "N", 3), (5, "V", None),
               (6, "R", None), (7, "V", None), (8, "G", None)]
PROJ_OUT = {0: "oAq", 1: "oAk", 2: "oAv", 3: "oBq", 4: "oBk", 5: "oBv", 6: "oCqk", 7: "oCv", 8: "oCg"}


def emit_proj(C, x, hT, wbuf, stage, kstage, w_in_d, mixnorm_d, gains, cos2, sin2, outs, scr):
    P = C.P
    emit_rmsnorm_hT(C, x, hT, mixnorm_d, "m", scr)
    res = scr["res"]

    def v3(ap, h=8):
        return ap.rearrange("p (h d) -> p h d", h=h)

    blocks = [(blk, kind, gi) for (blk, kind, gi) in PROJ_BLOCKS] + [(9 + g, "GATE", None) for g in range(6)]

    def wview(bi):
        return wbuf[bi % 2][:, 0:4096].rearrange("p (k c) -> p k c", k=8)

    def wkeys(bi):
        return [f"wb{3 * (bi % 2)}", f"wb{3 * (bi % 2) + 1}"]

    def issue_w(bi):
        if bi >= len(blocks):
            return
        blk = blocks[bi][0]
        P.op("pq", dma(wview(bi), w_in_d[:, blk * 512:(blk + 1) * 512].rearrange("(k p) c -> p k c", p=128)),
             writes=wkeys(bi))

    def emit_mm(bi, t, pb):
        wv = wview(bi)
        for k in range(8):
            P.op("pe", mm(C.bank(pb), hT[:, k, t * 128:(t + 1) * 128], wv[:, k, :], k == 0, k == 7),
                 reads=wkeys(bi) + [f"hT{t}"], writes=[f"ps{pb}"])

    issue_w(0)
    inst = 0
    pending = []
    for bi, (blk, kind, gi) in enumerate(blocks):
        issue_w(bi + 1)
        wv = wview(bi)
        wk = f"w{bi % 2}"
        stg = stage[bi % 2]
        sk = f"stage{bi % 2}"
        if kind == "GATE" and blk == 10 and outs.early is not None:
            outs.early()
        if kind == "GATE":
            for f in range(4):
                for tb in range(4):
                    pb = inst % 2
                    inst += 1
                    ps = C.bank(pb)
                    for k in range(8):
                        P.op("pe", mm(ps, wv[:, k, f * 128:(f + 1) * 128], hT[:, k, tb * 512:(tb + 1) * 512], k == 0, k == 7),
                             reads=wkeys(bi) + [f"hT{t}" for t in range(tb * 4, tb * 4 + 4)], writes=[f"ps{pb}"])
                    P.op("act", actf(stg[:, f, tb * 512:(tb + 1) * 512], ps, AF.Sigmoid), reads=[f"ps{pb}"], writes=[sk])
            P.op("sp", dma(outs.gate(blk - 9), stg[:]), reads=[sk])
            for fn in pending:
                fn()
            pending = []
            continue
        for t in range(NT):
            pb = inst % 2
            par = inst % 2
            inst += 1
            sq, xq, xg, t1, t2 = (scr[n][par] for n in ("sq2", "xq2", "xg2", "t12", "t22"))
            s8, l8, r8 = (scr[n][:, par * 8:(par + 1) * 8] for n in ("s8", "l8", "r8"))
            kq = lambda n: f"{n}_{par}"
            ps = C.bank(pb)
            if t == 0:
                emit_mm(bi, 0, pb)
            if t + 1 < NT:
                emit_mm(bi, t + 1, 1 - pb)
            dst = stg[:, :, t * 128:(t + 1) * 128]
            if kind == "V":
                P.op("act", actf(dst, v3(ps, 4), AF.Copy), reads=[f"ps{pb}"], writes=[sk])
                continue
            if kind == "G":
                P.op("act", actf(dst, v3(ps, 4), AF.Silu), reads=[f"ps{pb}"], writes=[sk])
                continue
            rs = res[par]
            rk = f"res{par}"
            if kind in ("NR", "N"):
                P.op("act", actf(sq, ps, AF.Square), reads=[f"ps{pb}"], writes=[kq("sq")])
                P.op("dve", lambda e, o=s8, i=v3(sq): e.tensor_reduce(out=o, in_=i, axis=AX.X, op=ALU.add),
                     reads=[kq("sq")], writes=[kq("s8")])
                P.op("act", actf(l8, s8, AF.Ln, scale=1.0 / 64, bias=C.epsc[:]), reads=[kq("s8"), "epsc"], writes=[kq("l8")])
                P.op("act", actf(r8, l8, AF.Exp, scale=-0.5), reads=[kq("l8")], writes=[kq("r8")])
                P.op("dve", tt(v3(xq), v3(ps), r8.unsqueeze(2).to_broadcast([128, 8, 64]), ALU.mult),
                     reads=[f"ps{pb}", kq("r8")], writes=[kq("xq")])
                gb = gains[:, gi, :].unsqueeze(1).to_broadcast([128, 8, 64])
                if kind == "N":
                    P.op("dve", tt(v3(rs[:]), v3(xq), gb, ALU.mult), reads=[kq("xq"), "gains"], writes=[rk])
                else:
                    P.op("dve", tt(v3(xg), v3(xq), gb, ALU.mult), reads=[kq("xq"), "gains"], writes=[kq("xg")])
                src, srck, eng2 = v3(xg), kq("xg"), "pool"
            else:
                src, srck, eng2 = v3(ps), f"ps{pb}", "dve"
            if kind in ("NR", "R"):
                cb = cos2[:, t, :].unsqueeze(1).to_broadcast([128, 8, 64])
                sa = sin2[:, t, 0:32].unsqueeze(1).to_broadcast([128, 8, 32])
                sb_ = sin2[:, t, 32:64].unsqueeze(1).to_broadcast([128, 8, 32])
                P.op("dve", tt(v3(t1), src, cb, ALU.mult), reads=[srck, "cs"], writes=[kq("t1")])
                P.op(eng2, tt(v3(t2)[:, :, 0:32], src[:, :, 32:64], sa, ALU.mult), reads=[srck, "cs"], writes=[kq("t2a")])
                P.op(eng2, tt(v3(t2)[:, :, 32:64], src[:, :, 0:32], sb_, ALU.mult), reads=[srck, "cs"], writes=[kq("t2b")])
                P.op("dve", tt(rs[:], t1, t2, ALU.add), reads=[kq("t1"), kq("t2a"), kq("t2b")], writes=[rk])
            tb = 2 + par
            pst = C.bank_bf(tb)[:, 0:512]
            for c in range(4):
                P.op("pe", tr(pst[:, c * 128:(c + 1) * 128], rs[:, c * 128:(c + 1) * 128], C.ident[:]),
                     reads=[rk, "ident"], writes=[f"ps{tb}"])
            P.op("act", actf(dst, v3(pst, 4), AF.Copy), reads=[f"ps{tb}"], writes=[sk])
            if kind == "R":
                P.op("pool", lambda e, o=kstage[:, :, t * 64:(t + 1) * 64], i=v3(rs[:, 256:512], 4): e.tensor_copy(out=o, in_=i),
                     reads=[rk], writes=["kstage"])
        for fn in pending:
            fn()
        pending = outs.store_block(P, blk, kind, stg, sk, kstage)
    for fn in pending:
        fn()


def rope_tables(core):
    pos0 = (core % 4) * TOK
    pos = (pos0 + np.arange(TOK)).astype(np.float32)
    half = 32
    inv = (10000.0 ** (-np.arange(half, dtype=np.float32) / half)).astype(np.float32)
    ang = pos[:, None] * inv[None, :]
    cos, sin = np.cos(ang).astype(np.float32), np.sin(ang).astype(np.float32)
    cos2 = np.concatenate([cos, cos], -1).reshape(NT, 128, 64).transpose(1, 0, 2)
    sin2 = np.concatenate([-sin, sin], -1).reshape(NT, 128, 64).transpose(1, 0, 2)
    return np.ascontiguousarray(cos2), np.ascontiguousarray(sin2)


def emit_tr_out(C, ybf, ykey, dst, dkey):
    P = C.P
    pst = C.bank_bf(7)[:, 0:128]
    P.op("pe", tr(pst, ybf, C.ident[:]), reads=[ykey, "ident"], writes=["ps7"])
    P.op("act", actf(dst, pst, AF.Copy), reads=["ps7"], writes=[dkey])


def emit_mix_bc(C, d, T):
    P = C.P
    bQ, bK, bV = T["bQ"], T["bK"], T["bV"]

    def Sb(hh):
        return C.bank(2 * hh, 2)[:, 0:640].rearrange("p (o q) -> p o q", o=5)

    def hv(t_, hh):
        return t_[:, hh * 640:(hh + 1) * 640].rearrange("p (o q) -> p o q", o=5)

    EB = T["EB"]
    Ob = C.bank(4)[:, 0:130]
    Ob3 = Ob.rearrange("p (h e) -> p h e", h=2)
    Sc = C.bank(5)[:, 0:128]
    kv = C.bank(5)[0:64, 256:384]
    Oc = C.bank(6)[:, 0:128]
    E, PTb = T["E"], T["PTb"]
    state, state_bf = T["state"], T["state_bf"]
    P.op("pool", lambda e: e.memset(state[:], 0.0), writes=["state"])
    P.op("pool", lambda e: e.memset(state_bf[1][:], 0.0), writes=["state_bf1"])

    def grp(n):
        g, tl = n // 16, n % 16
        gb = g % 2
        return g, tl, gb, f"cg{gb}", {k: T[k][gb] for k in ("cQ", "cK", "cKt", "cV", "cG", "qxi", "kz")}

    def load_group(g):
        gb = g % 2
        gk = f"cg{gb}"
        B = {k: T[k][gb] for k in ("cQ", "cK", "cKt", "cV", "cG", "qxi", "kz")}
        d.load(P, B["cQ"][:], "oCqk", g, [gk + "q"], rows=(0, 64))
        d.load(P, B["cK"][:], "oCqk", g, [gk + "k"], rows=(64, 64))
        d.load(P, B["cKt"][:], "oCkt", g, [gk + "kt"], t=16)
        d.load(P, B["cV"][:], "oCv", g, [gk + "v"], t=16)
        d.load(P, B["cG"][:], "oCg", g, [gk + "g"], t=16)

    def derive_group(g):
        gb = g % 2
        gk = f"cg{gb}"
        B = {k: T[k][gb] for k in ("cQ", "cKt", "qxi", "kz")}
        P.op("dve", tt(B["qxi"][:].rearrange("p (t q) -> p t q", t=16), B["cQ"][:].rearrange("p (t q) -> p t q", t=16),
                       T["xi"][:].unsqueeze(1).to_broadcast([64, 16, 128]), ALU.mult),
             reads=[gk + "q", "cconst"], writes=[gk + "qxi"])
        P.op("dve", ts(B["kz"][:], B["cKt"][:], T["zeta"][:, 0:1], ALU.mult), reads=[gk + "kt", "cconst"], writes=[gk + "kz"])

    def s1(n):
        g, tl, gb, gk, B = grp(n)
        if tl == 2 and g + 1 < 4:
            load_group(g + 1)
        if tl == 11 and g + 1 < 4:
            derive_group(g + 1)
        o_lo = max(0, 4 - n)
        for hh in range(2):
            for o in range(o_lo, 5):
                kt = n - 4 + o
                P.op("pe", mm(Sb(hh)[:, o, :], bK[hh * 64:(hh + 1) * 64, kt * 128:(kt + 1) * 128],
                              bQ[hh * 64:(hh + 1) * 64, n * 128:(n + 1) * 128], True, True),
                     reads=["bQ", "bK"], writes=[f"ps{2 * hh}", f"ps{2 * hh + 1}"])
        cols = slice(tl * 128, (tl + 1) * 128)
        P.op("pe", mm(Sc, B["cK"][:, cols], B["cQ"][:, cols], True, True), reads=[gk + "q", gk + "k"], writes=["ps5"])
        P.op("pe", mm(kv, B["kz"][:, tl, :], B["cV"][:, tl, :], True, True), reads=[gk + "kz", gk + "v"], writes=["ps5"])

    def s2(n):
        o_lo = max(0, 4 - n)
        for hh in range(2):
            P.op("act", actf(hv(E, hh)[:, o_lo:5, :], Sb(hh)[:, o_lo:5, :], AF.Exp, scale=0.125),
                 reads=[f"ps{2 * hh}", f"ps{2 * hh + 1}"], writes=[f"E{hh}"])
            P.op("dve", tt(hv(PTb, hh)[:, o_lo:5, :], hv(E, hh)[:, o_lo:5, :], hv(EB, hh)[:, o_lo:5, :], ALU.mult),
                 reads=[f"E{hh}", "EB"], writes=[f"PTb{hh}"])
        PTc = T["PTc"][n % 2]
        P.op("dve", tt(PTc[:], Sc, T["DT"][:], ALU.mult), reads=["ps5", "cconst"], writes=[f"PTc{n % 2}"])
        P.op("dve", stt(state[:], state[:], T["cd"][0:64, 0:1], kv, ALU.mult, ALU.add),
             reads=["state", "ps5", "cconst"], writes=["state"])
        P.op("act", actf(state_bf[n % 2][:], state[:], AF.Copy), reads=["state"], writes=[f"state_bf{n % 2}"])

    def s3(n):
        g, tl, gb, gk, B = grp(n)
        o_lo = max(0, 4 - n)
        for hh in range(2):
            for o in range(o_lo, 5):
                kt = n - 4 + o
                P.op("pe", mm(Ob[:, hh * 65:(hh + 1) * 65], hv(PTb, hh)[:, o, :], bV[:, kt, hh, :], o == o_lo, o == 4),
                     reads=[f"PTb{hh}", "bV"], writes=["ps4"])
        cols = slice(tl * 128, (tl + 1) * 128)
        PTc = T["PTc"][n % 2]
        P.op("pe", mm(Oc, PTc[:], B["cV"][:, tl, :], True, False), reads=[f"PTc{n % 2}", gk + "v"], writes=["ps6"])
        P.op("pe", mm(Oc, B["qxi"][:, cols], state_bf[(n - 1) % 2][:], False, True),
             reads=[gk + "qxi", f"state_bf{(n - 1) % 2}"], writes=["ps6"])

    def s4(n):
        g, tl, gb, gk, B = grp(n)
        rcb = T["rcb"]
        P.op("dve", lambda e, o_=rcb[:], i_=Ob3[:, :, 64:65]: e.reciprocal(out=o_, in_=i_), reads=["ps4"], writes=["rcb"])
        yb = T["yb"][n % 2]
        P.op("dve", tt(yb[:].rearrange("p (h e) -> p h e", h=2), Ob3[:, :, 0:64], rcb[:].to_broadcast([128, 2, 64]), ALU.mult),
             reads=["ps4", "rcb"], writes=[f"yb{n % 2}"])
        ssc, lsc, rsc, junkc, tmpc = T["ssc"], T["lsc"], T["rsc"], T["junkc"], T["tmpc"]
        P.op("act", actf(junkc[:], Oc, AF.Square, accum_out=ssc[:]), reads=["ps6"], writes=["ssc", "junkc"])
        P.op("act", actf(lsc[:], ssc[:], AF.Ln, scale=1.0 / 128, bias=C.epsc[:]), reads=["ssc", "epsc"], writes=["lsc"])
        P.op("act", actf(rsc[:], lsc[:], AF.Exp, scale=-0.5), reads=["lsc"], writes=["rsc"])
        P.op("dve", stt(tmpc[:], Oc, rsc[:, 0:1], T["cnorm"][:], ALU.mult, ALU.mult),
             reads=["ps6", "rsc", "cconst"], writes=["tmpc"])
        yc = T["yc"][n % 2]
        P.op("pool", tt(yc[:], tmpc[:], B["cG"][:, tl, :], ALU.mult), reads=["tmpc", gk + "g"], writes=[f"yc{n % 2}"])

    def s5(n):
        ysb = T["ystBC"][(n // 4) % 2]
        ysk = f"ystBC{(n // 4) % 2}"
        emit_tr_out(C, T["yb"][n % 2][:], f"yb{n % 2}", ysb[:, 0, (n % 4) * 128:(n % 4 + 1) * 128], ysk)
        emit_tr_out(C, T["yc"][n % 2][:], f"yc{n % 2}", ysb[:, 1, (n % 4) * 128:(n % 4 + 1) * 128], ysk)
        if n % 4 == 3:
            d.store_bc(P, n // 4, ysb[:], ysk)

    load_group(0)
    derive_group(0)
    s1(0)
    s2(0)
    for n in range(NKT):
        if n + 1 < NKT:
            s1(n + 1)
        s3(n)
        if n + 1 < NKT:
            s2(n + 1)
        s4(n)
        if n >= 1:
            s5(n - 1)
    s5(NKT - 1)


def emit_mix_a(C, d, T, hooks=None):
    P = C.P
    aQ, aK, aV = T["aQ"], T["aK"], T["aV"]
    nlam = T["nlam"]
    steps = [(qb, kt) for qb in range(16) for kt in range(4 * qb + 4)]

    def geom(i):
        qb, kt = steps[i]
        dd = kt - 4 * qb
        q0 = max(dd, 0) * 128
        sbi = i % 2
        S = C.bank(2 * sbi, 2).rearrange("p (c n) -> p c n", c=2)
        return qb, kt, dd, q0, S, [f"ps{2 * sbi}", f"ps{2 * sbi + 1}"]

    def emit_s(i):
        qb, kt, dd, q0, S, skeys = geom(i)
        for c in range(2):
            P.op("pe", mm(S[:, c, q0:512], aK[c * 64:(c + 1) * 64, kt * 128:(kt + 1) * 128],
                          aQ[c * 64:(c + 1) * 64, qb * 512 + q0:(qb + 1) * 512], True, True),
                 reads=["aQ", "aK"], writes=[skeys[c]])

    emit_s(0)
    for i in range(len(steps)):
        qb, kt, dd, q0, S, skeys = geom(i)
        PT = T["PT"][i % 3]
        pk = f"pt{i % 3}"
        P.op("act", actf(PT[:, :, q0:512], S[:, :, q0:512], AF.Exp, scale=0.125), reads=skeys, writes=[pk])
        if dd >= 0:
            P.op("pool", lambda e, o_=PT[64:128, :, q0:q0 + 64]: e.memset(o_, 0.0), reads=[], writes=[pk])
        if i + 1 < len(steps):
            emit_s(i + 1)
        for c in range(2):
            for qt in range(max(dd, 0), 4):
                r = qt * 2 + c
                bk = 4 + r // 3
                off = (r % 3) * 129
                P.op("pe", mm(C.bank(bk)[:, off:off + 129], PT[:, c, qt * 128:(qt + 1) * 128], aV[:, kt, :],
                              kt == 0 and r in (0, 4, 6), kt == 4 * qb + qt),
                     reads=[pk, "aV"], writes=[f"ps{bk}"])
        if kt != 4 * qb + 3:
            continue
        yst = T["ystA"][qb % 2]
        ysk = f"ystA{qb % 2}"
        for qt in range(4):
            r0, r1 = qt * 2, qt * 2 + 1
            O0 = C.bank(4 + r0 // 3)[:, (r0 % 3) * 129:(r0 % 3) * 129 + 129]
            O1 = C.bank(4 + r1 // 3)[:, (r1 % 3) * 129:(r1 % 3) * 129 + 129]
            k0, k1 = f"ps{4 + r0 // 3}", f"ps{4 + r1 // 3}"
            rc, oa, ob = T["rc"], T["oa"], T["ob"]
            P.op("dve", lambda e, o_=rc[:, 0:1], i_=O0[:, 128:129]: e.reciprocal(out=o_, in_=i_), reads=[k0], writes=["rc0"])
            P.op("dve", lambda e, o_=rc[:, 1:2], i_=O1[:, 128:129]: e.reciprocal(out=o_, in_=i_), reads=[k1], writes=["rc1"])
            P.op("dve", tt(rc[:, 2:3], rc[:, 1:2], nlam[:], ALU.mult), reads=["rc1", "lam"], writes=["rc2"])
            P.op("dve", ts(oa[:], O0[:, 0:128], rc[:, 0:1], ALU.mult), reads=[k0, "rc0"], writes=["oa"])
            P.op("dve", stt(ob[:], O1[:, 0:128], rc[:, 2:3], oa[:], ALU.mult, ALU.add), reads=[k1, "rc2", "oa"], writes=["ob"])
            ssa, lsa, rsa, junka = T["ssa"], T["lsa"], T["rsa"], T["junka"]
            P.op("act", actf(junka[:], ob[:], AF.Square, accum_out=ssa[:]), reads=["ob"], writes=["ssa", "junka"])
            P.op("act", actf(lsa[:], ssa[:], AF.Ln, scale=1.0 / 128, bias=C.epsc[:]), reads=["ssa", "epsc"], writes=["lsa"])
            P.op("act", actf(rsa[:], lsa[:], AF.Exp, scale=-0.5), reads=["lsa"], writes=["rsa"])
            ya = T["ya"][qt % 2]
            P.op("dve", stt(ya[:], ob[:], rsa[:, 0:1], T["sublnS"][:], ALU.mult, ALU.mult),
                 reads=["ob", "rsa", "sublnS"], writes=[f"ya{qt % 2}"])
            emit_tr_out(C, ya[:], f"ya{qt % 2}", yst[:, qt * 128:(qt + 1) * 128], ysk)
        d.store_a(P, qb, yst[:], ysk)
        if hooks and qb in hooks:
            hooks[qb]()


def mix_consts(j, rel_bias, layer):
    gamma = np.float32(1.0 - 2.0 ** (-5.0 - j))
    lg = np.log(gamma).astype(np.float32)
    pos = np.arange(128, dtype=np.float32)
    diff = pos[None, :] - pos[:, None]
    DT = np.where(diff >= 0, np.exp(lg * np.maximum(diff, 0.0)), 0.0).astype(np.float32) * np.float32(0.125)
    zeta = (np.exp(lg * (127.0 - pos)) * 0.125).astype(np.float32).reshape(128, 1)
    xi = np.broadcast_to(np.exp(lg * (pos + 1.0)).astype(np.float32)[None, :], (64, 128))
    cd = np.full((128, 1), np.exp(lg * 128.0), np.float32)
    k = np.arange(128)[:, None]
    q = np.arange(128)[None, :]
    biasT = np.zeros((128, 10, 128), np.float32)
    maskB = np.zeros((128, 5, 128), np.float32)
    for o in range(5):
        rel = q - k + 128 * (4 - o)
        idx = np.clip(rel, -63, 256) + 63
        for hh in range(2):
            biasT[:, hh * 5 + o, :] = rel_bias[2 * j + hh][idx]
        dc = (q >= 64).astype(np.int64) - (k >= 64).astype(np.int64) + 2 * (4 - o)
        maskB[:, o, :] = ((dc >= 0) & (dc <= 8)).astype(np.float32)
    lam_init = 0.8 - 0.6 * math.exp(-0.3 * layer)
    li = np.zeros((128, 2), np.float32)
    li[:, 0] = lam_init
    li[:, 1] = 1.0 - lam_init
    return dict(DT=DT, zeta=zeta, xi=np.ascontiguousarray(xi), cd=cd, biasT=biasT, maskB=maskB, laminit=li)


def emit_merge(C, x, mT, wbr, gb_all, yt_all, wbr_d, wout_d, yt_load, g_d, scr, after_first_loads=None):
    P = C.P
    allw = [f"wb{i}" for i in range(6)]
    for m in range(3):
        P.op("pq", dma(wbr[:, m, :, :], wbr_d[m].rearrange("(j p) c -> p j c", p=128)), writes=allw)
    gb2 = [gb_all[:, 0:12, :], gb_all[:, 12:24, :]]
    yt2 = [yt_all[:, 0:12, :], yt_all[:, 12:24, :]]
    seq = [(tb, h) for tb in range(4) for h in range(2)]

    def issue_loads(idx):
        tb, h = seq[idx]
        cols = slice(tb * 512, (tb + 1) * 512)
        if h == 0:
            yt_load(P, yt2[tb % 2], tb, f"ytb{tb % 2}")
        for m in range(3):
            f0 = m * 8 + h * 4
            P.op("sp", dma(gb2[idx % 2][:, m * 4:(m + 1) * 4, :], g_d[f0:f0 + 4, :, cols].rearrange("f p n -> p f n")),
                 writes=[f"gbuf{idx % 2}"])

    issue_loads(0)
    if after_first_loads is not None:
        after_first_loads()
    it = 0
    for idx, (tb, h) in enumerate(seq):
        if idx + 1 < len(seq):
            issue_loads(idx + 1)
        cols = slice(tb * 512, (tb + 1) * 512)
        ytb, gbuf = yt2[tb % 2], gb2[idx % 2]
        yk, gk = f"ytb{tb % 2}", f"gbuf{idx % 2}"
        for fo4 in range(4):
            fo = h * 4 + fo4
            par = it % 2
            b0 = 3 * par
            it += 1
            t1, t2, sq = scr["t12"][par], scr["t22"][par], scr["sq2"][par]
            k1, k2, k3 = f"t1_{par}", f"t2a_{par}", f"sq_{par}"
            for m in range(3):
                for j in range(4):
                    P.op("pe", mm(C.bank(b0 + m), wbr[:, m, j, fo * 128:(fo + 1) * 128], ytb[:, j * 3 + m, :], j == 0, j == 3),
                         reads=allw + [yk], writes=[f"ps{b0 + m}"])
            P.op("dve", tt(t1, C.bank(b0), gbuf[:, fo4, :], ALU.mult), reads=[f"ps{b0}", gk], writes=[k1])
            P.op("dve", tt(t2, C.bank(b0 + 1), gbuf[:, 4 + fo4, :], ALU.mult), reads=[f"ps{b0 + 1}", gk], writes=[k2])
            P.op("dve", tt(sq, C.bank(b0 + 2), gbuf[:, 8 + fo4, :], ALU.mult), reads=[f"ps{b0 + 2}", gk], writes=[k3])
            P.op("pool", tt(t1, t1, t2, ALU.add), reads=[k1, k2], writes=[k1])
            P.op("pool", tt(mT[:, fo, cols], t1, sq, ALU.add), reads=[k1, k3],
                 writes=[f"hT{t}" for t in range(tb * 4, tb * 4 + 4)])
    wo = yt_all.rearrange("p f n -> p (f n)")[:, 0:8192].rearrange("p (k c) -> p k c", k=8)
    P.op("pq", dma(wo, wout_d.rearrange("(k p) c -> p k c", p=128)), writes=["ytb0", "ytb1"])
    for t in range(NT):
        b0 = 6 * (t % 2)
        py = C.bank(b0, 2)
        for n in range(2):
            for k in range(8):
                P.op("pe", mm(py[:, n * 512:(n + 1) * 512], mT[:, k, t * 128:(t + 1) * 128], wo[:, k, n * 512:(n + 1) * 512], k == 0, k == 7),
                     reads=["ytb0", "ytb1", f"hT{t}"], writes=[f"ps{b0 + n}"])
        P.op("dve", tt(x[:, t, :], x[:, t, :], py, ALU.add), reads=[f"ps{b0}", f"ps{b0 + 1}", f"x{t}"], writes=[f"x{t}"])


DEPTH = 2
GROUPS = [[0, 1, 2, 3], [4, 5, 6, 7]]
PW = 512
SEG = {"oAq": ("A", 0), "oAk": ("A", 2048), "oAv": ("A", 4096),
       "oBq": ("B", 0), "oBk": ("B", 2048), "oBv": ("B", 4096),
       "oCqk": ("C", 0), "oCv": ("C", 2048), "oCg": ("C", 4096), "oCkt": ("C", 6144)}
SEGW = {k: 2048 for k in SEG}
SEGW["oCkt"] = 1024
FAMW = {"A": 6144, "B": 6144, "C": 7168}
NPIECE = {f: w // PW for f, w in FAMW.items()}
_RANK = {}


def _rank(e):
    k = id(e)
    if k not in _RANK:
        _RANK[k] = e.snap(e.partition_id() % 4, min_val=0, max_val=3)
    return _RANK[k]


def _gather(P, snd_ap, rcv_ap, reads, wkey):
    P.op("cc", lambda e: e.collective_compute("AllGather", ALU.bypass, replica_groups=GROUPS,
                                              ins=[snd_ap.opt()], outs=[rcv_ap.opt()]), reads=reads, writes=[wkey])


class PreIO:
    def __init__(self, snd, rcv, gsc):
        self.snd, self.rcv = snd, rcv
        self.g = gsc.ap()
        self.deferred = []
        self.early = None

    def gate(self, g):
        return self.g[g * 4:(g + 1) * 4].rearrange("c p n -> p c n")

    def piece(self, f, k):
        return self.snd[f].ap()[k].rearrange("(j p) w -> p j w", p=128)

    def store_block(self, P, blk, kind, stg, sk, kstage):
        f, off = SEG[PROJ_OUT[blk]]
        k0 = off // PW
        later = []
        for k in range(4):
            cs = slice(k * PW, (k + 1) * PW)
            dv = self.piece(f, k0 + k)
            keys = []
            if kind == "R":
                for j in range(4):
                    h2 = slice((j % 2) * 64, (j % 2 + 1) * 64)
                    keys += [f"s{f}{k0 + k}q{j}", f"s{f}{k0 + k}k{j}"]
                    P.op("sp", dma(dv[0:64, j, :], stg[h2, j // 2, cs]), reads=[sk], writes=[keys[-2]])
                    P.op("sp", dma(dv[64:128, j, :], stg[h2, 2 + j // 2, cs]), reads=[sk], writes=[keys[-1]])
            else:
                keys = [f"s{f}{k0 + k}"]
                P.op("sp", dma(dv, stg[:, :, cs]), reads=[sk], writes=keys)
            later.append(lambda extra=(), kk=k0 + k, keys=keys: _gather(P, self.snd[f].ap()[kk], self.rcv[f].ap()[kk], list(keys) + list(extra), f"rcv{f}{kk}"))
        if kind == "R":
            f2, off2 = SEG["oCkt"]
            for k in range(2):
                kk = off2 // PW + k
                P.op("sp", dma(self.piece(f2, kk), kstage[:, :, k * PW:(k + 1) * PW]), reads=["kstage"], writes=[f"s{f2}{kk}"])
                later.append(lambda extra=(), kk=kk: _gather(P, self.snd[f2].ap()[kk], self.rcv[f2].ap()[kk], [f"s{f2}{kk}"] + list(extra), f"rcv{f2}{kk}"))
        if f == "C":
            self.deferred += later
            return []
        return later


class MixIO:
    def __init__(self, rcv, mine, snd2, rcv2):
        self.r = {f: t.ap() for f, t in rcv.items()}
        self.m = {f: t.ap() for f, t in mine.items()}
        self.snd2, self.rcv2 = snd2, rcv2
        self.pend2 = []

    def flush2(self):
        for fn in self.pend2:
            fn()
        self.pend2 = []

    def fetch(self, P, f):
        def fn(e):
            src = self.r[f].rearrange("k (i q) w -> q k i w", q=512)[bass.ds(_rank(e) * 128, 128), :, :, :]
            return e.dma_start(out=self.m[f], in_=src)
        P.op("sp", fn, reads=[f"rcv{f}{k}" for k in range(NPIECE[f])], writes=["mine" + f])

    def key(self, name):
        return "mine" + SEG[name][0]

    def load(self, P, dst, name, i, writes, rows=(0, 128), t=None, he=None):
        (f, off), w = SEG[name], SEGW[name]
        for kk in range(w // PW):
            src = self.m[f][rows[0]:rows[0] + rows[1], off // PW + kk, i, :]
            if he is not None:
                h, e_, hh = he
                a = PW // (h * e_)
                src = src.rearrange("p (a h e) -> p a h e", h=h, e=e_)[:, :, hh, :]
                dv = dst[:, kk * a:(kk + 1) * a, :]
            elif t is not None:
                dd = w // t
                a = PW // dd
                src = src.rearrange("p (a d) -> p a d", d=dd)
                dv = dst[:, kk * a:(kk + 1) * a, :]
            else:
                dv = dst[:, kk * PW:(kk + 1) * PW]
            P.op("sp", dma(dv, src), reads=["mine" + f], writes=writes)

    def store_a(self, P, qb, yst, ysk):
        k = qb // 4
        P.op("sp", dma(self.snd2["A"].ap()[k][:, (qb % 4) * 512:(qb % 4 + 1) * 512], yst), reads=[ysk], writes=[f"s2A{k}_{qb % 4}"])
        self.flush2()
        if qb % 4 == 3:
            self.pend2.append(lambda k=k: _gather(P, self.snd2["A"].ap()[k], self.rcv2["A"].ap()[k], [f"s2A{k}_{i}" for i in range(4)], f"rcv2A{k}"))

    def store_bc(self, P, q4, ysb, ysk):
        k = q4 // 2
        dv = self.snd2["BC"].ap()[k].rearrange("(m p) n -> p m n", p=128)
        P.op("sp", dma(dv[:, :, (q4 % 2) * 512:(q4 % 2 + 1) * 512], ysb), reads=[ysk], writes=[f"s2BC{k}_{q4 % 2}"])
        self.flush2()
        if q4 % 2 == 1:
            self.pend2.append(lambda k=k: _gather(P, self.snd2["BC"].ap()[k], self.rcv2["BC"].ap()[k], [f"s2BC{k}_{i}" for i in range(2)], f"rcv2BC{k}"))


MIXC = {"DT": ([128, 128], F32), "zeta": ([128, 1], F32), "xi": ([64, 128], F32), "cd": ([128, 1], F32),
        "maskB": ([128, 5, 128], F32)}
MIXL = {"biasT": ([128, 10, 128], F32), "laminit": ([128, 2], F32), "lamv": ([4, 64], F32),
        "a_subln": ([128], F32), "cnorm": ([128], F32)}
WNAMES = {"ffn1_norm": [DM], "ffn1_w_gate": [DM, DFF], "ffn1_w_up": [DM, DFF], "ffn1_w_down": [DFF, DM],
          "mix_norm": [DM], "w_in": [DM, IN_COLS], "a_q_norm": [64], "a_k_norm": [64], "b_q_norm": [64], "b_k_norm": [64],
          "w_branch_a": [512, DM], "w_branch_b": [512, DM], "w_branch_c": [512, DM], "w_out": [DM, DM],
          "ffn2_norm": [DM], "ffn2_w_gate": [DM, DFF], "ffn2_w_up": [DM, DFF], "ffn2_w_down": [DFF, DM]}


def emit_mix_setup(C, d, T, cst, l):
    P = C.P
    for n, sh in (("cQ", [64, TOK]), ("cK", [64, TOK]), ("cKt", [128, 16, 64]), ("cV", [128, 16, 128]),
                  ("cG", [128, 16, 128]), ("qxi", [64, TOK]), ("kz", [128, 16, 64])):
        T[n] = [C.sb(f"{n}{i}", sh, BF16) for i in range(2)]
    T["DT"] = C.sb("DT", [128, 128], F32)
    T["zeta"] = C.sb("zeta", [128, 1], F32)
    T["xi"] = C.sb("xi", [64, 128], F32)
    T["cd"] = C.sb("cd", [128, 1], F32)
    T["cnorm"] = C.sb("cnorm", [128, 128], F32)
    for n in ("DT", "zeta", "xi", "cd"):
        P.op("sp", dma(T[n], cst[n]), writes=["cconst"])
    P.op("sp", dma_nc(T["cnorm"], cst["cnorm"][l].partition_broadcast(128)), writes=["cconst"])
    bT = C.sb("biasT", [128, 10, 128], F32)
    mB = C.sb("maskB", [128, 5, 128], F32)
    T["EB"] = C.sb("EB", [128, 1280], BF16)
    P.op("sp", dma(bT, cst["biasT"][l]), writes=["bT"])
    P.op("sp", dma(mB, cst["maskB"]), writes=["mB"])
    P.op("act", actf(bT, bT, AF.Exp), reads=["bT"], writes=["bT"])
    P.op("dve", tt(T["EB"].rearrange("p (h o q) -> p h o q", h=2, o=5), bT.rearrange("p (h o) q -> p h o q", h=2),
                   mB.unsqueeze(1).to_broadcast([128, 2, 5, 128]), ALU.mult), reads=["bT", "mB"], writes=["EB"])
    lv = C.sb("lamv", [128, 4, 64], F32)
    P.op("sp", dma_nc(lv, cst["lamv"][l].partition_broadcast(128)), writes=["lv"])
    li = C.sb("laminit", [128, 2], F32)
    P.op("sp", dma(li, cst["laminit"][l]), writes=["li"])
    lp = C.sb("lamp", [128, 2, 64], F32)
    ls = C.sb("lams", [128, 4], F32)
    lv4 = lv.rearrange("p (a b) d -> p a b d", a=2)
    P.op("dve", tt(lp, lv4[:, :, 0, :], lv4[:, :, 1, :], ALU.mult), reads=["lv"], writes=["lp"])
    P.op("dve", lambda e: e.tensor_reduce(out=ls[:, 0:2], in_=lp, axis=AX.X, op=ALU.add), reads=["lp"], writes=["ls01"])
    P.op("act", actf(ls[:, 0:2], ls[:, 0:2], AF.Exp), reads=["ls01"], writes=["ls01"])
    P.op("dve", tt(ls[:, 2:3], ls[:, 0:1], ls[:, 1:2], ALU.subtract), reads=["ls01"], writes=["ls2"])
    P.op("dve", tt(ls[:, 3:4], ls[:, 2:3], li[:, 0:1], ALU.add), reads=["ls2", "li"], writes=["ls3"])
    T["nlam"] = C.sb("nlam", [128, 1], F32)
    P.op("dve", ts(T["nlam"], ls[:, 3:4], -1.0, ALU.mult), reads=["ls3"], writes=["lam"])
    sub = C.sb("subln", [128, 128], F32)
    P.op("sp", dma_nc(sub, cst["a_subln"][l].partition_broadcast(128)), writes=["sub"])
    T["sublnS"] = C.sb("sublnS", [128, 128], F32)
    P.op("dve", ts(T["sublnS"], sub, li[:, 1:2], ALU.mult), reads=["sub", "li"], writes=["sublnS"])
    T["E"] = C.sb("E", [128, 1280], BF16)
    T["PTb"] = C.sb("PTb", [128, 1280], BF16)
    T["rcb"] = C.sb("rcb", [128, 2, 1], F32)
    for n in ("yb", "yc", "ya", "PTc"):
        T[n] = [C.sb(f"{n}{i}", [128, 128], BF16) for i in range(2)]
    T["ystBC"] = [C.sb(f"ystBC{i}", [128, 2, 512], BF16) for i in range(2)]
    T["ystA"] = [C.sb(f"ystA{i}", [128, 512], BF16) for i in range(2)]
    T["state"] = C.sb("state", [64, 128], F32)
    T["state_bf"] = [C.sb(f"state_bf{i}", [64, 128], BF16) for i in range(2)]
    for n in ("ssc", "lsc", "rsc", "ssa", "lsa", "rsa"):
        T[n] = C.sb(n, [128, 1], F32)
    for n in ("junkc", "tmpc", "junka", "oa", "ob"):
        T[n] = C.sb(n, [128, 128], F32)
    T["rc"] = C.sb("rc", [128, 3], F32)
    T["PT"] = [C.sb(f"PT{i}", [128, 2, 512], BF16) for i in range(3)]


def emit_mix_setup_b(C, d, T):
    P = C.P
    d.fetch(P, "B")
    for i in range(4):
        for n, src in (("bQ", "oBq"), ("bK", "oBk")):
            d.load(P, T[n][:, i * TOK:(i + 1) * TOK], src, i, [n])
        for hh in range(2):
            d.load(P, T["bV"][:, i * 16:(i + 1) * 16, hh, 0:64], "oBv", i, ["bV"], he=(2, 64, hh))


HEADW = 1152


def emit_pre_tok(C, x, hT, wbuf, stage, w_in_d, mixnorm_d, snd_h, rcv_h, gsc, scr):
    P = C.P
    emit_rmsnorm_hT(C, x, hT, mixnorm_d, "m", scr)

    def wview(bi):
        return wbuf[bi % 2][:, 0:4096].rearrange("p (k c) -> p k c", k=8)

    def wkeys(bi):
        return [f"wb{3 * (bi % 2)}", f"wb{3 * (bi % 2) + 1}"]

    def issue_w(g):
        if g < 6:
            blk = 9 + g
            P.op("pq", dma(wview(g), w_in_d[:, blk * 512:(blk + 1) * 512].rearrange("(k p) c -> p k c", p=128)), writes=wkeys(g))

    issue_w(0)
    issue_w(1)
    for p in range(8):
        P.op("sp", dma(snd_h.ap()[p].rearrange("q (k n) -> q k n", k=8), hT[:, :, p * 256:(p + 1) * 256]),
             reads=[f"hT{2 * p}", f"hT{2 * p + 1}"], writes=[f"sndh{p}"])
    for p in range(8):
        _gather(P, snd_h.ap()[p], rcv_h.ap()[p], [f"sndh{p}"], f"rcvh{p}")
    inst = 0
    for g in range(6):
        if g >= 1:
            issue_w(g + 1)
        wv = wview(g)
        stg = stage[g % 2]
        sk = f"stage{g % 2}"
        for f in range(4):
            for tb in range(4):
                pb = inst % 2
                inst += 1
                ps = C.bank(pb)
                for k in range(8):
                    P.op("pe", mm(ps, wv[:, k, f * 128:(f + 1) * 128], hT[:, k, tb * 512:(tb + 1) * 512], k == 0, k == 7),
                         reads=wkeys(g) + [f"hT{t}" for t in range(tb * 4, tb * 4 + 4)], writes=[f"ps{pb}"])
                P.op("act", actf(stg[:, f, tb * 512:(tb + 1) * 512], ps, AF.Sigmoid), reads=[f"ps{pb}"], writes=[sk])
        P.op("sp", dma(gsc.ap()[g * 4:(g + 1) * 4].rearrange("c p n -> p c n"), stg[:]), reads=[sk])


def emit_head_proj(C, T, rcv_h, mineC, wh_d, cos_d, sin_d, gains_d, l):
    P = C.P
    QK4 = T["QK4"]
    Wh = C.sb("Wh", [128, 8, HEADW], BF16)
    hTt = [C.sb(f"hTt{i}", [128, 8, 512], BF16) for i in range(2)]
    cs = [(C.sb(f"cosg{i}", [128, 16, 64], F32), C.sb(f"sing{i}", [128, 16, 64], F32)) for i in range(2)]
    G8 = C.sb("G8", [128, 8, 64], F32)
    scr = {}
    ND = 2
    for n in ("sq", "xq", "xg", "t1", "t2"):
        scr[n] = [C.sb(f"h{n}{i}", [128, 512], F32) for i in range(ND)]
    res = [C.sb(f"hres{i}", [128, 512], BF16) for i in range(ND)]
    resz = [C.sb(f"hresz{i}", [128, 128], BF16) for i in range(ND)]
    zt = [[C.sb(f"hz{n}{i}", [128, 128], F32) for i in range(ND)] for n in ("a", "b")]
    s8 = C.sb("hs8", [128, 8 * ND], F32)
    l8 = C.sb("hl8", [128, 8 * ND], F32)
    r8 = C.sb("hr8", [128, 8 * ND], F32)
    eg = [C.sb(f"heg{i}", [128, 128], F32) for i in range(ND)]
    cQK2 = [C.sb(f"cQKst{i}", [128, 2048], BF16) for i in range(2)]
    cVs2 = [C.sb(f"cVst{i}", [128, 16, 128], BF16) for i in range(2)]
    cGs2 = [C.sb(f"cGst{i}", [128, 16, 128], BF16) for i in range(2)]
    cKs2 = [C.sb(f"cKst{i}", [128, 16, 64], BF16) for i in range(2)]
    P.op("pq", dma(Wh, wh_d.rearrange("(k p) c -> p k c", p=128)), writes=["Wh"])
    for i in range(4):
        for r in range(2):
            P.op("sp", dma_nc(G8[:, 2 * i + r, :], gains_d[i].partition_broadcast(128)), writes=["G8"])
    rh = rcv_h.ap().rearrange("t (i p) (k n) -> t i p k n", p=128, k=8)

    def v3(ap, h=8):
        return ap.rearrange("p (h d) -> p h d", h=h)

    def mm_stage(n):
        i, tl = n // 16, n % 16
        par = n % 2
        if n % 4 == 0:
            hb = (n // 4) % 2
            for half in range(2):
                pc = 2 * (tl // 4) + half
                P.op("sp", dma(hTt[hb][:, :, half * 256:(half + 1) * 256], rh[pc][i]),
                     reads=[f"rcvh{pc}"], writes=[f"hTt{hb}_{half}"])
        hb = (n // 4) % 2
        hcols = slice((n % 4) * 128, (n % 4 + 1) * 128)
        pX, pY, pZ = C.bank(par), C.bank(2 + par), C.bank(4 + par)[:, 0:128]
        for (ps, c0, c1, key) in ((pX, 0, 512, f"ps{par}"), (pY, 512, 1024, f"ps{2 + par}"), (pZ, 1024, 1152, f"ps{4 + par}")):
            for k in range(8):
                P.op("pe", mm(ps, hTt[hb][:, k, hcols], Wh[:, k, c0:c1], k == 0, k == 7),
                     reads=[f"hTt{hb}_{(n % 4) // 2}", "Wh"], writes=[key])

    mm_stage(0)
    for n in range(NKT):
        i, tl = n // 16, n % 16
        par = n % 2
        if tl == 0:
            cosg, sing = cs[i % 2]
            P.op("sp", dma(cosg, cos_d[i]), writes=[f"cs{i % 2}"])
            P.op("sp", dma(sing, sin_d[i]), writes=[f"cs{i % 2}"])
        if n + 1 < NKT:
            mm_stage(n + 1)
        cosg, sing = cs[i % 2]
        csk = f"cs{i % 2}"
        cQK, cVs, cGs, cKs = cQK2[i % 2], cVs2[i % 2], cGs2[i % 2], cKs2[i % 2]
        stk = f"_{i % 2}"
        pX, pY, pZ = C.bank(par), C.bank(2 + par), C.bank(4 + par)[:, 0:128]
        sd = n % ND
        sq, xq, xg, t1, t2 = (scr[m][sd] for m in ("sq", "xq", "xg", "t1", "t2"))
        s8p, l8p, r8p = s8[:, sd * 8:sd * 8 + 8], l8[:, sd * 8:sd * 8 + 8], r8[:, sd * 8:sd * 8 + 8]
        kq = lambda m: f"h{m}_{sd}"
        P.op("act", actf(sq, pX, AF.Square), reads=[f"ps{par}"], writes=[kq("sq")])
        P.op("dve", lambda e, o=s8p, i_=v3(sq): e.tensor_reduce(out=o, in_=i_, axis=AX.X, op=ALU.add), reads=[kq("sq")], writes=[kq("s8")])
        P.op("act", actf(l8p, s8p, AF.Ln, scale=1.0 / 64, bias=C.epsc[:]), reads=[kq("s8"), "epsc"], writes=[kq("l8")])
        P.op("act", actf(r8p, l8p, AF.Exp, scale=-0.5), reads=[kq("l8")], writes=[kq("r8")])
        P.op("dve", tt(v3(xq), v3(pX), r8p.unsqueeze(2).to_broadcast([128, 8, 64]), ALU.mult), reads=[f"ps{par}", kq("r8")], writes=[kq("xq")])
        rs = res[sd]
        rk = f"hres{sd}"
        P.op("dve", tt(v3(xg)[:, 0:4, :], v3(xq)[:, 0:4, :], G8[:, 0:4, :], ALU.mult), reads=[kq("xq"), "G8"], writes=[kq("xg")])
        P.op("dve", tt(v3(rs)[:, 4:8, :], v3(xq)[:, 4:8, :], G8[:, 4:8, :], ALU.mult), reads=[kq("xq"), "G8"], writes=[rk + "b"])
        xa = v3(xg)[:, 0:4, :]
        cb = cosg[:, tl, :].unsqueeze(1).to_broadcast([128, 4, 64])
        sa = sing[:, tl, 0:32].unsqueeze(1).to_broadcast([128, 4, 32])
        sb_ = sing[:, tl, 32:64].unsqueeze(1).to_broadcast([128, 4, 32])
        P.op("dve", tt(v3(t1)[:, 0:4, :], xa, cb, ALU.mult), reads=[kq("xg"), csk], writes=[kq("t1")])
        P.op("pool", tt(v3(t2)[:, 0:4, 0:32], xa[:, :, 32:64], sa, ALU.mult), reads=[kq("xg"), csk], writes=[kq("t2a")])
        P.op("pool", tt(v3(t2)[:, 0:4, 32:64], xa[:, :, 0:32], sb_, ALU.mult), reads=[kq("xg"), csk], writes=[kq("t2b")])
        P.op("dve", tt(rs[:, 0:256], t1[:, 0:256], t2[:, 0:256], ALU.add), reads=[kq("t1"), kq("t2a"), kq("t2b")], writes=[rk + "a"])
        za, zb, rz = zt[0][sd], zt[1][sd], resz[sd]
        zk = f"hz{sd}"
        cb2 = cosg[:, tl, :].unsqueeze(1).to_broadcast([128, 2, 64])
        sa2 = sing[:, tl, 0:32].unsqueeze(1).to_broadcast([128, 2, 32])
        sb2 = sing[:, tl, 32:64].unsqueeze(1).to_broadcast([128, 2, 32])
        pz3 = v3(pZ, 2)
        P.op("dve", tt(v3(za, 2), pz3, cb2, ALU.mult), reads=[f"ps{4 + par}", csk], writes=[zk + "a"])
        P.op("dve", tt(v3(zb, 2)[:, :, 0:32], pz3[:, :, 32:64], sa2, ALU.mult), reads=[f"ps{4 + par}", csk], writes=[zk + "b0"])
        P.op("dve", tt(v3(zb, 2)[:, :, 32:64], pz3[:, :, 0:32], sb2, ALU.mult), reads=[f"ps{4 + par}", csk], writes=[zk + "b1"])
        P.op("dve", tt(rz, za, zb, ALU.add), reads=[zk + "a", zk + "b0", zk + "b1"], writes=[zk + "r"])
        P.op("pool", lambda e, o=cKs[:, tl, :], i_=rz[:, 64:128]: e.tensor_copy(out=o, in_=i_), reads=[zk + "r"], writes=["cKst" + stk])
        P.op("act", actf(T["aV"][:, n, 0:128], pY[:, 0:128], AF.Copy), reads=[f"ps{2 + par}"], writes=["aV"])
        P.op("act", actf(T["bV"][:, n, :, 0:64], pY[:, 128:256].rearrange("p (h e) -> p h e", h=2), AF.Copy),
             reads=[f"ps{2 + par}"], writes=["bV"])
        P.op("act", actf(cVs[:, tl, :], pY[:, 256:384], AF.Copy), reads=[f"ps{2 + par}"], writes=["cVst" + stk])
        P.op("act", actf(eg[sd], pY[:, 384:512], AF.Exp, scale=-1.0), reads=[f"ps{2 + par}"], writes=[f"heg{sd}"])
        P.op("dve", ts(eg[sd], eg[sd], 1.0, ALU.add), reads=[f"heg{sd}"], writes=[f"heg{sd}"])
        P.op("dve", lambda e, o=eg[sd], i_=eg[sd]: e.reciprocal(out=o, in_=i_), reads=[f"heg{sd}"], writes=[f"heg{sd}"])
        P.op("dve", tt(cGs[:, tl, :], pY[:, 384:512], eg[sd], ALU.mult), reads=[f"ps{2 + par}", f"heg{sd}"], writes=["cGst" + stk])
        tb = 6 + par
        pst = C.bank_bf(tb)[:, 0:640]
        for c in range(4):
            P.op("pe", tr(pst[:, c * 128:(c + 1) * 128], rs[:, c * 128:(c + 1) * 128], C.ident[:]),
                 reads=[rk + "a", rk + "b", "ident"], writes=[f"ps{tb}"])
        P.op("pe", tr(pst[:, 512:640], rz, C.ident[:]), reads=[zk + "r", "ident"], writes=[f"ps{tb}"])
        P.op("act", actf(QK4[:, :, n * 128:(n + 1) * 128], pst[:, 0:512].rearrange("p (c q) -> p c q", c=4), AF.Copy),
             reads=[f"ps{tb}"], writes=["QK4"])
        P.op("act", actf(cQK[:, tl * 128:(tl + 1) * 128], pst[:, 512:640], AF.Copy), reads=[f"ps{tb}"], writes=["cQKst" + stk])
        if tl == 15:
            mc = mineC.ap()
            P.op("sp", dma(mc[:, 0:4, i, :], cQK.rearrange("p (k w) -> p k w", w=PW)), reads=["cQKst" + stk], writes=[f"mineCq{i}"])
            P.op("sp", dma(mc[:, 4:8, i, :], cVs.rearrange("p t d -> p (t d)").rearrange("p (k w) -> p k w", w=PW)), reads=["cVst" + stk], writes=[f"mineCv{i}"])
            P.op("sp", dma(mc[:, 8:12, i, :], cGs.rearrange("p t d -> p (t d)").rearrange("p (k w) -> p k w", w=PW)), reads=["cGst" + stk], writes=[f"mineCg{i}"])
            P.op("sp", dma(mc[:, 12:14, i, :], cKs.rearrange("p t d -> p (t d)").rearrange("p (k w) -> p k w", w=PW)), reads=["cKst" + stk], writes=[f"mineCk{i}"])


def build_fused():
    nc = bass.Bass("TRN2", target_bir_lowering=False)

    def din(name, shape, dt=F32):
        return nc.dram_tensor(name, shape, dt, kind="ExternalInput").ap()

    x_d = din("x", [TOK, DM])
    Wd = {n: din(n, [DEPTH] + sh) for n, sh in WNAMES.items()}
    wh_d = din("w_head", [DEPTH, DM, HEADW])
    cos_d, sin_d = din("cos_all", [4, 128, NT, 64]), din("sin_all", [4, 128, NT, 64])
    idn = din("ident", [128, 128], BF16)
    cst = {n: din(n, sh, dt) for n, (sh, dt) in MIXC.items()}
    cst.update({n: din(n, [DEPTH] + sh, dt) for n, (sh, dt) in MIXL.items()})
    out_d = nc.dram_tensor("x_out", [TOK, DM], F32, kind="ExternalOutput").ap()
    snd_h = nc.dram_tensor("snd_h", [8, 128, TOK], BF16)
    rcv_h = nc.dram_tensor("rcv_h", [8, 4 * 128, TOK], BF16)
    mineC = nc.dram_tensor("mineC", [128, NPIECE["C"], 4, PW], BF16)
    snd2 = {"A": nc.dram_tensor("snd2A", [4, 128, TOK], BF16), "BC": nc.dram_tensor("snd2BC", [8, 256, 1024], BF16)}
    rcv2 = {"A": nc.dram_tensor("rcv2A", [4, 4 * 128, TOK], BF16), "BC": nc.dram_tensor("rcv2BC", [8, 4 * 256, 1024], BF16)}
    mine2 = {"A": nc.dram_tensor("mine2A", [4 * 128, TOK], BF16), "BC": nc.dram_tensor("mine2BC", [2, 4 * 256, 1024], BF16)}
    gsc = nc.dram_tensor("gsc", [24, 128, TOK], BF16)
    xs = nc.dram_tensor("xs", [TOK, DM], F32)
    with contextlib.ExitStack() as st:
        C = Ctx(nc, st)
        P = C.P
        P.op("sp", dma(C.ident, idn), writes=["ident"])
        base = C.mark()

        def token_layout():
            C.reset(base)
            L = {}
            L["x"] = C.sb("x", [128, NT, DM], F32)
            L["hT"] = C.sb("hT", [128, 8, TOK], BF16)
            wball = C.sb("wball", [128, 2 * 6144], BF16)
            L["wball"] = wball
            L["wbuf"] = [wball[:, 0:6144], wball[:, 6144:12288]]
            L["wbr"] = wball.rearrange("p (m j c) -> p m j c", m=3, j=4)
            u = C.mark()
            L["stage"] = [C.sb(f"stage{i}", [128, 4, TOK], BF16) for i in range(2)]
            L["kstage"] = C.sb("kstage", [128, 4, 1024], BF16)
            e1 = C.mark()
            C.reset(u)
            L["gbuf"] = C.sb("gbuf", [128, 24, 512], BF16)
            L["ytb"] = C.sb("ytb", [128, 24, 512], BF16)
            C.reset(max(e1, C.mark()))
            L["scr"] = make_scr(C)
            return L

        mix_io = MixIO({}, {"C": mineC}, snd2, rcv2)

        def yt_fetch_a(e):
            return e.dma_start(out=mine2["A"].ap(), in_=rcv2["A"].ap()[bass.ds(_rank(e), 1), :, :].rearrange("o r n -> (o r) n"))

        def yt_fetch_bc(e):
            return e.dma_start(out=mine2["BC"].ap(), in_=rcv2["BC"].ap()[bass.ds(_rank(e) * 2, 2), :, :])

        def yt_load(P, ytb, tb, key):
            y4 = ytb.rearrange("p (j m) n -> p j m n", m=3)
            P.op("sp", dma(y4[:, :, 0, :], mine2["A"].ap()[:, tb * 512:(tb + 1) * 512].rearrange("(j p) n -> p j n", p=128)),
                 reads=["mine2A"], writes=[key])
            for mm_ in range(2):
                P.op("sp", dma(y4[:, :, 1 + mm_, :], mine2["BC"].ap()[tb // 2][:, (tb % 2) * 512:(tb % 2 + 1) * 512]
                               .rearrange("(j m p) n -> p j m n", m=2, p=128)[:, :, mm_, :]), reads=["mine2BC"], writes=[key])

        for l in range(DEPTH):
            L = token_layout()
            x, hT, scr = L["x"], L["hT"], L["scr"]
            if l == 0:
                load_x(C, x, x_d)
            aT2 = [L["stage"][0][:, 0:2, :], L["stage"][0][:, 2:4, :]]
            wdb = [L["kstage"][:, 0:2, :].rearrange("p a n -> p (a n)"), L["kstage"][:, 2:4, :].rearrange("p a n -> p (a n)")]
            emit_ffn(C, x, hT, L["wball"], wdb, aT2, Wd["ffn1_w_gate"][l], Wd["ffn1_w_up"][l], Wd["ffn1_w_down"][l],
                     Wd["ffn1_norm"][l], scr, aT_keys=["stage0"], wd_keys=["kstage"])
            store_x(C, x, xs.ap())
            emit_pre_tok(C, x, hT, L["wbuf"], L["stage"], Wd["w_in"][l], Wd["mix_norm"][l], snd_h, rcv_h, gsc, scr)
            P.barrier()
            C.reset(base)
            T = {}
            T["QK4"] = C.sb("QK4", [128, 4, SEQ], BF16)
            for i, n in enumerate(("aQ", "aK", "bQ", "bK")):
                T[n] = T["QK4"][:, i, :]
            T["aV"] = C.sb("aV", [128, NKT, 129], BF16)
            T["bV"] = C.sb("bV", [128, NKT, 2, 65], BF16)
            P.op("pool", lambda e, T=T: e.memset(T["aV"][:, :, 128:129], 1.0), writes=["aV"])
            P.op("pool", lambda e, T=T: e.memset(T["bV"][:, :, :, 64:65], 1.0), writes=["bV"])
            m1 = C.mark()
            emit_head_proj(C, T, rcv_h, mineC, wh_d[l], cos_d, sin_d,
                           [Wd[n][l] for n in ("a_q_norm", "a_k_norm", "b_q_norm", "b_k_norm")], l)
            P.barrier()
            C.reset(m1)
            emit_mix_setup(C, mix_io, T, cst, l)
            emit_mix_a(C, mix_io, T)
            emit_mix_bc(C, mix_io, T)
            mix_io.flush2()
            P.barrier()
            P.op("sp", yt_fetch_a, reads=[f"rcv2A{k}" for k in range(4)], writes=["mine2A"])
            P.op("sp", yt_fetch_bc, reads=[f"rcv2BC{k}" for k in range(8)], writes=["mine2BC"])
            L = token_layout()
            x, hT, scr = L["x"], L["hT"], L["scr"]
            emit_merge(C, x, hT, L["wbr"], L["gbuf"], L["ytb"],
                       [Wd["w_branch_a"][l], Wd["w_branch_b"][l], Wd["w_branch_c"][l]], Wd["w_out"][l],
                       yt_load, gsc.ap(), scr, after_first_loads=lambda x=x: load_x(C, x, xs.ap()))
            gflat = L["gbuf"].rearrange("p a n -> p (a n)")
            aT2 = [gflat[:, 0:4096].rearrange("p (f n) -> p f n", f=2), gflat[:, 4096:8192].rearrange("p (f n) -> p f n", f=2)]
            yflat = L["ytb"].rearrange("p a n -> p (a n)")
            wdb = [yflat[:, 8192:10240], yflat[:, 10240:12288]]
            emit_ffn(C, x, hT, L["wball"], wdb, aT2, Wd["ffn2_w_gate"][l], Wd["ffn2_w_up"][l], Wd["ffn2_w_down"][l],
                     Wd["ffn2_norm"][l], scr, aT_keys=["gbuf0", "gbuf1"], wd_keys=["ytb1"])
            if l < DEPTH - 1:
                P.barrier()
        store_x(C, x, out_d)
        P.emit()
    return nc


_BF = ml_dtypes.bfloat16
_PROG = []


def head_cols(j):
    r = lambda a, n: list(range(a, a + n))
    return (r(j * 128, 128) + r(512 + j * 128, 128) + r(1536 + j * 128, 128) + r(2048 + j * 128, 128)
            + r(1024 + j * 128, 128) + r(2560 + j * 128, 128) + r(3584 + j * 128, 128) + r(4096 + j * 128, 128)
            + r(3072 + j * 64, 64) + r(3328 + j * 64, 64))


def kernel(**inp):
    x = np.asarray(inp["x"], np.float32)
    if not _PROG:
        _RANK.clear()
        _PROG.append(build_fused())
    nc = _PROG[0]
    ident = np.eye(128, dtype=_BF)
    W = {n: np.ascontiguousarray(np.asarray(inp[n], np.float32)) for n in WNAMES}
    lamv = np.ascontiguousarray(np.stack([inp["a_lambda_q1"], inp["a_lambda_k1"], inp["a_lambda_q2"], inp["a_lambda_k2"]], axis=1)).astype(np.float32)
    tabs = [rope_tables(i) for i in range(4)]
    cos_all = np.ascontiguousarray(np.stack([t[0] for t in tabs]))
    sin_all = np.ascontiguousarray(np.stack([t[1] for t in tabs]))
    maps = []
    for c in range(NCORE):
        b, j = c // 4, c % 4
        m = dict(W)
        m["x"] = np.ascontiguousarray(x[b, j * TOK:(j + 1) * TOK])
        m["w_head"] = np.ascontiguousarray(W["w_in"][:, :, head_cols(j)])
        m["cos_all"], m["sin_all"] = cos_all, sin_all
        m["ident"] = ident
        mc = [mix_consts(j, np.asarray(inp["b_rel_bias"][l], np.float32), l) for l in range(DEPTH)]
        for n in MIXC:
            m[n] = mc[0][n]
        m["biasT"] = np.stack([mc[l]["biasT"] for l in range(DEPTH)])
        m["laminit"] = np.stack([mc[l]["laminit"] for l in range(DEPTH)])
        m["lamv"] = lamv
        m["a_subln"] = np.ascontiguousarray(np.asarray(inp["a_subln"], np.float32))
        m["cnorm"] = np.ascontiguousarray(np.asarray(inp["c_out_norm"], np.float32)[:, j])
        maps.append(m)
    res = run_bass_kernel_spmd(nc, maps, core_ids=list(range(NCORE))).results
    out = np.empty_like(x)
    for c in range(NCORE):
        out[c // 4, (c % 4) * TOK:(c % 4 + 1) * TOK] = np.asarray(res[c]["x_out"])
    return out
```

```python
import contextlib
import math
import numpy as np
import ml_dtypes
import concourse.bass as bass
import concourse.mybir as mybir
from concourse.bass_utils import run_bass_kernel_spmd

F32 = mybir.dt.float32
BF16 = mybir.dt.bfloat16
ALU = mybir.AluOpType
AF = mybir.ActivationFunctionType
AX = mybir.AxisListType

COMPUTE = ("pe", "act", "dve", "pool")
QUEUES = ("sp", "pq", "cc")
NSEM_DMA = 8
EPOCH = 24000

DM = 1024
DFF = 2816
NCORE = 8
TOK = 2048
NT = 16
SEQ = 8192
NKT = 64
EPS = 1e-6
IN_COLS = 7680


class Op:
    __slots__ = ("eng", "fn", "deps", "signaled", "sigcount", "dma", "dma_i")

    def __init__(self, eng, fn, dma):
        self.eng = eng
        self.fn = fn
        self.deps = []
        self.signaled = False
        self.sigcount = 0
        self.dma = dma
        self.dma_i = -1


class Prog:
    def __init__(self, nc):
        self.nc = nc
        self.streams = {"pe": [], "act": [], "dve": [], "pool": [], "sp": []}
        self.last_writer = {}
        self.readers = {}
        self.dma_count = {"sp": 0, "pq": 0, "cc": 0}

    @staticmethod
    def stream_of(eng):
        return "pool" if eng in ("pq", "cc") else eng

    def op(self, eng, fn, reads=(), writes=()):
        dma = eng in QUEUES
        o = Op(eng, fn, dma)
        deps = {}
        for k in reads:
            w = self.last_writer.get(k)
            if w is not None:
                deps[id(w)] = w
        for k in writes:
            w = self.last_writer.get(k)
            if w is not None:
                deps[id(w)] = w
            for r in self.readers.get(k, ()):
                deps[id(r)] = r
        for k in writes:
            self.last_writer[k] = o
            self.readers[k] = []
        for k in reads:
            lst = self.readers.setdefault(k, [])
            if not dma:
                lst[:] = [r for r in lst if r.eng != eng]
            lst.append(o)
        for d in deps.values():
            if d is o:
                continue
            if d.eng == "pe" and eng == "pe":
                continue
            o.deps.append(d)
            d.signaled = True
        if dma:
            o.signaled = True
            o.dma_i = self.dma_count[eng]
            self.dma_count[eng] += 1
        self.streams[self.stream_of(eng)].append(o)
        return o

    def barrier(self, keep=()):
        last = {}
        for e in COMPUTE:
            for o in reversed(self.streams[e]):
                if isinstance(o, Op) and not o.dma:
                    o.signaled = True
                    last[e] = o
                    break
        b = ("barrier", last, dict(self.dma_count))
        for stream in self.streams:
            self.streams[stream].append(b)
        kept = {k: w for k, w in self.last_writer.items() if k in keep or w.eng == "cc"}
        self.last_writer.clear()
        self.readers.clear()
        self.last_writer.update(kept)

    def emit(self):
        nc = self.nc
        cnt = {e: 0 for e in COMPUTE}
        for ops in self.streams.values():
            for o in ops:
                if isinstance(o, Op) and not o.dma and o.signaled:
                    cnt[o.eng] += 1
                    o.sigcount = cnt[o.eng]
        nep = {e: max(1, (cnt[e] + EPOCH - 1) // EPOCH) for e in COMPUTE}
        with contextlib.ExitStack() as st:
            sems = {}
            for e in COMPUTE:
                sems[e] = [st.enter_context(nc.semaphore(f"s_{e}{i}")) for i in range(nep[e])]
            for q in ("sp", "pq"):
                sems[q] = [st.enter_context(nc.semaphore(f"s_{q}{i}")) for i in range(NSEM_DMA)]
            sems["cc"] = [st.enter_context(nc.semaphore("s_cc"))]
            block = st.enter_context(nc.Block())

            def sem_val(d):
                if d.eng == "cc":
                    return sems["cc"][0], d.dma_i + 1
                if d.dma:
                    return sems[d.eng][d.dma_i % NSEM_DMA], 16 * (d.dma_i // NSEM_DMA + 1)
                ep = (d.sigcount - 1) // EPOCH
                return sems[d.eng][ep], d.sigcount - ep * EPOCH

            def mk(stream):
                def body(engh):
                    waited = {}

                    def wait(s, v):
                        if waited.get(id(s), 0) >= v:
                            return
                        waited[id(s)] = v
                        engh.wait_ge(s, v)

                    for o in self.streams[stream]:
                        if not isinstance(o, Op):
                            _, last, counts = o
                            for d in last.values():
                                wait(*sem_val(d))
                            for q in ("sp", "pq"):
                                n = counts[q]
                                for i in range(min(n, NSEM_DMA)):
                                    wait(sems[q][i], 16 * ((n - 1 - i) // NSEM_DMA + 1))
                            continue
                        for d in o.deps:
                            wait(*sem_val(d))
                        if o.eng == "cc":
                            pass
                        elif o.dma and o.dma_i >= NSEM_DMA:
                            wait(sems[o.eng][o.dma_i % NSEM_DMA], 16 * (o.dma_i // NSEM_DMA))
                        ins = o.fn(engh)
                        if o.signaled:
                            s, _ = sem_val(o)
                            ins.then_inc(s, 16 if (o.dma and o.eng != "cc") else 1)
                    if stream == "sp":
                        if self.dma_count["cc"]:
                            wait(sems["cc"][0], self.dma_count["cc"])
                        for q in ("sp", "pq"):
                            n = self.dma_count[q]
                            for i in range(min(n, NSEM_DMA)):
                                tot = (n - 1 - i) // NSEM_DMA + 1
                                wait(sems[q][i], 16 * tot)
                return body

            block.tensor(mk("pe"))
            block.scalar(mk("act"))
            block.vector(mk("dve"))
            block.gpsimd(mk("pool"))
            block.sync(mk("sp"))


def mm(out, lhsT, rhs, start, stop):
    return lambda e: e.matmul(out, lhsT=lhsT, rhs=rhs, start=start, stop=stop, skip_group_check=True)


def tr(out, in_, ident):
    return lambda e: e.transpose(out, in_, ident)


def actf(out, in_, func, scale=1.0, bias=None, accum_out=None):
    kw = {}
    if bias is not None:
        kw["bias"] = bias
    if accum_out is not None:
        kw["accum_out"] = accum_out
    return lambda e: e.activation(out=out, in_=in_, func=func, scale=scale, **kw)


def tt(out, in0, in1, op):
    return lambda e: e.tensor_tensor(out=out, in0=in0, in1=in1, op=op)


def ts(out, in0, s1, op0, s2=None, op1=None):
    if op1 is None:
        return lambda e: e.tensor_scalar(out=out, in0=in0, scalar1=s1, scalar2=None, op0=op0)
    return lambda e: e.tensor_scalar(out=out, in0=in0, scalar1=s1, scalar2=s2, op0=op0, op1=op1)


def stt(out, in0, scalar, in1, op0, op1):
    return lambda e: e.scalar_tensor_tensor(out=out, in0=in0, scalar=scalar, in1=in1, op0=op0, op1=op1)


def dma(out, in_):
    return lambda e: e.dma_start(out=out, in_=in_)


def dma_nc(out, in_):
    return lambda e: e.dma_start(out=out, in_=in_, allow_slow_non_contiguous=True)


ARENA_KIB = 204


class Ctx:
    def __init__(self, nc, st):
        self.nc = nc
        self.st = st
        self.P = Prog(nc)
        self.psum = st.enter_context(nc.psum_tensor("psum", [128, 4096], F32))
        self.arena = st.enter_context(nc.sbuf_tensor("arena", [128, ARENA_KIB * 512], BF16))
        self.off = 0
        self.ident = self.sb("ident", [128, 128], BF16)
        self.epsc = self.sb("epsc", [128, 1], F32)
        self.P.op("pool", lambda e: e.memset(self.epsc, EPS), writes=["epsc"])

    def mark(self):
        return self.off

    def reset(self, m):
        self.off = m

    def sb(self, name, shape, dtype):
        n = 1
        for d in shape[1:]:
            n *= d
        nbytes = n * (4 if dtype == F32 else 2)
        nbytes = (nbytes + 63) // 64 * 64
        assert self.off + nbytes <= ARENA_KIB * 1024, (name, self.off, nbytes)
        v = self.arena[:, self.off // 2:(self.off + nbytes) // 2]
        self.off += nbytes
        if dtype == F32:
            v = v.bitcast(F32)
        v = v[:, 0:n]
        if len(shape) == 3:
            v = v.rearrange("p (a b) -> p a b", a=shape[1])
        elif len(shape) == 4:
            v = v.rearrange("p (a b c) -> p a b c", a=shape[1], b=shape[2])
        if shape[0] < 128:
            v = v[0:shape[0]]
        return v

    def bank(self, b, n=1):
        return self.psum[:, b * 512:(b + n) * 512]

    def bank_bf(self, b, n=1):
        return self.psum[:, b * 512:(b + n) * 512].bitcast(BF16)


def emit_rmsnorm_hT(C, x, hT, gain_d, tag, scr):
    P = C.P
    ss, lnv, rstd, gainT, junk, xn = scr["ss"], scr["lnv"], scr["rstd"], scr["gainT"], scr["junk"], scr["xn"]
    P.op("sp", dma_nc(gainT[:], gain_d.rearrange("(k p) -> p k", p=128)), writes=["gainT"])
    for t in range(NT):
        P.op("act", actf(junk, x[:, t, :], AF.Square, accum_out=ss[:, t:t + 1]),
             reads=[f"x{t}"], writes=[f"ss{t}", "sq_0"])
    allss = [f"ss{t}" for t in range(NT)]
    P.op("act", actf(lnv[:], ss[:], AF.Ln, scale=1.0 / DM, bias=C.epsc[:]), reads=allss + ["epsc"], writes=["lnv"])
    P.op("act", actf(rstd[:], lnv[:], AF.Exp, scale=-0.5), reads=["lnv"], writes=["rstd"])
    for t in range(NT):
        b = t % 2
        xk = scr["xnk"][b]
        P.op("act", actf(xn[b], x[:, t, :], AF.Copy, scale=rstd[:, t:t + 1]), reads=[f"x{t}", "rstd"], writes=[xk])
        pst = C.bank_bf(b)
        for k in range(8):
            P.op("pe", tr(pst[:, k * 128:(k + 1) * 128], xn[b][:, k * 128:(k + 1) * 128], C.ident[:]),
                 reads=[xk, "ident"], writes=[f"ps{b}"])
        P.op("dve", tt(hT[:, :, t * 128:(t + 1) * 128], pst.rearrange("p (k c) -> p k c", k=8),
                       gainT[:].unsqueeze(2).to_broadcast([128, 8, 128]), ALU.mult),
             reads=[f"ps{b}", "gainT"], writes=[f"hT{t}"])


FFN_CHUNKS = [(2 * i, 2) for i in range(11)]


def emit_ffn(C, x, hT, wball, wdbuf, aT2, wg_d, wu_d, wd_d, norm_d, scr, aT_keys=(), wd_keys=()):
    P = C.P
    emit_rmsnorm_hT(C, x, hT, norm_d, "f", scr)
    sg = scr["sg"]
    allhT = [f"hT{t}" for t in range(NT)]
    nch = len(FFN_CHUNKS)
    st = {"nmm": 0}

    def views(ci):
        f0, nf = FFN_CHUNKS[ci]
        cw = nf * 128
        j = ci % 3
        wb = wball[:, j * 4096:(j + 1) * 4096]
        wgv = wb[:, 0:8 * cw].rearrange("p (k c) -> p k c", k=8)
        wuv = wb[:, 2048:2048 + 8 * cw].rearrange("p (k c) -> p k c", k=8)
        wdv = wdbuf[ci % 2][:, 0:nf * 1024].rearrange("p (f c) -> p f c", f=nf)
        return f0, nf, cw, wgv, wuv, wdv, [f"wb{2 * j}", f"wb{2 * j + 1}"], f"wd{ci % 2}"

    def issue_wgu(ci):
        if ci >= nch:
            return
        f0, nf, cw, wgv, wuv, wdv, gk, dk = views(ci)
        P.op("pq", dma(wgv, wg_d[:, f0 * 128:f0 * 128 + cw].rearrange("(k p) c -> p k c", p=128)), writes=[gk[0]])
        P.op("pq", dma(wuv, wu_d[:, f0 * 128:f0 * 128 + cw].rearrange("(k p) c -> p k c", p=128)), writes=[gk[1]])

    def issue_wd(ci):
        if ci >= nch:
            return
        f0, nf, cw, wgv, wuv, wdv, gk, dk = views(ci)
        P.op("pq", dma(wdv, wd_d[f0 * 128:f0 * 128 + cw, :].rearrange("(f p) c -> p f c", p=128)),
             writes=[dk] + (list(wd_keys) if ci < 2 else []))

    def gateup(ci, tb, f):
        f0, nf, cw, wgv, wuv, wdv, gk, dk = views(ci)
        aT = aT2[ci % 2]
        pb = 2 * (st["nmm"] % 2)
        st["nmm"] += 1
        par = st["nmm"] % 2
        pg, pu = C.bank(pb), C.bank(pb + 1)
        for k in range(8):
            P.op("pe", mm(pg, wgv[:, k, f * 128:(f + 1) * 128], hT[:, k, tb * 512:(tb + 1) * 512], k == 0, k == 7),
                 reads=[gk[0]] + allhT[tb * 4:tb * 4 + 4], writes=[f"ps{pb}"])
        for k in range(8):
            P.op("pe", mm(pu, wuv[:, k, f * 128:(f + 1) * 128], hT[:, k, tb * 512:(tb + 1) * 512], k == 0, k == 7),
                 reads=[gk[1]] + allhT[tb * 4:tb * 4 + 4], writes=[f"ps{pb + 1}"])
        P.op("act", actf(sg[par][:], pg, AF.Silu), reads=[f"ps{pb}"], writes=[f"sg{par}"])
        P.op("dve", tt(aT[:, f, tb * 512:(tb + 1) * 512], sg[par][:], pu, ALU.mult),
             reads=[f"sg{par}", f"ps{pb + 1}"], writes=[f"aT{ci % 2}_{f}_{tb}"] + (list(aT_keys) if ci < 2 else []))

    def down(ci, t):
        f0, nf, cw, wgv, wuv, wdv, gk, dk = views(ci)
        aT = aT2[ci % 2]
        pb = 4 + 2 * (t % 2)
        py = C.bank(pb, 2)
        for n in range(2):
            for f in range(nf):
                P.op("pe", mm(py[:, n * 512:(n + 1) * 512], aT[:, f, t * 128:(t + 1) * 128], wdv[:, f, n * 512:(n + 1) * 512],
                              f == 0, f == nf - 1),
                     reads=[dk, f"aT{ci % 2}_{f}_{t // 4}"], writes=[f"ps{pb + n}"])
        P.op("dve", stt(x[:, t, :], py, 0.5, x[:, t, :], ALU.mult, ALU.add),
             reads=[f"ps{pb}", f"ps{pb + 1}", f"x{t}"], writes=[f"x{t}"])

    issue_wgu(0)
    issue_wd(0)
    issue_wgu(1)
    for tb in range(4):
        for f in range(FFN_CHUNKS[0][1]):
            gateup(0, tb, f)
    for ci in range(nch):
        issue_wgu(ci + 2)
        issue_wd(ci + 1)
        nxt = [(tb, f) for tb in range(4) for f in range(FFN_CHUNKS[ci + 1][1])] if ci + 1 < nch else []
        for t in range(NT):
            down(ci, t)
            if nxt and t % 2 == 1:
                gateup(ci + 1, *nxt[t // 2])


def make_scr(C):
    s = {}
    s["ss"] = C.sb("ss", [128, NT], F32)
    s["lnv"] = C.sb("lnv", [128, NT], F32)
    s["rstd"] = C.sb("rstd", [128, NT], F32)
    s["gainT"] = C.sb("gainT", [128, 8], F32)
    for n in ("sq", "xq", "xg", "t1", "t2"):
        s[n + "2"] = [C.sb(f"{n}{i}", [128, 512], F32) for i in range(2)]
        s[n] = s[n + "2"][0]
    s["junk"] = s["sq"].bitcast(BF16)
    s["xn"] = [s["xq"].bitcast(BF16), s["xg"].bitcast(BF16)]
    s["xnk"] = ["xq_0", "xg_0"]
    s["sg"] = [C.sb(f"sg{i}", [128, 512], BF16) for i in range(2)]
    s["res"] = [C.sb(f"res{i}", [128, 512], BF16) for i in range(2)]
    s["s8"] = C.sb("s8", [128, 16], F32)
    s["l8"] = C.sb("l8", [128, 16], F32)
    s["r8"] = C.sb("r8", [128, 16], F32)
    return s


def load_x(C, x, x_d):
    xv = x_d.rearrange("(t p) c -> p t c", p=128)
    for t4 in range(4):
        C.P.op("sp", dma(x[:, t4 * 4:(t4 + 1) * 4, :], xv[:, t4 * 4:(t4 + 1) * 4, :]),
               writes=[f"x{t}" for t in range(t4 * 4, t4 * 4 + 4)])


def store_x(C, x, x_d):
    xv = x_d.rearrange("(t p) c -> p t c", p=128)
    for t4 in range(4):
        C.P.op("sp", dma(xv[:, t4 * 4:(t4 + 1) * 4, :], x[:, t4 * 4:(t4 + 1) * 4, :]),
               reads=[f"x{t}" for t in range(t4 * 4, t4 * 4 + 4)])


PROJ_BLOCKS = [(0, "NR", 0), (1, "NR", 1), (2, "V", None), (3, "N", 2), (4, "N", 3), (5, "V", None),
               (6, "R", None), (7, "V", None), (8, "G", None)]
PROJ_OUT = {0: "oAq", 1: "oAk", 2: "oAv", 3: "oBq", 4: "oBk", 5: "oBv", 6: "oCqk", 7: "oCv", 8: "oCg"}


def emit_proj(C, x, hT, wbuf, stage, kstage, w_in_d, mixnorm_d, gains, cos2, sin2, outs, scr):
    P = C.P
    emit_rmsnorm_hT(C, x, hT, mixnorm_d, "m", scr)
    res = scr["res"]

    def v3(ap, h=8):
        return ap.rearrange("p (h d) -> p h d", h=h)

    blocks = [(blk, kind, gi) for (blk, kind, gi) in PROJ_BLOCKS] + [(9 + g, "GATE", None) for g in range(6)]

    def wview(bi):
        return wbuf[bi % 2][:, 0:4096].rearrange("p (k c) -> p k c", k=8)

    def wkeys(bi):
        return [f"wb{3 * (bi % 2)}", f"wb{3 * (bi % 2) + 1}"]

    def issue_w(bi):
        if bi >= len(blocks):
            return
        blk = blocks[bi][0]
        P.op("pq", dma(wview(bi), w_in_d[:, blk * 512:(blk + 1) * 512].rearrange("(k p) c -> p k c", p=128)),
             writes=wkeys(bi))

    def emit_mm(bi, t, pb):
        wv = wview(bi)
        for k in range(8):
            P.op("pe", mm(C.bank(pb), hT[:, k, t * 128:(t + 1) * 128], wv[:, k, :], k == 0, k == 7),
                 reads=wkeys(bi) + [f"hT{t}"], writes=[f"ps{pb}"])

    issue_w(0)
    inst = 0
    pending = []
    for bi, (blk, kind, gi) in enumerate(blocks):
        issue_w(bi + 1)
        wv = wview(bi)
        wk = f"w{bi % 2}"
        stg = stage[bi % 2]
        sk = f"stage{bi % 2}"
        if kind == "GATE" and blk == 10 and outs.early is not None:
            outs.early()
        if kind == "GATE":
            for f in range(4):
                for tb in range(4):
                    pb = inst % 2
                    inst += 1
                    ps = C.bank(pb)
                    for k in range(8):
                        P.op("pe", mm(ps, wv[:, k, f * 128:(f + 1) * 128], hT[:, k, tb * 512:(tb + 1) * 512], k == 0, k == 7),
                             reads=wkeys(bi) + [f"hT{t}" for t in range(tb * 4, tb * 4 + 4)], writes=[f"ps{pb}"])
                    P.op("act", actf(stg[:, f, tb * 512:(tb + 1) * 512], ps, AF.Sigmoid), reads=[f"ps{pb}"], writes=[sk])
            P.op("sp", dma(outs.gate(blk - 9), stg[:]), reads=[sk])
            for fn in pending:
                fn()
            pending = []
            continue
        for t in range(NT):
            pb = inst % 2
            par = inst % 2
            inst += 1
            sq, xq, xg, t1, t2 = (scr[n][par] for n in ("sq2", "xq2", "xg2", "t12", "t22"))
            s8, l8, r8 = (scr[n][:, par * 8:(par + 1) * 8] for n in ("s8", "l8", "r8"))
            kq = lambda n: f"{n}_{par}"
            ps = C.bank(pb)
            if t == 0:
                emit_mm(bi, 0, pb)
            if t + 1 < NT:
                emit_mm(bi, t + 1, 1 - pb)
            dst = stg[:, :, t * 128:(t + 1) * 128]
            if kind == "V":
                P.op("act", actf(dst, v3(ps, 4), AF.Copy), reads=[f"ps{pb}"], writes=[sk])
                continue
            if kind == "G":
                P.op("act", actf(dst, v3(ps, 4), AF.Silu), reads=[f"ps{pb}"], writes=[sk])
                continue
            rs = res[par]
            rk = f"res{par}"
            if kind in ("NR", "N"):
                P.op("act", actf(sq, ps, AF.Square), reads=[f"ps{pb}"], writes=[kq("sq")])
                P.op("dve", lambda e, o=s8, i=v3(sq): e.tensor_reduce(out=o, in_=i, axis=AX.X, op=ALU.add),
                     reads=[kq("sq")], writes=[kq("s8")])
                P.op("act", actf(l8, s8, AF.Ln, scale=1.0 / 64, bias=C.epsc[:]), reads=[kq("s8"), "epsc"], writes=[kq("l8")])
                P.op("act", actf(r8, l8, AF.Exp, scale=-0.5), reads=[kq("l8")], writes=[kq("r8")])
                P.op("dve", tt(v3(xq), v3(ps), r8.unsqueeze(2).to_broadcast([128, 8, 64]), ALU.mult),
                     reads=[f"ps{pb}", kq("r8")], writes=[kq("xq")])
                gb = gains[:, gi, :].unsqueeze(1).to_broadcast([128, 8, 64])
                if kind == "N":
                    P.op("dve", tt(v3(rs[:]), v3(xq), gb, ALU.mult), reads=[kq("xq"), "gains"], writes=[rk])
                else:
                    P.op("dve", tt(v3(xg), v3(xq), gb, ALU.mult), reads=[kq("xq"), "gains"], writes=[kq("xg")])
                src, srck, eng2 = v3(xg), kq("xg"), "pool"
            else:
                src, srck, eng2 = v3(ps), f"ps{pb}", "dve"
            if kind in ("NR", "R"):
                cb = cos2[:, t, :].unsqueeze(1).to_broadcast([128, 8, 64])
                sa = sin2[:, t, 0:32].unsqueeze(1).to_broadcast([128, 8, 32])
                sb_ = sin2[:, t, 32:64].unsqueeze(1).to_broadcast([128, 8, 32])
                P.op("dve", tt(v3(t1), src, cb, ALU.mult), reads=[srck, "cs"], writes=[kq("t1")])
                P.op(eng2, tt(v3(t2)[:, :, 0:32], src[:, :, 32:64], sa, ALU.mult), reads=[srck, "cs"], writes=[kq("t2a")])
                P.op(eng2, tt(v3(t2)[:, :, 32:64], src[:, :, 0:32], sb_, ALU.mult), reads=[srck, "cs"], writes=[kq("t2b")])
                P.op("dve", tt(rs[:], t1, t2, ALU.add), reads=[kq("t1"), kq("t2a"), kq("t2b")], writes=[rk])
            tb = 2 + par
            pst = C.bank_bf(tb)[:, 0:512]
            for c in range(4):
                P.op("pe", tr(pst[:, c * 128:(c + 1) * 128], rs[:, c * 128:(c + 1) * 128], C.ident[:]),
                     reads=[rk, "ident"], writes=[f"ps{tb}"])
            P.op("act", actf(dst, v3(pst, 4), AF.Copy), reads=[f"ps{tb}"], writes=[sk])
            if kind == "R":
                P.op("pool", lambda e, o=kstage[:, :, t * 64:(t + 1) * 64], i=v3(rs[:, 256:512], 4): e.tensor_copy(out=o, in_=i),
                     reads=[rk], writes=["kstage"])
        for fn in pending:
            fn()
        pending = outs.store_block(P, blk, kind, stg, sk, kstage)
    for fn in pending:
        fn()


def rope_tables(core):
    pos0 = (core % 4) * TOK
    pos = (pos0 + np.arange(TOK)).astype(np.float32)
    half = 32
    inv = (10000.0 ** (-np.arange(half, dtype=np.float32) / half)).astype(np.float32)
    ang = pos[:, None] * inv[None, :]
    cos, sin = np.cos(ang).astype(np.float32), np.sin(ang).astype(np.float32)
    cos2 = np.concatenate([cos, cos], -1).reshape(NT, 128, 64).transpose(1, 0, 2)
    sin2 = np.concatenate([-sin, sin], -1).reshape(NT, 128, 64).transpose(1, 0, 2)
    return np.ascontiguousarray(cos2), np.ascontiguousarray(sin2)


def emit_tr_out(C, ybf, ykey, dst, dkey):
    P = C.P
    pst = C.bank_bf(7)[:, 0:128]
    P.op("pe", tr(pst, ybf, C.ident[:]), reads=[ykey, "ident"], writes=["ps7"])
    P.op("act", actf(dst, pst, AF.Copy), reads=["ps7"], writes=[dkey])


def emit_mix_bc(C, d, T):
    P = C.P
    bQ, bK, bV = T["bQ"], T["bK"], T["bV"]

    def Sb(hh):
        return C.bank(2 * hh, 2)[:, 0:640].rearrange("p (o q) -> p o q", o=5)

    def hv(t_, hh):
        return t_[:, hh * 640:(hh + 1) * 640].rearrange("p (o q) -> p o q", o=5)

    EB = T["EB"]
    Ob = C.bank(4)[:, 0:130]
    Ob3 = Ob.rearrange("p (h e) -> p h e", h=2)
    Sc = C.bank(5)[:, 0:128]
    kv = C.bank(5)[0:64, 256:384]
    Oc = C.bank(6)[:, 0:128]
    E, PTb = T["E"], T["PTb"]
    state, state_bf = T["state"], T["state_bf"]
    P.op("pool", lambda e: e.memset(state[:], 0.0), writes=["state"])
    P.op("pool", lambda e: e.memset(state_bf[1][:], 0.0), writes=["state_bf1"])

    def grp(n):
        g, tl = n // 16, n % 16
        gb = g % 2
        return g, tl, gb, f"cg{gb}", {k: T[k][gb] for k in ("cQ", "cK", "cKt", "cV", "cG", "qxi", "kz")}

    def load_group(g):
        gb = g % 2
        gk = f"cg{gb}"
        B = {k: T[k][gb] for k in ("cQ", "cK", "cKt", "cV", "cG", "qxi", "kz")}
        d.load(P, B["cQ"][:], "oCqk", g, [gk + "q"], rows=(0, 64))
        d.load(P, B["cK"][:], "oCqk", g, [gk + "k"], rows=(64, 64))
        d.load(P, B["cKt"][:], "oCkt", g, [gk + "kt"], t=16)
        d.load(P, B["cV"][:], "oCv", g, [gk + "v"], t=16)
        d.load(P, B["cG"][:], "oCg", g, [gk + "g"], t=16)

    def derive_group(g):
        gb = g % 2
        gk = f"cg{gb}"
        B = {k: T[k][gb] for k in ("cQ", "cKt", "qxi", "kz")}
        P.op("dve", tt(B["qxi"][:].rearrange("p (t q) -> p t q", t=16), B["cQ"][:].rearrange("p (t q) -> p t q", t=16),
                       T["xi"][:].unsqueeze(1).to_broadcast([64, 16, 128]), ALU.mult),
             reads=[gk + "q", "cconst"], writes=[gk + "qxi"])
        P.op("dve", ts(B["kz"][:], B["cKt"][:], T["zeta"][:, 0:1], ALU.mult), reads=[gk + "kt", "cconst"], writes=[gk + "kz"])

    def s1(n):
        g, tl, gb, gk, B = grp(n)
        if tl == 2 and g + 1 < 4:
            load_group(g + 1)
        if tl == 11 and g + 1 < 4:
            derive_group(g + 1)
        o_lo = max(0, 4 - n)
        for hh in range(2):
            for o in range(o_lo, 5):
                kt = n - 4 + o
                P.op("pe", mm(Sb(hh)[:, o, :], bK[hh * 64:(hh + 1) * 64, kt * 128:(kt + 1) * 128],
                              bQ[hh * 64:(hh + 1) * 64, n * 128:(n + 1) * 128], True, True),
                     reads=["bQ", "bK"], writes=[f"ps{2 * hh}", f"ps{2 * hh + 1}"])
        cols = slice(tl * 128, (tl + 1) * 128)
        P.op("pe", mm(Sc, B["cK"][:, cols], B["cQ"][:, cols], True, True), reads=[gk + "q", gk + "k"], writes=["ps5"])
        P.op("pe", mm(kv, B["kz"][:, tl, :], B["cV"][:, tl, :], True, True), reads=[gk + "kz", gk + "v"], writes=["ps5"])

    def s2(n):
        o_lo = max(0, 4 - n)
        for hh in range(2):
            P.op("act", actf(hv(E, hh)[:, o_lo:5, :], Sb(hh)[:, o_lo:5, :], AF.Exp, scale=0.125),
                 reads=[f"ps{2 * hh}", f"ps{2 * hh + 1}"], writes=[f"E{hh}"])
            P.op("dve", tt(hv(PTb, hh)[:, o_lo:5, :], hv(E, hh)[:, o_lo:5, :], hv(EB, hh)[:, o_lo:5, :], ALU.mult),
                 reads=[f"E{hh}", "EB"], writes=[f"PTb{hh}"])
        PTc = T["PTc"][n % 2]
        P.op("dve", tt(PTc[:], Sc, T["DT"][:], ALU.mult), reads=["ps5", "cconst"], writes=[f"PTc{n % 2}"])
        P.op("dve", stt(state[:], state[:], T["cd"][0:64, 0:1], kv, ALU.mult, ALU.add),
             reads=["state", "ps5", "cconst"], writes=["state"])
        P.op("act", actf(state_bf[n % 2][:], state[:], AF.Copy), reads=["state"], writes=[f"state_bf{n % 2}"])

    def s3(n):
        g, tl, gb, gk, B = grp(n)
        o_lo = max(0, 4 - n)
        for hh in range(2):
            for o in range(o_lo, 5):
                kt = n - 4 + o
                P.op("pe", mm(Ob[:, hh * 65:(hh + 1) * 65], hv(PTb, hh)[:, o, :], bV[:, kt, hh, :], o == o_lo, o == 4),
                     reads=[f"PTb{hh}", "bV"], writes=["ps4"])
        cols = slice(tl * 128, (tl + 1) * 128)
        PTc = T["PTc"][n % 2]
        P.op("pe", mm(Oc, PTc[:], B["cV"][:, tl, :], True, False), reads=[f"PTc{n % 2}", gk + "v"], writes=["ps6"])
        P.op("pe", mm(Oc, B["qxi"][:, cols], state_bf[(n - 1) % 2][:], False, True),
             reads=[gk + "qxi", f"state_bf{(n - 1) % 2}"], writes=["ps6"])

    def s4(n):
        g, tl, gb, gk, B = grp(n)
        rcb = T["rcb"]
        P.op("dve", lambda e, o_=rcb[:], i_=Ob3[:, :, 64:65]: e.reciprocal(out=o_, in_=i_), reads=["ps4"], writes=["rcb"])
        yb = T["yb"][n % 2]
        P.op("dve", tt(yb[:].rearrange("p (h e) -> p h e", h=2), Ob3[:, :, 0:64], rcb[:].to_broadcast([128, 2, 64]), ALU.mult),
             reads=["ps4", "rcb"], writes=[f"yb{n % 2}"])
        ssc, lsc, rsc, junkc, tmpc = T["ssc"], T["lsc"], T["rsc"], T["junkc"], T["tmpc"]
        P.op("act", actf(junkc[:], Oc, AF.Square, accum_out=ssc[:]), reads=["ps6"], writes=["ssc", "junkc"])
        P.op("act", actf(lsc[:], ssc[:], AF.Ln, scale=1.0 / 128, bias=C.epsc[:]), reads=["ssc", "epsc"], writes=["lsc"])
        P.op("act", actf(rsc[:], lsc[:], AF.Exp, scale=-0.5), reads=["lsc"], writes=["rsc"])
        P.op("dve", stt(tmpc[:], Oc, rsc[:, 0:1], T["cnorm"][:], ALU.mult, ALU.mult),
             reads=["ps6", "rsc", "cconst"], writes=["tmpc"])
        yc = T["yc"][n % 2]
        P.op("pool", tt(yc[:], tmpc[:], B["cG"][:, tl, :], ALU.mult), reads=["tmpc", gk + "g"], writes=[f"yc{n % 2}"])

    def s5(n):
        ysb = T["ystBC"][(n // 4) % 2]
        ysk = f"ystBC{(n // 4) % 2}"
        emit_tr_out(C, T["yb"][n % 2][:], f"yb{n % 2}", ysb[:, 0, (n % 4) * 128:(n % 4 + 1) * 128], ysk)
        emit_tr_out(C, T["yc"][n % 2][:], f"yc{n % 2}", ysb[:, 1, (n % 4) * 128:(n % 4 + 1) * 128], ysk)
        if n % 4 == 3:
            d.store_bc(P, n // 4, ysb[:], ysk)

    load_group(0)
    derive_group(0)
    s1(0)
    s2(0)
    for n in range(NKT):
        if n + 1 < NKT:
            s1(n + 1)
        s3(n)
        if n + 1 < NKT:
            s2(n + 1)
        s4(n)
        if n >= 1:
            s5(n - 1)
    s5(NKT - 1)


def emit_mix_a(C, d, T, hooks=None):
    P = C.P
    aQ, aK, aV = T["aQ"], T["aK"], T["aV"]
    nlam = T["nlam"]
    steps = [(qb, kt) for qb in range(16) for kt in range(4 * qb + 4)]

    def geom(i):
        qb, kt = steps[i]
        dd = kt - 4 * qb
        q0 = max(dd, 0) * 128
        sbi = i % 2
        S = C.bank(2 * sbi, 2).rearrange("p (c n) -> p c n", c=2)
        return qb, kt, dd, q0, S, [f"ps{2 * sbi}", f"ps{2 * sbi + 1}"]

    def emit_s(i):
        qb, kt, dd, q0, S, skeys = geom(i)
        for c in range(2):
            P.op("pe", mm(S[:, c, q0:512], aK[c * 64:(c + 1) * 64, kt * 128:(kt + 1) * 128],
                          aQ[c * 64:(c + 1) * 64, qb * 512 + q0:(qb + 1) * 512], True, True),
                 reads=["aQ", "aK"], writes=[skeys[c]])

    emit_s(0)
    for i in range(len(steps)):
        qb, kt, dd, q0, S, skeys = geom(i)
        PT = T["PT"][i % 3]
        pk = f"pt{i % 3}"
        P.op("act", actf(PT[:, :, q0:512], S[:, :, q0:512], AF.Exp, scale=0.125), reads=skeys, writes=[pk])
        if dd >= 0:
            P.op("pool", lambda e, o_=PT[64:128, :, q0:q0 + 64]: e.memset(o_, 0.0), reads=[], writes=[pk])
        if i + 1 < len(steps):
            emit_s(i + 1)
        for c in range(2):
            for qt in range(max(dd, 0), 4):
                r = qt * 2 + c
                bk = 4 + r // 3
                off = (r % 3) * 129
                P.op("pe", mm(C.bank(bk)[:, off:off + 129], PT[:, c, qt * 128:(qt + 1) * 128], aV[:, kt, :],
                              kt == 0 and r in (0, 4, 6), kt == 4 * qb + qt),
                     reads=[pk, "aV"], writes=[f"ps{bk}"])
        if kt != 4 * qb + 3:
            continue
        yst = T["ystA"][qb % 2]
        ysk = f"ystA{qb % 2}"
        for qt in range(4):
            r0, r1 = qt * 2, qt * 2 + 1
            O0 = C.bank(4 + r0 // 3)[:, (r0 % 3) * 129:(r0 % 3) * 129 + 129]
            O1 = C.bank(4 + r1 // 3)[:, (r1 % 3) * 129:(r1 % 3) * 129 + 129]
            k0, k1 = f"ps{4 + r0 // 3}", f"ps{4 + r1 // 3}"
            rc, oa, ob = T["rc"], T["oa"], T["ob"]
            P.op("dve", lambda e, o_=rc[:, 0:1], i_=O0[:, 128:129]: e.reciprocal(out=o_, in_=i_), reads=[k0], writes=["rc0"])
            P.op("dve", lambda e, o_=rc[:, 1:2], i_=O1[:, 128:129]: e.reciprocal(out=o_, in_=i_), reads=[k1], writes=["rc1"])
            P.op("dve", tt(rc[:, 2:3], rc[:, 1:2], nlam[:], ALU.mult), reads=["rc1", "lam"], writes=["rc2"])
            P.op("dve", ts(oa[:], O0[:, 0:128], rc[:, 0:1], ALU.mult), reads=[k0, "rc0"], writes=["oa"])
            P.op("dve", stt(ob[:], O1[:, 0:128], rc[:, 2:3], oa[:], ALU.mult, ALU.add), reads=[k1, "rc2", "oa"], writes=["ob"])
            ssa, lsa, rsa, junka = T["ssa"], T["lsa"], T["rsa"], T["junka"]
            P.op("act", actf(junka[:], ob[:], AF.Square, accum_out=ssa[:]), reads=["ob"], writes=["ssa", "junka"])
            P.op("act", actf(lsa[:], ssa[:], AF.Ln, scale=1.0 / 128, bias=C.epsc[:]), reads=["ssa", "epsc"], writes=["lsa"])
            P.op("act", actf(rsa[:], lsa[:], AF.Exp, scale=-0.5), reads=["lsa"], writes=["rsa"])
            ya = T["ya"][qt % 2]
            P.op("dve", stt(ya[:], ob[:], rsa[:, 0:1], T["sublnS"][:], ALU.mult, ALU.mult),
                 reads=["ob", "rsa", "sublnS"], writes=[f"ya{qt % 2}"])
            emit_tr_out(C, ya[:], f"ya{qt % 2}", yst[:, qt * 128:(qt + 1) * 128], ysk)
        d.store_a(P, qb, yst[:], ysk)
        if hooks and qb in hooks:
            hooks[qb]()


def mix_consts(j, rel_bias, layer):
    gamma = np.float32(1.0 - 2.0 ** (-5.0 - j))
    lg = np.log(gamma).astype(np.float32)
    pos = np.arange(128, dtype=np.float32)
    diff = pos[None, :] - pos[:, None]
    DT = np.where(diff >= 0, np.exp(lg * np.maximum(diff, 0.0)), 0.0).astype(np.float32) * np.float32(0.125)
    zeta = (np.exp(lg * (127.0 - pos)) * 0.125).astype(np.float32).reshape(128, 1)
    xi = np.broadcast_to(np.exp(lg * (pos + 1.0)).astype(np.float32)[None, :], (64, 128))
    cd = np.full((128, 1), np.exp(lg * 128.0), np.float32)
    k = np.arange(128)[:, None]
    q = np.arange(128)[None, :]
    biasT = np.zeros((128, 10, 128), np.float32)
    maskB = np.zeros((128, 5, 128), np.float32)
    for o in range(5):
        rel = q - k + 128 * (4 - o)
        idx = np.clip(rel, -63, 256) + 63
        for hh in range(2):
            biasT[:, hh * 5 + o, :] = rel_bias[2 * j + hh][idx]
        dc = (q >= 64).astype(np.int64) - (k >= 64).astype(np.int64) + 2 * (4 - o)
        maskB[:, o, :] = ((dc >= 0) & (dc <= 8)).astype(np.float32)
    lam_init = 0.8 - 0.6 * math.exp(-0.3 * layer)
    li = np.zeros((128, 2), np.float32)
    li[:, 0] = lam_init
    li[:, 1] = 1.0 - lam_init
    return dict(DT=DT, zeta=zeta, xi=np.ascontiguousarray(xi), cd=cd, biasT=biasT, maskB=maskB, laminit=li)


def emit_merge(C, x, mT, wbr, gb_all, yt_all, wbr_d, wout_d, yt_load, g_d, scr, after_first_loads=None):
    P = C.P
    allw = [f"wb{i}" for i in range(6)]
    for m in range(3):
        P.op("pq", dma(wbr[:, m, :, :], wbr_d[m].rearrange("(j p) c -> p j c", p=128)), writes=allw)
    gb2 = [gb_all[:, 0:12, :], gb_all[:, 12:24, :]]
    yt2 = [yt_all[:, 0:12, :], yt_all[:, 12:24, :]]
    seq = [(tb, h) for tb in range(4) for h in range(2)]

    def issue_loads(idx):
        tb, h = seq[idx]
        cols = slice(tb * 512, (tb + 1) * 512)
        if h == 0:
            yt_load(P, yt2[tb % 2], tb, f"ytb{tb % 2}")
        for m in range(3):
            f0 = m * 8 + h * 4
            P.op("sp", dma(gb2[idx % 2][:, m * 4:(m + 1) * 4, :], g_d[f0:f0 + 4, :, cols].rearrange("f p n -> p f n")),
                 writes=[f"gbuf{idx % 2}"])

    issue_loads(0)
    if after_first_loads is not None:
        after_first_loads()
    it = 0
    for idx, (tb, h) in enumerate(seq):
        if idx + 1 < len(seq):
            issue_loads(idx + 1)
        cols = slice(tb * 512, (tb + 1) * 512)
        ytb, gbuf = yt2[tb % 2], gb2[idx % 2]
        yk, gk = f"ytb{tb % 2}", f"gbuf{idx % 2}"
        for fo4 in range(4):
            fo = h * 4 + fo4
            par = it % 2
            b0 = 3 * par
            it += 1
            t1, t2, sq = scr["t12"][par], scr["t22"][par], scr["sq2"][par]
            k1, k2, k3 = f"t1_{par}", f"t2a_{par}", f"sq_{par}"
            for m in range(3):
                for j in range(4):
                    P.op("pe", mm(C.bank(b0 + m), wbr[:, m, j, fo * 128:(fo + 1) * 128], ytb[:, j * 3 + m, :], j == 0, j == 3),
                         reads=allw + [yk], writes=[f"ps{b0 + m}"])
            P.op("dve", tt(t1, C.bank(b0), gbuf[:, fo4, :], ALU.mult), reads=[f"ps{b0}", gk], writes=[k1])
            P.op("dve", tt(t2, C.bank(b0 + 1), gbuf[:, 4 + fo4, :], ALU.mult), reads=[f"ps{b0 + 1}", gk], writes=[k2])
            P.op("dve", tt(sq, C.bank(b0 + 2), gbuf[:, 8 + fo4, :], ALU.mult), reads=[f"ps{b0 + 2}", gk], writes=[k3])
            P.op("pool", tt(t1, t1, t2, ALU.add), reads=[k1, k2], writes=[k1])
            P.op("pool", tt(mT[:, fo, cols], t1, sq, ALU.add), reads=[k1, k3],
                 writes=[f"hT{t}" for t in range(tb * 4, tb * 4 + 4)])
    wo = yt_all.rearrange("p f n -> p (f n)")[:, 0:8192].rearrange("p (k c) -> p k c", k=8)
    P.op("pq", dma(wo, wout_d.rearrange("(k p) c -> p k c", p=128)), writes=["ytb0", "ytb1"])
    for t in range(NT):
        b0 = 6 * (t % 2)
        py = C.bank(b0, 2)
        for n in range(2):
            for k in range(8):
                P.op("pe", mm(py[:, n * 512:(n + 1) * 512], mT[:, k, t * 128:(t + 1) * 128], wo[:, k, n * 512:(n + 1) * 512], k == 0, k == 7),
                     reads=["ytb0", "ytb1", f"hT{t}"], writes=[f"ps{b0 + n}"])
        P.op("dve", tt(x[:, t, :], x[:, t, :], py, ALU.add), reads=[f"ps{b0}", f"ps{b0 + 1}", f"x{t}"], writes=[f"x{t}"])


DEPTH = 2
GROUPS = [[0, 1, 2, 3], [4, 5, 6, 7]]
PW = 512
SEG = {"oAq": ("A", 0), "oAk": ("A", 2048), "oAv": ("A", 4096),
       "oBq": ("B", 0), "oBk": ("B", 2048), "oBv": ("B", 4096),
       "oCqk": ("C", 0), "oCv": ("C", 2048), "oCg": ("C", 4096), "oCkt": ("C", 6144)}
SEGW = {k: 2048 for k in SEG}
SEGW["oCkt"] = 1024
FAMW = {"A": 6144, "B": 6144, "C": 7168}
NPIECE = {f: w // PW for f, w in FAMW.items()}
_RANK = {}


def _rank(e):
    k = id(e)
    if k not in _RANK:
        _RANK[k] = e.snap(e.partition_id() % 4, min_val=0, max_val=3)
    return _RANK[k]


def _gather(P, snd_ap, rcv_ap, reads, wkey):
    P.op("cc", lambda e: e.collective_compute("AllGather", ALU.bypass, replica_groups=GROUPS,
                                              ins=[snd_ap.opt()], outs=[rcv_ap.opt()]), reads=reads, writes=[wkey])


class PreIO:
    def __init__(self, snd, rcv, gsc):
        self.snd, self.rcv = snd, rcv
        self.g = gsc.ap()
        self.deferred = []
        self.early = None

    def gate(self, g):
        return self.g[g * 4:(g + 1) * 4].rearrange("c p n -> p c n")

    def piece(self, f, k):
        return self.snd[f].ap()[k].rearrange("(j p) w -> p j w", p=128)

    def store_block(self, P, blk, kind, stg, sk, kstage):
        f, off = SEG[PROJ_OUT[blk]]
        k0 = off // PW
        later = []
        for k in range(4):
            cs = slice(k * PW, (k + 1) * PW)
            dv = self.piece(f, k0 + k)
            keys = []
            if kind == "R":
                for j in range(4):
                    h2 = slice((j % 2) * 64, (j % 2 + 1) * 64)
                    keys += [f"s{f}{k0 + k}q{j}", f"s{f}{k0 + k}k{j}"]
                    P.op("sp", dma(dv[0:64, j, :], stg[h2, j // 2, cs]), reads=[sk], writes=[keys[-2]])
                    P.op("sp", dma(dv[64:128, j, :], stg[h2, 2 + j // 2, cs]), reads=[sk], writes=[keys[-1]])
            else:
                keys = [f"s{f}{k0 + k}"]
                P.op("sp", dma(dv, stg[:, :, cs]), reads=[sk], writes=keys)
            later.append(lambda extra=(), kk=k0 + k, keys=keys: _gather(P, self.snd[f].ap()[kk], self.rcv[f].ap()[kk], list(keys) + list(extra), f"rcv{f}{kk}"))
        if kind == "R":
            f2, off2 = SEG["oCkt"]
            for k in range(2):
                kk = off2 // PW + k
                P.op("sp", dma(self.piece(f2, kk), kstage[:, :, k * PW:(k + 1) * PW]), reads=["kstage"], writes=[f"s{f2}{kk}"])
                later.append(lambda extra=(), kk=kk: _gather(P, self.snd[f2].ap()[kk], self.rcv[f2].ap()[kk], [f"s{f2}{kk}"] + list(extra), f"rcv{f2}{kk}"))
        if f == "C":
            self.deferred += later
            return []
        return later


class MixIO:
    def __init__(self, rcv, mine, snd2, rcv2):
        self.r = {f: t.ap() for f, t in rcv.items()}
        self.m = {f: t.ap() for f, t in mine.items()}
        self.snd2, self.rcv2 = snd2, rcv2
        self.pend2 = []

    def flush2(self):
        for fn in self.pend2:
            fn()
        self.pend2 = []

    def fetch(self, P, f):
        def fn(e):
            src = self.r[f].rearrange("k (i q) w -> q k i w", q=512)[bass.ds(_rank(e) * 128, 128), :, :, :]
            return e.dma_start(out=self.m[f], in_=src)
        P.op("sp", fn, reads=[f"rcv{f}{k}" for k in range(NPIECE[f])], writes=["mine" + f])

    def key(self, name):
        return "mine" + SEG[name][0]

    def load(self, P, dst, name, i, writes, rows=(0, 128), t=None, he=None):
        (f, off), w = SEG[name], SEGW[name]
        for kk in range(w // PW):
            src = self.m[f][rows[0]:rows[0] + rows[1], off // PW + kk, i, :]
            if he is not None:
                h, e_, hh = he
                a = PW // (h * e_)
                src = src.rearrange("p (a h e) -> p a h e", h=h, e=e_)[:, :, hh, :]
                dv = dst[:, kk * a:(kk + 1) * a, :]
            elif t is not None:
                dd = w // t
                a = PW // dd
                src = src.rearrange("p (a d) -> p a d", d=dd)
                dv = dst[:, kk * a:(kk + 1) * a, :]
            else:
                dv = dst[:, kk * PW:(kk + 1) * PW]
            P.op("sp", dma(dv, src), reads=["mine" + f], writes=writes)

    def store_a(self, P, qb, yst, ysk):
        k = qb // 4
        P.op("sp", dma(self.snd2["A"].ap()[k][:, (qb % 4) * 512:(qb % 4 + 1) * 512], yst), reads=[ysk], writes=[f"s2A{k}_{qb % 4}"])
        self.flush2()
        if qb % 4 == 3:
            self.pend2.append(lambda k=k: _gather(P, self.snd2["A"].ap()[k], self.rcv2["A"].ap()[k], [f"s2A{k}_{i}" for i in range(4)], f"rcv2A{k}"))

    def store_bc(self, P, q4, ysb, ysk):
        k = q4 // 2
        dv = self.snd2["BC"].ap()[k].rearrange("(m p) n -> p m n", p=128)
        P.op("sp", dma(dv[:, :, (q4 % 2) * 512:(q4 % 2 + 1) * 512], ysb), reads=[ysk], writes=[f"s2BC{k}_{q4 % 2}"])
        self.flush2()
        if q4 % 2 == 1:
            self.pend2.append(lambda k=k: _gather(P, self.snd2["BC"].ap()[k], self.rcv2["BC"].ap()[k], [f"s2BC{k}_{i}" for i in range(2)], f"rcv2BC{k}"))


MIXC = {"DT": ([128, 128], F32), "zeta": ([128, 1], F32), "xi": ([64, 128], F32), "cd": ([128, 1], F32),
        "maskB": ([128, 5, 128], F32)}
MIXL = {"biasT": ([128, 10, 128], F32), "laminit": ([128, 2], F32), "lamv": ([4, 64], F32),
        "a_subln": ([128], F32), "cnorm": ([128], F32)}
WNAMES = {"ffn1_norm": [DM], "ffn1_w_gate": [DM, DFF], "ffn1_w_up": [DM, DFF], "ffn1_w_down": [DFF, DM],
          "mix_norm": [DM], "w_in": [DM, IN_COLS], "a_q_norm": [64], "a_k_norm": [64], "b_q_norm": [64], "b_k_norm": [64],
          "w_branch_a": [512, DM], "w_branch_b": [512, DM], "w_branch_c": [512, DM], "w_out": [DM, DM],
          "ffn2_norm": [DM], "ffn2_w_gate": [DM, DFF], "ffn2_w_up": [DM, DFF], "ffn2_w_down": [DFF, DM]}


def emit_mix_setup(C, d, T, cst, l):
    P = C.P
    for n, sh in (("cQ", [64, TOK]), ("cK", [64, TOK]), ("cKt", [128, 16, 64]), ("cV", [128, 16, 128]),
                  ("cG", [128, 16, 128]), ("qxi", [64, TOK]), ("kz", [128, 16, 64])):
        T[n] = [C.sb(f"{n}{i}", sh, BF16) for i in range(2)]
    T["DT"] = C.sb("DT", [128, 128], F32)
    T["zeta"] = C.sb("zeta", [128, 1], F32)
    T["xi"] = C.sb("xi", [64, 128], F32)
    T["cd"] = C.sb("cd", [128, 1], F32)
    T["cnorm"] = C.sb("cnorm", [128, 128], F32)
    for n in ("DT", "zeta", "xi", "cd"):
        P.op("sp", dma(T[n], cst[n]), writes=["cconst"])
    P.op("sp", dma_nc(T["cnorm"], cst["cnorm"][l].partition_broadcast(128)), writes=["cconst"])
    bT = C.sb("biasT", [128, 10, 128], F32)
    mB = C.sb("maskB", [128, 5, 128], F32)
    T["EB"] = C.sb("EB", [128, 1280], BF16)
    P.op("sp", dma(bT, cst["biasT"][l]), writes=["bT"])
    P.op("sp", dma(mB, cst["maskB"]), writes=["mB"])
    P.op("act", actf(bT, bT, AF.Exp), reads=["bT"], writes=["bT"])
    P.op("dve", tt(T["EB"].rearrange("p (h o q) -> p h o q", h=2, o=5), bT.rearrange("p (h o) q -> p h o q", h=2),
                   mB.unsqueeze(1).to_broadcast([128, 2, 5, 128]), ALU.mult), reads=["bT", "mB"], writes=["EB"])
    lv = C.sb("lamv", [128, 4, 64], F32)
    P.op("sp", dma_nc(lv, cst["lamv"][l].partition_broadcast(128)), writes=["lv"])
    li = C.sb("laminit", [128, 2], F32)
    P.op("sp", dma(li, cst["laminit"][l]), writes=["li"])
    lp = C.sb("lamp", [128, 2, 64], F32)
    ls = C.sb("lams", [128, 4], F32)
    lv4 = lv.rearrange("p (a b) d -> p a b d", a=2)
    P.op("dve", tt(lp, lv4[:, :, 0, :], lv4[:, :, 1, :], ALU.mult), reads=["lv"], writes=["lp"])
    P.op("dve", lambda e: e.tensor_reduce(out=ls[:, 0:2], in_=lp, axis=AX.X, op=ALU.add), reads=["lp"], writes=["ls01"])
    P.op("act", actf(ls[:, 0:2], ls[:, 0:2], AF.Exp), reads=["ls01"], writes=["ls01"])
    P.op("dve", tt(ls[:, 2:3], ls[:, 0:1], ls[:, 1:2], ALU.subtract), reads=["ls01"], writes=["ls2"])
    P.op("dve", tt(ls[:, 3:4], ls[:, 2:3], li[:, 0:1], ALU.add), reads=["ls2", "li"], writes=["ls3"])
    T["nlam"] = C.sb("nlam", [128, 1], F32)
    P.op("dve", ts(T["nlam"], ls[:, 3:4], -1.0, ALU.mult), reads=["ls3"], writes=["lam"])
    sub = C.sb("subln", [128, 128], F32)
    P.op("sp", dma_nc(sub, cst["a_subln"][l].partition_broadcast(128)), writes=["sub"])
    T["sublnS"] = C.sb("sublnS", [128, 128], F32)
    P.op("dve", ts(T["sublnS"], sub, li[:, 1:2], ALU.mult), reads=["sub", "li"], writes=["sublnS"])
    T["E"] = C.sb("E", [128, 1280], BF16)
    T["PTb"] = C.sb("PTb", [128, 1280], BF16)
    T["rcb"] = C.sb("rcb", [128, 2, 1], F32)
    for n in ("yb", "yc", "ya", "PTc"):
        T[n] = [C.sb(f"{n}{i}", [128, 128], BF16) for i in range(2)]
    T["ystBC"] = [C.sb(f"ystBC{i}", [128, 2, 512], BF16) for i in range(2)]
    T["ystA"] = [C.sb(f"ystA{i}", [128, 512], BF16) for i in range(2)]
    T["state"] = C.sb("state", [64, 128], F32)
    T["state_bf"] = [C.sb(f"state_bf{i}", [64, 128], BF16) for i in range(2)]
    for n in ("ssc", "lsc", "rsc", "ssa", "lsa", "rsa"):
        T[n] = C.sb(n, [128, 1], F32)
    for n in ("junkc", "tmpc", "junka", "oa", "ob"):
        T[n] = C.sb(n, [128, 128], F32)
    T["rc"] = C.sb("rc", [128, 3], F32)
    T["PT"] = [C.sb(f"PT{i}", [128, 2, 512], BF16) for i in range(3)]


def emit_mix_setup_b(C, d, T):
    P = C.P
    d.fetch(P, "B")
    for i in range(4):
        for n, src in (("bQ", "oBq"), ("bK", "oBk")):
            d.load(P, T[n][:, i * TOK:(i + 1) * TOK], src, i, [n])
        for hh in range(2):
            d.load(P, T["bV"][:, i * 16:(i + 1) * 16, hh, 0:64], "oBv", i, ["bV"], he=(2, 64, hh))


HEADW = 1152


def emit_pre_tok(C, x, hT, wbuf, stage, w_in_d, mixnorm_d, snd_h, rcv_h, gsc, scr):
    P = C.P
    emit_rmsnorm_hT(C, x, hT, mixnorm_d, "m", scr)

    def wview(bi):
        return wbuf[bi % 2][:, 0:4096].rearrange("p (k c) -> p k c", k=8)

    def wkeys(bi):
        return [f"wb{3 * (bi % 2)}", f"wb{3 * (bi % 2) + 1}"]

    def issue_w(g):
        if g < 6:
            blk = 9 + g
            P.op("pq", dma(wview(g), w_in_d[:, blk * 512:(blk + 1) * 512].rearrange("(k p) c -> p k c", p=128)), writes=wkeys(g))

    issue_w(0)
    issue_w(1)
    for p in range(8):
        P.op("sp", dma(snd_h.ap()[p].rearrange("q (k n) -> q k n", k=8), hT[:, :, p * 256:(p + 1) * 256]),
             reads=[f"hT{2 * p}", f"hT{2 * p + 1}"], writes=[f"sndh{p}"])
    for p in range(8):
        _gather(P, snd_h.ap()[p], rcv_h.ap()[p], [f"sndh{p}"], f"rcvh{p}")
    inst = 0
    for g in range(6):
        if g >= 1:
            issue_w(g + 1)
        wv = wview(g)
        stg = stage[g % 2]
        sk = f"stage{g % 2}"
        for f in range(4):
            for tb in range(4):
                pb = inst % 2
                inst += 1
                ps = C.bank(pb)
                for k in range(8):
                    P.op("pe", mm(ps, wv[:, k, f * 128:(f + 1) * 128], hT[:, k, tb * 512:(tb + 1) * 512], k == 0, k == 7),
                         reads=wkeys(g) + [f"hT{t}" for t in range(tb * 4, tb * 4 + 4)], writes=[f"ps{pb}"])
                P.op("act", actf(stg[:, f, tb * 512:(tb + 1) * 512], ps, AF.Sigmoid), reads=[f"ps{pb}"], writes=[sk])
        P.op("sp", dma(gsc.ap()[g * 4:(g + 1) * 4].rearrange("c p n -> p c n"), stg[:]), reads=[sk])


def emit_head_proj(C, T, rcv_h, mineC, wh_d, cos_d, sin_d, gains_d, l):
    P = C.P
    QK4 = T["QK4"]
    Wh = C.sb("Wh", [128, 8, HEADW], BF16)
    hTt = [C.sb(f"hTt{i}", [128, 8, 512], BF16) for i in range(2)]
    cs = [(C.sb(f"cosg{i}", [128, 16, 64], F32), C.sb(f"sing{i}", [128, 16, 64], F32)) for i in range(2)]
    G8 = C.sb("G8", [128, 8, 64], F32)
    scr = {}
    ND = 2
    for n in ("sq", "xq", "xg", "t1", "t2"):
        scr[n] = [C.sb(f"h{n}{i}", [128, 512], F32) for i in range(ND)]
    res = [C.sb(f"hres{i}", [128, 512], BF16) for i in range(ND)]
    resz = [C.sb(f"hresz{i}", [128, 128], BF16) for i in range(ND)]
    zt = [[C.sb(f"hz{n}{i}", [128, 128], F32) for i in range(ND)] for n in ("a", "b")]
    s8 = C.sb("hs8", [128, 8 * ND], F32)
    l8 = C.sb("hl8", [128, 8 * ND], F32)
    r8 = C.sb("hr8", [128, 8 * ND], F32)
    eg = [C.sb(f"heg{i}", [128, 128], F32) for i in range(ND)]
    cQK2 = [C.sb(f"cQKst{i}", [128, 2048], BF16) for i in range(2)]
    cVs2 = [C.sb(f"cVst{i}", [128, 16, 128], BF16) for i in range(2)]
    cGs2 = [C.sb(f"cGst{i}", [128, 16, 128], BF16) for i in range(2)]
    cKs2 = [C.sb(f"cKst{i}", [128, 16, 64], BF16) for i in range(2)]
    P.op("pq", dma(Wh, wh_d.rearrange("(k p) c -> p k c", p=128)), writes=["Wh"])
    for i in range(4):
        for r in range(2):
            P.op("sp", dma_nc(G8[:, 2 * i + r, :], gains_d[i].partition_broadcast(128)), writes=["G8"])
    rh = rcv_h.ap().rearrange("t (i p) (k n) -> t i p k n", p=128, k=8)

    def v3(ap, h=8):
        return ap.rearrange("p (h d) -> p h d", h=h)

    def mm_stage(n):
        i, tl = n // 16, n % 16
        par = n % 2
        if n % 4 == 0:
            hb = (n // 4) % 2
            for half in range(2):
                pc = 2 * (tl // 4) + half
                P.op("sp", dma(hTt[hb][:, :, half * 256:(half + 1) * 256], rh[pc][i]),
                     reads=[f"rcvh{pc}"], writes=[f"hTt{hb}_{half}"])
        hb = (n // 4) % 2
        hcols = slice((n % 4) * 128, (n % 4 + 1) * 128)
        pX, pY, pZ = C.bank(par), C.bank(2 + par), C.bank(4 + par)[:, 0:128]
        for (ps, c0, c1, key) in ((pX, 0, 512, f"ps{par}"), (pY, 512, 1024, f"ps{2 + par}"), (pZ, 1024, 1152, f"ps{4 + par}")):
            for k in range(8):
                P.op("pe", mm(ps, hTt[hb][:, k, hcols], Wh[:, k, c0:c1], k == 0, k == 7),
                     reads=[f"hTt{hb}_{(n % 4) // 2}", "Wh"], writes=[key])

    def load_rope(i):
        cosg, sing = cs[i % 2]
        P.op("sp", dma(cosg, cos_d[i]), writes=[f"cs{i % 2}"])
        P.op("sp", dma(sing, sin_d[i]), writes=[f"cs{i % 2}"])

    load_rope(0)
    mm_stage(0)
    for n in range(NKT):
        i, tl = n // 16, n % 16
        par = n % 2
        if tl == 8 and i + 1 < 4:
            load_rope(i + 1)
        if n + 1 < NKT:
            mm_stage(n + 1)
        cosg, sing = cs[i % 2]
        csk = f"cs{i % 2}"
        cQK, cVs, cGs, cKs = cQK2[i % 2], cVs2[i % 2], cGs2[i % 2], cKs2[i % 2]
        stk = f"_{i % 2}"
        pX, pY, pZ = C.bank(par), C.bank(2 + par), C.bank(4 + par)[:, 0:128]
        sd = n % ND
        sq, xq, xg, t1, t2 = (scr[m][sd] for m in ("sq", "xq", "xg", "t1", "t2"))
        s8p, l8p, r8p = s8[:, sd * 8:sd * 8 + 8], l8[:, sd * 8:sd * 8 + 8], r8[:, sd * 8:sd * 8 + 8]
        kq = lambda m: f"h{m}_{sd}"
        P.op("act", actf(sq, pX, AF.Square), reads=[f"ps{par}"], writes=[kq("sq")])
        P.op("dve", lambda e, o=s8p, i_=v3(sq): e.tensor_reduce(out=o, in_=i_, axis=AX.X, op=ALU.add), reads=[kq("sq")], writes=[kq("s8")])
        P.op("act", actf(l8p, s8p, AF.Ln, scale=1.0 / 64, bias=C.epsc[:]), reads=[kq("s8"), "epsc"], writes=[kq("l8")])
        P.op("act", actf(r8p, l8p, AF.Exp, scale=-0.5), reads=[kq("l8")], writes=[kq("r8")])
        P.op("dve", tt(v3(xq), v3(pX), r8p.unsqueeze(2).to_broadcast([128, 8, 64]), ALU.mult), reads=[f"ps{par}", kq("r8")], writes=[kq("xq")])
        rs = res[sd]
        rk = f"hres{sd}"
        P.op("dve", tt(v3(xg)[:, 0:4, :], v3(xq)[:, 0:4, :], G8[:, 0:4, :], ALU.mult), reads=[kq("xq"), "G8"], writes=[kq("xg")])
        P.op("dve", tt(v3(rs)[:, 4:8, :], v3(xq)[:, 4:8, :], G8[:, 4:8, :], ALU.mult), reads=[kq("xq"), "G8"], writes=[rk + "b"])
        xa = v3(xg)[:, 0:4, :]
        cb = cosg[:, tl, :].unsqueeze(1).to_broadcast([128, 4, 64])
        sa = sing[:, tl, 0:32].unsqueeze(1).to_broadcast([128, 4, 32])
        sb_ = sing[:, tl, 32:64].unsqueeze(1).to_broadcast([128, 4, 32])
        P.op("dve", tt(v3(t1)[:, 0:4, :], xa, cb, ALU.mult), reads=[kq("xg"), csk], writes=[kq("t1")])
        P.op("pool", tt(v3(t2)[:, 0:4, 0:32], xa[:, :, 32:64], sa, ALU.mult), reads=[kq("xg"), csk], writes=[kq("t2a")])
        P.op("pool", tt(v3(t2)[:, 0:4, 32:64], xa[:, :, 0:32], sb_, ALU.mult), reads=[kq("xg"), csk], writes=[kq("t2b")])
        P.op("dve", tt(rs[:, 0:256], t1[:, 0:256], t2[:, 0:256], ALU.add), reads=[kq("t1"), kq("t2a"), kq("t2b")], writes=[rk + "a"])
        za, zb, rz = zt[0][sd], zt[1][sd], resz[sd]
        zk = f"hz{sd}"
        cb2 = cosg[:, tl, :].unsqueeze(1).to_broadcast([128, 2, 64])
        sa2 = sing[:, tl, 0:32].unsqueeze(1).to_broadcast([128, 2, 32])
        sb2 = sing[:, tl, 32:64].unsqueeze(1).to_broadcast([128, 2, 32])
        pz3 = v3(pZ, 2)
        P.op("dve", tt(v3(za, 2), pz3, cb2, ALU.mult), reads=[f"ps{4 + par}", csk], writes=[zk + "a"])
        P.op("dve", tt(v3(zb, 2)[:, :, 0:32], pz3[:, :, 32:64], sa2, ALU.mult), reads=[f"ps{4 + par}", csk], writes=[zk + "b0"])
        P.op("dve", tt(v3(zb, 2)[:, :, 32:64], pz3[:, :, 0:32], sb2, ALU.mult), reads=[f"ps{4 + par}", csk], writes=[zk + "b1"])
        P.op("dve", tt(rz, za, zb, ALU.add), reads=[zk + "a", zk + "b0", zk + "b1"], writes=[zk + "r"])
        P.op("pool", lambda e, o=cKs[:, tl, :], i_=rz[:, 64:128]: e.tensor_copy(out=o, in_=i_), reads=[zk + "r"], writes=["cKst" + stk])
        P.op("act", actf(T["aV"][:, n, 0:128], pY[:, 0:128], AF.Copy), reads=[f"ps{2 + par}"], writes=["aV"])
        P.op("act", actf(T["bV"][:, n, :, 0:64], pY[:, 128:256].rearrange("p (h e) -> p h e", h=2), AF.Copy),
             reads=[f"ps{2 + par}"], writes=["bV"])
        P.op("act", actf(cVs[:, tl, :], pY[:, 256:384], AF.Copy), reads=[f"ps{2 + par}"], writes=["cVst" + stk])
        P.op("act", actf(eg[sd], pY[:, 384:512], AF.Exp, scale=-1.0), reads=[f"ps{2 + par}"], writes=[f"heg{sd}"])
        P.op("dve", ts(eg[sd], eg[sd], 1.0, ALU.add), reads=[f"heg{sd}"], writes=[f"heg{sd}"])
        P.op("dve", lambda e, o=eg[sd], i_=eg[sd]: e.reciprocal(out=o, in_=i_), reads=[f"heg{sd}"], writes=[f"heg{sd}"])
        P.op("dve", tt(cGs[:, tl, :], pY[:, 384:512], eg[sd], ALU.mult), reads=[f"ps{2 + par}", f"heg{sd}"], writes=["cGst" + stk])
        tb = 6 + par
        pst = C.bank_bf(tb)[:, 0:640]
        for c in range(4):
            P.op("pe", tr(pst[:, c * 128:(c + 1) * 128], rs[:, c * 128:(c + 1) * 128], C.ident[:]),
                 reads=[rk + "a", rk + "b", "ident"], writes=[f"ps{tb}"])
        P.op("pe", tr(pst[:, 512:640], rz, C.ident[:]), reads=[zk + "r", "ident"], writes=[f"ps{tb}"])
        P.op("act", actf(QK4[:, :, n * 128:(n + 1) * 128], pst[:, 0:512].rearrange("p (c q) -> p c q", c=4), AF.Copy),
             reads=[f"ps{tb}"], writes=["QK4"])
        P.op("act", actf(cQK[:, tl * 128:(tl + 1) * 128], pst[:, 512:640], AF.Copy), reads=[f"ps{tb}"], writes=["cQKst" + stk])
        if tl == 15:
            mc = mineC.ap()
            P.op("sp", dma(mc[:, 0:4, i, :], cQK.rearrange("p (k w) -> p k w", w=PW)), reads=["cQKst" + stk], writes=[f"mineCq{i}"])
            P.op("sp", dma(mc[:, 4:8, i, :], cVs.rearrange("p t d -> p (t d)").rearrange("p (k w) -> p k w", w=PW)), reads=["cVst" + stk], writes=[f"mineCv{i}"])
            P.op("sp", dma(mc[:, 8:12, i, :], cGs.rearrange("p t d -> p (t d)").rearrange("p (k w) -> p k w", w=PW)), reads=["cGst" + stk], writes=[f"mineCg{i}"])
            P.op("sp", dma(mc[:, 12:14, i, :], cKs.rearrange("p t d -> p (t d)").rearrange("p (k w) -> p k w", w=PW)), reads=["cKst" + stk], writes=[f"mineCk{i}"])


def build_fused():
    nc = bass.Bass("TRN2", target_bir_lowering=False)

    def din(name, shape, dt=F32):
        return nc.dram_tensor(name, shape, dt, kind="ExternalInput").ap()

    x_d = din("x", [TOK, DM])
    Wd = {n: din(n, [DEPTH] + sh) for n, sh in WNAMES.items()}
    wh_d = din("w_head", [DEPTH, DM, HEADW])
    cos_d, sin_d = din("cos_all", [4, 128, NT, 64]), din("sin_all", [4, 128, NT, 64])
    idn = din("ident", [128, 128], BF16)
    cst = {n: din(n, sh, dt) for n, (sh, dt) in MIXC.items()}
    cst.update({n: din(n, [DEPTH] + sh, dt) for n, (sh, dt) in MIXL.items()})
    out_d = nc.dram_tensor("x_out", [TOK, DM], F32, kind="ExternalOutput").ap()
    snd_h = nc.dram_tensor("snd_h", [8, 128, TOK], BF16)
    rcv_h = nc.dram_tensor("rcv_h", [8, 4 * 128, TOK], BF16)
    mineC = nc.dram_tensor("mineC", [128, NPIECE["C"], 4, PW], BF16)
    snd2 = {"A": nc.dram_tensor("snd2A", [4, 128, TOK], BF16), "BC": nc.dram_tensor("snd2BC", [8, 256, 1024], BF16)}
    rcv2 = {"A": nc.dram_tensor("rcv2A", [4, 4 * 128, TOK], BF16), "BC": nc.dram_tensor("rcv2BC", [8, 4 * 256, 1024], BF16)}
    mine2 = {"A": nc.dram_tensor("mine2A", [4 * 128, TOK], BF16), "BC": nc.dram_tensor("mine2BC", [2, 4 * 256, 1024], BF16)}
    gsc = nc.dram_tensor("gsc", [24, 128, TOK], BF16)
    xs = nc.dram_tensor("xs", [TOK, DM], F32)
    with contextlib.ExitStack() as st:
        C = Ctx(nc, st)
        P = C.P
        P.op("sp", dma(C.ident, idn), writes=["ident"])
        base = C.mark()

        def token_layout():
            C.reset(base)
            L = {}
            L["x"] = C.sb("x", [128, NT, DM], F32)
            L["hT"] = C.sb("hT", [128, 8, TOK], BF16)
            wball = C.sb("wball", [128, 2 * 6144], BF16)
            L["wball"] = wball
            L["wbuf"] = [wball[:, 0:6144], wball[:, 6144:12288]]
            L["wbr"] = wball.rearrange("p (m j c) -> p m j c", m=3, j=4)
            u = C.mark()
            L["stage"] = [C.sb(f"stage{i}", [128, 4, TOK], BF16) for i in range(2)]
            L["kstage"] = C.sb("kstage", [128, 4, 1024], BF16)
            e1 = C.mark()
            C.reset(u)
            L["gbuf"] = C.sb("gbuf", [128, 24, 512], BF16)
            L["ytb"] = C.sb("ytb", [128, 24, 512], BF16)
            C.reset(max(e1, C.mark()))
            L["scr"] = make_scr(C)
            return L

        mix_io = MixIO({}, {"C": mineC}, snd2, rcv2)

        def yt_fetch_a(e):
            return e.dma_start(out=mine2["A"].ap(), in_=rcv2["A"].ap()[bass.ds(_rank(e), 1), :, :].rearrange("o r n -> (o r) n"))

        def yt_fetch_bc(e):
            return e.dma_start(out=mine2["BC"].ap(), in_=rcv2["BC"].ap()[bass.ds(_rank(e) * 2, 2), :, :])

        def yt_load(P, ytb, tb, key):
            y4 = ytb.rearrange("p (j m) n -> p j m n", m=3)
            P.op("sp", dma(y4[:, :, 0, :], mine2["A"].ap()[:, tb * 512:(tb + 1) * 512].rearrange("(j p) n -> p j n", p=128)),
                 reads=["mine2A"], writes=[key])
            for mm_ in range(2):
                P.op("sp", dma(y4[:, :, 1 + mm_, :], mine2["BC"].ap()[tb // 2][:, (tb % 2) * 512:(tb % 2 + 1) * 512]
                               .rearrange("(j m p) n -> p j m n", m=2, p=128)[:, :, mm_, :]), reads=["mine2BC"], writes=[key])

        for l in range(DEPTH):
            L = token_layout()
            x, hT, scr = L["x"], L["hT"], L["scr"]
            if l == 0:
                load_x(C, x, x_d)
            aT2 = [L["stage"][0][:, 0:2, :], L["stage"][0][:, 2:4, :]]
            wdb = [L["kstage"][:, 0:2, :].rearrange("p a n -> p (a n)"), L["kstage"][:, 2:4, :].rearrange("p a n -> p (a n)")]
            emit_ffn(C, x, hT, L["wball"], wdb, aT2, Wd["ffn1_w_gate"][l], Wd["ffn1_w_up"][l], Wd["ffn1_w_down"][l],
                     Wd["ffn1_norm"][l], scr, aT_keys=["stage0"], wd_keys=["kstage"])
            store_x(C, x, xs.ap())
            emit_pre_tok(C, x, hT, L["wbuf"], L["stage"], Wd["w_in"][l], Wd["mix_norm"][l], snd_h, rcv_h, gsc, scr)
            P.barrier()
            C.reset(base)
            T = {}
            T["QK4"] = C.sb("QK4", [128, 4, SEQ], BF16)
            for i, n in enumerate(("aQ", "aK", "bQ", "bK")):
                T[n] = T["QK4"][:, i, :]
            T["aV"] = C.sb("aV", [128, NKT, 129], BF16)
            T["bV"] = C.sb("bV", [128, NKT, 2, 65], BF16)
            P.op("pool", lambda e, T=T: e.memset(T["aV"][:, :, 128:129], 1.0), writes=["aV"])
            P.op("pool", lambda e, T=T: e.memset(T["bV"][:, :, :, 64:65], 1.0), writes=["bV"])
            m1 = C.mark()
            emit_head_proj(C, T, rcv_h, mineC, wh_d[l], cos_d, sin_d,
                           [Wd[n][l] for n in ("a_q_norm", "a_k_norm", "b_q_norm", "b_k_norm")], l)
            P.barrier()
            C.reset(m1)
            emit_mix_setup(C, mix_io, T, cst, l)
            emit_mix_a(C, mix_io, T)
            emit_mix_bc(C, mix_io, T)
            mix_io.flush2()
            P.barrier()
            P.op("sp", yt_fetch_a, reads=[f"rcv2A{k}" for k in range(4)], writes=["mine2A"])
            P.op("sp", yt_fetch_bc, reads=[f"rcv2BC{k}" for k in range(8)], writes=["mine2BC"])
            L = token_layout()
            x, hT, scr = L["x"], L["hT"], L["scr"]
            emit_merge(C, x, hT, L["wbr"], L["gbuf"], L["ytb"],
                       [Wd["w_branch_a"][l], Wd["w_branch_b"][l], Wd["w_branch_c"][l]], Wd["w_out"][l],
                       yt_load, gsc.ap(), scr, after_first_loads=lambda x=x: load_x(C, x, xs.ap()))
            gflat = L["gbuf"].rearrange("p a n -> p (a n)")
            aT2 = [gflat[:, 0:4096].rearrange("p (f n) -> p f n", f=2), gflat[:, 4096:8192].rearrange("p (f n) -> p f n", f=2)]
            yflat = L["ytb"].rearrange("p a n -> p (a n)")
            wdb = [yflat[:, 8192:10240], yflat[:, 10240:12288]]
            emit_ffn(C, x, hT, L["wball"], wdb, aT2, Wd["ffn2_w_gate"][l], Wd["ffn2_w_up"][l], Wd["ffn2_w_down"][l],
                     Wd["ffn2_norm"][l], scr, aT_keys=["gbuf0", "gbuf1"], wd_keys=["ytb1"])
            if l < DEPTH - 1:
                P.barrier()
        store_x(C, x, out_d)
        P.emit()
    return nc


_BF = ml_dtypes.bfloat16
_PROG = []


def head_cols(j):
    r = lambda a, n: list(range(a, a + n))
    return (r(j * 128, 128) + r(512 + j * 128, 128) + r(1536 + j * 128, 128) + r(2048 + j * 128, 128)
            + r(1024 + j * 128, 128) + r(2560 + j * 128, 128) + r(3584 + j * 128, 128) + r(4096 + j * 128, 128)
            + r(3072 + j * 64, 64) + r(3328 + j * 64, 64))


def kernel(**inp):
    x = np.asarray(inp["x"], np.float32)
    if not _PROG:
        _RANK.clear()
        _PROG.append(build_fused())
    nc = _PROG[0]
    ident = np.eye(128, dtype=_BF)
    W = {n: np.ascontiguousarray(np.asarray(inp[n], np.float32)) for n in WNAMES}
    lamv = np.ascontiguousarray(np.stack([inp["a_lambda_q1"], inp["a_lambda_k1"], inp["a_lambda_q2"], inp["a_lambda_k2"]], axis=1)).astype(np.float32)
    tabs = [rope_tables(i) for i in range(4)]
    cos_all = np.ascontiguousarray(np.stack([t[0] for t in tabs]))
    sin_all = np.ascontiguousarray(np.stack([t[1] for t in tabs]))
    maps = []
    for c in range(NCORE):
        b, j = c // 4, c % 4
        m = dict(W)
        m["x"] = np.ascontiguousarray(x[b, j * TOK:(j + 1) * TOK])
        m["w_head"] = np.ascontiguousarray(W["w_in"][:, :, head_cols(j)])
        m["cos_all"], m["sin_all"] = cos_all, sin_all
        m["ident"] = ident
        mc = [mix_consts(j, np.asarray(inp["b_rel_bias"][l], np.float32), l) for l in range(DEPTH)]
        for n in MIXC:
            m[n] = mc[0][n]
        m["biasT"] = np.stack([mc[l]["biasT"] for l in range(DEPTH)])
        m["laminit"] = np.stack([mc[l]["laminit"] for l in range(DEPTH)])
        m["lamv"] = lamv
        m["a_subln"] = np.ascontiguousarray(np.asarray(inp["a_subln"], np.float32))
        m["cnorm"] = np.ascontiguousarray(np.asarray(inp["c_out_norm"], np.float32)[:, j])
        maps.append(m)
    res = run_bass_kernel_spmd(nc, maps, core_ids=list(range(NCORE))).results
    out = np.empty_like(x)
    for c in range(NCORE):
        out[c // 4, (c % 4) * TOK:(c % 4 + 1) * TOK] = np.asarray(res[c]["x_out"])
    return out
```
